# Optimizing a Trainium2 kernel written in Bass

```python
import jax, jax.numpy as jnp
from jax import lax
import numpy as np

D_MODEL = 2048
BATCH = 32
SEQ = 256
DEPTH = 2
DEC_BATCH = 8
DEC_SEQ = 2048
PAST_LEN = 256

GRID_W = 64
HEAD_DIM = 128
NA_HEADS = 12
NA_KR = 8
NA_KC = 16
NA_QC = 16
NA_KCB = 32
FNET_GROUPS = 4
FNET_CH = 128
POOL_WINDOWS = (2, 4, 8, 16)
POOL_CH = 128
GQA_Q_HEADS = 12
GQA_KV_HEADS = 4
D_FF = 5632
CONV_W = 3
ROPE_BASE = 10000.0
Q_BLOCK = 128
EPS = 1e-6
NEG_INF = -1e30
N_EVEN = (DEPTH + 1) // 2
N_ODD = DEPTH // 2
NA_WIDTH = NA_HEADS * HEAD_DIM
FNET_WIDTH = FNET_GROUPS * FNET_CH
POOL_WIDTH = len(POOL_WINDOWS) * POOL_CH
GQA_Q_WIDTH = GQA_Q_HEADS * HEAD_DIM
GQA_KV_WIDTH = GQA_KV_HEADS * HEAD_DIM
EVEN_IN = 3 * NA_WIDTH + FNET_WIDTH
ODD_IN = POOL_WIDTH + GQA_Q_WIDTH + 2 * GQA_KV_WIDTH
MIX_WIDTH = NA_WIDTH + FNET_WIDTH

kernel_name = 'hybrid_diffusion_prefix_step'

f32 = jnp.float32


def rms_norm(x, g):
    x32 = x.astype(f32)
    y = x32 * lax.rsqrt(jnp.mean(x32 * x32, axis=-1, keepdims=True) + EPS)
    return (y * g.astype(f32)).astype(x.dtype)


def ada_modulation(cond, w, b):
    m = jax.nn.silu(cond) @ w + b
    return m.reshape(cond.shape[0], 6, 1, D_MODEL)


def modulate(x, g, shift, scale):
    return rms_norm(x, g) * (1 + scale) + shift


def axial_rope(x):
    n = x.shape[1]
    t = jnp.arange(n)
    rows = (t // GRID_W).astype(f32)
    cols = (t % GRID_W).astype(f32)
    half = HEAD_DIM // 2
    inv_freq = jnp.power(ROPE_BASE, -jnp.arange(0, half, 2, dtype=f32) / half)

    def rot(xa, pos):
        ang = pos[:, None] * inv_freq[None, :]
        cos = jnp.cos(ang)[None, :, None, :]
        sin = jnp.sin(ang)[None, :, None, :]
        x1, x2 = xa[..., :half // 2], xa[..., half // 2:]
        return jnp.concatenate([x1 * cos - x2 * sin, x2 * cos + x1 * sin], axis=-1)

    x32 = x.astype(f32)
    out = jnp.concatenate([rot(x32[..., :half], rows), rot(x32[..., half:], cols)], axis=-1)
    return out.astype(x.dtype)


def block_attention(q, k, v):
    B, S, Hq, dh = q.shape
    Hkv = k.shape[2]
    G = Hq // Hkv
    nb = S // Q_BLOCK
    scale = dh ** -0.5
    qb = q.reshape(B, nb, Q_BLOCK, Hkv, G, dh).transpose(1, 0, 2, 3, 4, 5)

    def one(qi):
        s = jnp.einsum('bqhgd,bkhd->bhgqk', qi, k).astype(f32) * scale
        p = jax.nn.softmax(s, axis=-1).astype(v.dtype)
        return jnp.einsum('bhgqk,bkhd->bqhgd', p, v)

    o = lax.map(one, qb)
    return o.transpose(1, 0, 2, 3, 4, 5).reshape(B, S, Hq * dh)


def neighbourhood_attention(q, k, v, ctx_k, ctx_v, rel_bias):
    B, N, H, dh = q.shape
    R = N // GRID_W
    kr = min(NA_KR, R)
    n_cb = GRID_W // NA_QC
    qg = q.reshape(B, R, GRID_W, H, dh)
    kg = k.reshape(B, R, GRID_W, H, dh)
    vg = v.reshape(B, R, GRID_W, H, dh)
    cb_start = jnp.clip(jnp.arange(n_cb) * NA_QC - NA_KC // 2, 0, GRID_W - NA_KCB)
    col_idx = cb_start[:, None] + jnp.arange(NA_KCB)[None, :]
    q_col = jnp.arange(GRID_W).reshape(n_cb, NA_QC)
    q_start = jnp.clip(q_col - NA_KC // 2, 0, GRID_W - NA_KC)
    kc = col_idx[:, None, :]
    col_valid = (kc >= q_start[..., None]) & (kc < q_start[..., None] + NA_KC)
    dc_idx = jnp.clip(kc - q_col[..., None], -(NA_KC - 1), NA_KC - 1) + NA_KC - 1
    col_bias = rel_bias[:, :, dc_idx].astype(f32)
    scale = dh ** -0.5

    def one_row(r):
        r0 = jnp.clip(r - kr // 2, 0, R - kr)
        k_blk = lax.dynamic_slice_in_dim(kg, r0, kr, axis=1)[:, :, col_idx]
        v_blk = lax.dynamic_slice_in_dim(vg, r0, kr, axis=1)[:, :, col_idx]
        qr = lax.dynamic_index_in_dim(qg, r, axis=1, keepdims=False).reshape(B, n_cb, NA_QC, H, dh)
        dr_idx = r0 + jnp.arange(kr) - r + NA_KR - 1
        bias = col_bias[:, dr_idx].transpose(0, 2, 3, 1, 4)
        s_loc = jnp.einsum('bjqhd,bijchd->bhjqic', qr, k_blk).astype(f32) * scale + bias[None]
        s_loc = jnp.where(col_valid[:, :, None, :], s_loc, NEG_INF).reshape(B, H, n_cb, NA_QC, kr * NA_KCB)
        s_ctx = jnp.einsum('bjqhd,blhd->bhjql', qr, ctx_k).astype(f32) * scale
        p = jax.nn.softmax(jnp.concatenate([s_loc, s_ctx], axis=-1), axis=-1).astype(v.dtype)
        p_loc = p[..., :kr * NA_KCB].reshape(B, H, n_cb, NA_QC, kr, NA_KCB)
        p_ctx = p[..., kr * NA_KCB:]
        o = (jnp.einsum('bhjqic,bijchd->bjqhd', p_loc, v_blk)
             + jnp.einsum('bhjql,blhd->bjqhd', p_ctx, ctx_v))
        return o.reshape(B, GRID_W, H * dh)

    out = lax.map(one_row, jnp.arange(R))
    return out.transpose(1, 0, 2, 3).reshape(B, N, H * dh)


def fourier_mix(xb, fnet_w):
    B, N, _ = xb.shape
    xg = xb.astype(f32).reshape(B, N, FNET_GROUPS, FNET_CH)
    f = jnp.fft.fft2(xg, axes=(1, 3), norm='ortho').real.astype(xb.dtype)
    return jnp.einsum('bngc,gcd->bngd', f, fnet_w).reshape(B, N, FNET_WIDTH)


def pool_mix(xc, pool_w, pool_scale):
    B, N, _ = xc.shape
    xg = xc.reshape(B, N, len(POOL_WINDOWS), POOL_CH)
    cs = jnp.concatenate([jnp.zeros((B, 1, len(POOL_WINDOWS), POOL_CH), f32),
                          jnp.cumsum(xg.astype(f32), axis=1)], axis=1)
    t = jnp.arange(N)
    outs = []
    for g, w in enumerate(POOL_WINDOWS):
        lo = jnp.clip(t - w // 2, 0, N)
        hi = jnp.clip(t + w // 2, 0, N)
        mean = (cs[:, hi, g] - cs[:, lo, g]) / (hi - lo).astype(f32)[None, :, None]
        outs.append(mean - xg[:, :, g].astype(f32))
    pooled = jnp.stack(outs, axis=2).astype(xc.dtype)
    return jnp.einsum('bngc,gcd->bngd', pooled, pool_w).reshape(B, N, POOL_WIDTH) * pool_scale


def conv_ffn(h, w_up, conv_w, conv_b, w_down):
    n = h.shape[1]
    u = h @ w_up
    up = jnp.pad(u, ((0, 0), (CONV_W // 2, CONV_W // 2), (0, 0)))
    u = sum(up[:, i:i + n] * conv_w[i] for i in range(CONV_W)) + conv_b
    val, gate = jnp.split(u, 2, axis=-1)
    return (jax.nn.silu(gate) * val) @ w_down


def split_even(proj):
    B, N, _ = proj.shape
    qa, ka, va, xb = jnp.split(proj, [NA_WIDTH, 2 * NA_WIDTH, 3 * NA_WIDTH], axis=-1)
    heads = lambda a: a.reshape(B, N, NA_HEADS, HEAD_DIM)
    return heads(qa), heads(ka), heads(va), xb


def split_odd(proj):
    B, N, _ = proj.shape
    xc, q, k, v = jnp.split(proj, [POOL_WIDTH, POOL_WIDTH + GQA_Q_WIDTH,
                                   POOL_WIDTH + GQA_Q_WIDTH + GQA_KV_WIDTH], axis=-1)
    return (xc, q.reshape(B, N, GQA_Q_HEADS, HEAD_DIM),
            k.reshape(B, N, GQA_KV_HEADS, HEAD_DIM), v.reshape(B, N, GQA_KV_HEADS, HEAD_DIM))


def even_mixer_context(h, w_in, fnet_w, w_out):
    q, k, v, xb = split_even(h @ w_in)
    a = block_attention(q, k, v)
    out = jnp.concatenate([a, fourier_mix(xb, fnet_w)], axis=-1) @ w_out
    return out, k, v


def even_mixer_latent(h, ctx_k, ctx_v, w_in, na_bias, fnet_w, w_out):
    q, k, v, xb = split_even(h @ w_in)
    a = neighbourhood_attention(q, k, v, ctx_k, ctx_v, na_bias)
    return jnp.concatenate([a, fourier_mix(xb, fnet_w)], axis=-1) @ w_out


def odd_mixer_context(h, w_in, pool_w, pool_scale, q_g, k_g, w_out):
    xc, q, k, v = split_odd(h @ w_in)
    q = rms_norm(q, q_g)
    k = rms_norm(k, k_g)
    a = block_attention(q, k, v)
    out = jnp.concatenate([pool_mix(xc, pool_w, pool_scale), a], axis=-1) @ w_out
    return out, k, v


def odd_mixer_latent(h, ctx_k, ctx_v, w_in, pool_w, pool_scale, q_g, k_g, w_out):
    xc, q, k, v = split_odd(h @ w_in)
    q = axial_rope(rms_norm(q, q_g))
    k = axial_rope(rms_norm(k, k_g))
    a = block_attention(q, jnp.concatenate([ctx_k, k], axis=1), jnp.concatenate([ctx_v, v], axis=1))
    return jnp.concatenate([pool_mix(xc, pool_w, pool_scale), a], axis=-1) @ w_out


def setup_inputs(seed: int = 0) -> dict:
    key = jax.random.key(seed)
    ks = jax.random.split(key, 28)

    def nrm(k, shape, scale):
        return jax.random.normal(k, shape, f32) * scale

    return {
        'x_prompt': nrm(ks[0], (BATCH, SEQ, D_MODEL), 1.0),
        'x_sample': nrm(ks[1], (DEC_BATCH, DEC_SEQ, D_MODEL), 1.0),
        'c': nrm(ks[2], (DEC_BATCH, D_MODEL), 1.0),
        'cache_na_k': nrm(ks[3], (DEC_BATCH, N_EVEN, PAST_LEN, NA_HEADS, HEAD_DIM), 1.0),
        'cache_na_v': nrm(ks[4], (DEC_BATCH, N_EVEN, PAST_LEN, NA_HEADS, HEAD_DIM), 1.0),
        'cache_gqa_k': nrm(ks[5], (DEC_BATCH, N_ODD, PAST_LEN, GQA_KV_HEADS, HEAD_DIM), 1.0),
        'cache_gqa_v': nrm(ks[6], (DEC_BATCH, N_ODD, PAST_LEN, GQA_KV_HEADS, HEAD_DIM), 1.0),
        'c_ctx': nrm(ks[7], (D_MODEL,), 1.0),
        'norm1_g': 1.0 + nrm(ks[8], (DEPTH, D_MODEL), 0.05),
        'norm2_g': 1.0 + nrm(ks[9], (DEPTH, D_MODEL), 0.05),
        'ada_w': nrm(ks[10], (DEPTH, D_MODEL, 6 * D_MODEL), 0.5 * D_MODEL ** -0.5),
        'ada_b': nrm(ks[11], (DEPTH, 6 * D_MODEL), 0.02),
        'ev_w_in': nrm(ks[12], (N_EVEN, D_MODEL, EVEN_IN), D_MODEL ** -0.5),
        'ev_na_bias': nrm(ks[13], (N_EVEN, NA_HEADS, 2 * NA_KR - 1, 2 * NA_KC - 1), 0.1),
        'ev_fnet_w': nrm(ks[14], (N_EVEN, FNET_GROUPS, FNET_CH, FNET_CH), FNET_CH ** -0.5),
        'ev_w_out': nrm(ks[15], (N_EVEN, MIX_WIDTH, D_MODEL), MIX_WIDTH ** -0.5),
        'od_w_in': nrm(ks[16], (N_ODD, D_MODEL, ODD_IN), D_MODEL ** -0.5),
        'od_pool_w': nrm(ks[17], (N_ODD, len(POOL_WINDOWS), POOL_CH, POOL_CH), POOL_CH ** -0.5),
        'od_pool_scale': 1.0 + nrm(ks[18], (N_ODD, POOL_WIDTH), 0.1),
        'od_q_norm_g': 1.0 + nrm(ks[19], (N_ODD, HEAD_DIM), 0.05),
        'od_k_norm_g': 1.0 + nrm(ks[20], (N_ODD, HEAD_DIM), 0.05),
        'od_w_out': nrm(ks[21], (N_ODD, MIX_WIDTH, D_MODEL), MIX_WIDTH ** -0.5),
        'ffn_w_up': nrm(ks[22], (DEPTH, D_MODEL, 2 * D_FF), D_MODEL ** -0.5),
        'ffn_conv_w': nrm(ks[23], (DEPTH, CONV_W, 2 * D_FF), CONV_W ** -0.5),
        'ffn_conv_b': nrm(ks[24], (DEPTH, 2 * D_FF), 0.02),
        'ffn_w_down': nrm(ks[25], (DEPTH, D_FF, D_MODEL), D_FF ** -0.5),
        'final_norm_g': 1.0 + nrm(ks[26], (D_MODEL,), 0.05),
    }


def reference(x_prompt, x_sample, c, cache_na_k, cache_na_v, cache_gqa_k, cache_gqa_v, c_ctx,
              norm1_g, norm2_g, ada_w, ada_b,
              ev_w_in, ev_na_bias, ev_fnet_w, ev_w_out,
              od_w_in, od_pool_w, od_pool_scale, od_q_norm_g, od_k_norm_g, od_w_out,
              ffn_w_up, ffn_conv_w, ffn_conv_b, ffn_w_down, final_norm_g):
    xp = x_prompt
    xs = x_sample
    na_k_new, na_v_new, gqa_k_new, gqa_v_new = [], [], [], []
    for i in range(DEPTH):
        j = i // 2
        mc = ada_modulation(c_ctx[None, :], ada_w[i], ada_b[i])
        ms = ada_modulation(c, ada_w[i], ada_b[i])
        hp = modulate(xp, norm1_g[i], mc[:, 0], mc[:, 1])
        hs = modulate(xs, norm1_g[i], ms[:, 0], ms[:, 1])
        if i % 2 == 0:
            op, kp, vp = even_mixer_context(hp, ev_w_in[j], ev_fnet_w[j], ev_w_out[j])
            osm = even_mixer_latent(hs, cache_na_k[:, j], cache_na_v[:, j], ev_w_in[j],
                                    ev_na_bias[j], ev_fnet_w[j], ev_w_out[j])
            na_k_new.append(kp)
            na_v_new.append(vp)
        else:
            op, kp, vp = odd_mixer_context(hp, od_w_in[j], od_pool_w[j], od_pool_scale[j],
                                           od_q_norm_g[j], od_k_norm_g[j], od_w_out[j])
            osm = odd_mixer_latent(hs, cache_gqa_k[:, j], cache_gqa_v[:, j], od_w_in[j], od_pool_w[j],
                                   od_pool_scale[j], od_q_norm_g[j], od_k_norm_g[j], od_w_out[j])
            gqa_k_new.append(kp)
            gqa_v_new.append(vp)
        xp = xp + mc[:, 2] * op
        xs = xs + ms[:, 2] * osm
        xp = xp + mc[:, 5] * conv_ffn(modulate(xp, norm2_g[i], mc[:, 3], mc[:, 4]),
                                      ffn_w_up[i], ffn_conv_w[i], ffn_conv_b[i], ffn_w_down[i])
        xs = xs + ms[:, 5] * conv_ffn(modulate(xs, norm2_g[i], ms[:, 3], ms[:, 4]),
                                      ffn_w_up[i], ffn_conv_w[i], ffn_conv_b[i], ffn_w_down[i])
    y_prompt = rms_norm(xp, final_norm_g)
    y_sample = rms_norm(xs, final_norm_g)
    new_na_k = jnp.stack(na_k_new, axis=1)
    new_na_v = jnp.stack(na_v_new, axis=1)
    new_gqa_k = jnp.stack(gqa_k_new, axis=1)
    new_gqa_v = jnp.stack(gqa_v_new, axis=1)
    return (y_prompt, y_sample, new_na_k, new_na_v, new_gqa_k, new_gqa_v)
```

```python
import math
from contextlib import ExitStack

import numpy as np
import concourse.bass as bass
import concourse.mybir as mybir
from concourse.bass_utils import run_bass_kernel_spmd

F32 = mybir.dt.float32
BF16 = mybir.dt.bfloat16
AF = mybir.ActivationFunctionType
ALU = mybir.AluOpType
AX = mybir.AxisListType

T = 3072
TS = 2048
TP = 1024
D = 2048
KC = 16
DFF = 5632
EPS = 1e-6
NCORES = 8
SQ128 = math.sqrt(128.0)
ISQ128 = 1.0 / SQ128


class Buf:
    def __init__(self, name, is_sbuf=False, persistent=False):
        self.name = name
        self.is_sbuf = is_sbuf
        self.persistent = persistent
        self.w = {}
        self.r = {}
        self.wsem = {}
        self.rsem = {}


class SemC:
    def __init__(self, sem):
        self.sem = sem
        self.count = 0


class Eng:
    def __init__(self, name):
        self.name = name
        self.is_pe = name == 'pe'
        self.semc = None
        self.ops = []
        self.seen = {}
        self.pending_unsig = False


class Sched:
    def __init__(self, nc, stack):
        self.nc = nc
        self.stack = stack
        self.engs = {n: Eng(n) for n in ('pe', 'act', 'dve', 'pool', 'sp')}
        self.sems = []
        self.free_semcs = {}
        self.phase_semcs = []
        self.nsem = 0
        for n, e in self.engs.items():
            e.semc = self.newsem('prog_' + n, True)

    def newsem(self, name, persistent=False, cls='eng'):
        fl = self.free_semcs.setdefault(cls, [])
        if not persistent and fl:
            s = fl.pop()
        else:
            self.nsem += 1
            s = SemC(self.stack.enter_context(self.nc.semaphore('%s_%d' % (name, self.nsem))))
            s.cls = cls
            self.sems.append(s)
        if not persistent:
            self.phase_semcs.append(s)
        return s

    def recycle(self):
        for sc in self.phase_semcs:
            self.free_semcs.setdefault(sc.cls, []).append(sc)
        self.phase_semcs = []

    def _collect(self, e, reads, writes):
        need = {}
        for b in reads:
            for k, v in b.w.items():
                if need.get(k, 0) < v:
                    need[k] = v
        for b in writes:
            for d in (b.w, b.r):
                for k, v in d.items():
                    if need.get(k, 0) < v:
                        need[k] = v
        waits = []
        for k, v in need.items():
            if k is e.semc and e.is_pe:
                continue
            if e.seen.get(k, 0) >= v:
                continue
            e.seen[k] = v
            waits.append((k.sem, v))
        return waits

    def op(self, eng, fn, reads=(), writes=(), sig=True):
        e = self.engs[eng]
        waits = self._collect(e, reads, writes)
        if sig:
            e.semc.count += 1
            val = e.semc.count
            inc = (e.semc.sem, 1)
            e.pending_unsig = False
        else:
            val = e.semc.count + 1
            inc = None
            e.pending_unsig = True
        e.ops.append((waits, fn, inc))
        for b in reads:
            if b.r.get(e.semc, 0) < val:
                b.r[e.semc] = val
        for b in writes:
            b.w = {e.semc: val}
            b.r = {}

    def dma(self, queue, out_ap, in_ap, src, dst, **kw):
        e = self.engs[queue]
        waits = self._collect(e, [src], [dst])
        if dst.is_sbuf:
            if queue not in dst.wsem:
                dst.wsem[queue] = self.newsem('w%s_%s' % (queue, dst.name), dst.persistent, queue)
            sc = dst.wsem[queue]
        else:
            if queue not in src.rsem:
                src.rsem[queue] = self.newsem('r%s_%s' % (queue, src.name), src.persistent, queue)
            sc = src.rsem[queue]
        sc.count += 16
        val = sc.count
        e.ops.append((waits, (lambda h, o=out_ap, i=in_ap, kw=kw: h.dma_start(out=o, in_=i, **kw)), (sc.sem, 16)))
        if src.r.get(sc, 0) < val:
            src.r[sc] = val
        if dst.is_sbuf:
            dst.w = {sc: val}
            dst.r = {}
        else:
            dst.w = {**dst.w, sc: val}

    def barrier(self):
        for n, e in self.engs.items():
            if e.pending_unsig:
                raise RuntimeError('unsignalled op pending on ' + n)
        for n, e in self.engs.items():
            waits = []
            for sc in self.sems:
                v = sc.count
                if v == 0 or e.seen.get(sc, 0) >= v:
                    continue
                if sc is e.semc and e.is_pe:
                    continue
                e.seen[sc] = v
                waits.append((sc.sem, v))
            if waits:
                e.ops.append((waits, None, None))

    def finish(self):
        self.barrier()

    def emit(self):
        nc = self.nc
        with nc.Block() as block:
            def run(h, e):
                for waits, fn, inc in e.ops:
                    for s, v in waits:
                        h.wait_ge(s, v)
                    if fn is None:
                        continue
                    ins = fn(h)
                    if inc is not None:
                        ins.then_inc(inc[0], inc[1])

            @block.tensor
            def _(h):
                run(h, self.engs['pe'])

            @block.scalar
            def _(h):
                run(h, self.engs['act'])

            @block.vector
            def _(h):
                run(h, self.engs['dve'])

            @block.gpsimd
            def _(h):
                run(h, self.engs['pool'])

            @block.sync
            def _(h):
                run(h, self.engs['sp'])


class Arena:
    def __init__(self, nc, base, limit):
        self.nc = nc
        self.base = base
        self.off = base
        self.limit = limit
        self.n = 0

    def reset(self, off=None):
        self.off = self.base if off is None else off

    def alloc(self, name, shape, dtype):
        self.n += 1
        esz = 2 if dtype == BF16 else 4
        size = esz
        for s in shape[1:]:
            size *= s
        off = (max(self.off, 0) + 31) // 32 * 32
        assert off + size <= self.limit, (name, off, size, self.limit)
        t = self.nc.alloc_sbuf_tensor_at('%s_%d' % (name, self.n), list(shape), dtype, offset=off)
        self.off = off + size
        return t


def build(debug=False, stop_after=None):
    nc = bass.Bass("TRN2", target_bir_lowering=False)
    st = ExitStack()
    S = Sched(nc, st)

    def din(name, shape, dt=F32):
        return nc.dram_tensor(name, list(shape), dt, kind="ExternalInput").ap()

    def dout(name, shape, dt=F32):
        return nc.dram_tensor(name, list(shape), dt, kind="ExternalOutput").ap()

    def dscr(name, shape, dt=F32):
        if dt == BF16:
            shp = list(shape[:-1]) + [shape[-1] // 2]
            return nc.dram_tensor(name, shp, F32, kind="ExternalOutput").ap().bitcast(BF16)
        return nc.dram_tensor(name, list(shape), F32, kind="ExternalOutput").ap()

    xs_d = din("xs", [TS, D])
    xp_d = din("xp", [TP, D])
    condT_d = din("condT", [D, 2])
    cnk_d = din("cnk", [256, 1536])
    cnv_d = din("cnv", [256, 1536])
    cgk_d = din("cgk", [256, 512])
    cgv_d = din("cgv", [256, 512])
    ada_w_d = din("ada_w", [2, D, 6 * D])
    ada_bT_d = din("ada_bT", [2, 128, 96])
    ngT_d = din("ngT", [128, 5, 16])
    ev_w_in_d = din("ev_w_in", [D, 5120])
    ev_w_out_d = din("ev_w_out", [D, D])
    od_w_in_d = din("od_w_in", [D, 3072])
    od_w_out_d = din("od_w_out", [D, D])
    w_up_d = din("w_up", [2, D, 2 * DFF])
    w_down_d = din("w_down", [2, DFF, D])
    cwT_d = din("cwT", [128, 2, 88, 3])
    cbT_d = din("cbT", [128, 2, 88])
    ident_d = din("ident", [128, 128])
    fin_g_d = din("fin_g", [D])
    bt_d = din("bt", [12, 21, 128, 128])
    cn2k_d = din("cn2k", [2048, 2048])
    sn2k_d = din("sn2k", [2048, 2048])
    cn256_d = din("cn256", [256, 256])
    sn256_d = din("sn256", [256, 256])
    cc_d = din("cc", [128, 256])
    fw_d = din("fw", [4, 128, 128])
    pw_d = din("pw", [4, 128, 128])
    pscT_d = din("pscT", [128, 4])
    gq4_d = din("gq4", [512])
    gk4_d = din("gk4", [512])
    ropeC_d = din("ropeC", [TS, 128])
    ropeS_d = din("ropeS", [TS, 128])
    invc_s_d = din("invc_s", [4, TS])
    invc_p_d = din("invc_p", [4, 256])
    yp_d = dout("yp", [TP, D])
    ys_d = dout("ys", [TS, D])
    nak_d = dout("nak", [TP, 1536])
    nav_d = dout("nav", [TP, 1536])
    gqk_d = dout("gqk", [TP, 512])
    gqv_d = dout("gqv", [TP, 512])
    XT_d = dscr("XT", [16, 128, T])
    QT_d = dscr("QT", [12, 128, T], BF16)
    KT_d = dscr("KT", [12, 128, T], BF16)
    V_d = dscr("Vs", [T, 1536], BF16)
    XB_d = dscr("XB", [4, 128, T], F32)
    XT_b = Buf("XT")
    QT_b = Buf("QT")
    KT_b = Buf("KT")
    V_b = Buf("V")
    XB_b = Buf("XB")
    IN_b = Buf("inputs")
    OUT_b = Buf("outputs")

    PS = nc.alloc_psum_tensor("psall", [128, 4096], F32)
    PSB = [Buf("psb%d" % i) for i in range(8)]

    def bank(i):
        return PS[:, i * 512:(i + 1) * 512]

    BASE = 16512
    LIMIT = BASE + 212800
    pers = Arena(nc, BASE, BASE + 12288)
    ident_f = pers.alloc("ident_f", [128, 128], F32)
    ident_b = pers.alloc("ident_b", [128, 128], BF16)
    ones_b = pers.alloc("ones_b", [128, 128], BF16)
    modT = pers.alloc("modT", [128, 2, 96, 2], F32)
    gs = pers.alloc("gs", [128, 2, 2, 16, 2], F32)
    ngT = pers.alloc("ngT", [128, 5, 16], F32)
    cw = pers.alloc("cw", [128, 2, 88, 3], F32)
    cb = pers.alloc("cb", [128, 2, 88], F32)
    scT = pers.alloc("scT", [128, 16, 2], BF16)
    condT = pers.alloc("condT", [128, 16, 2], F32)
    adab = pers.alloc("adab", [128, 2, 96], F32)
    hsave = pers.alloc("hsave", [128, 16, 2], BF16)
    HS_b = Buf("hsave", True, True)
    PERS_b = Buf("pers", True, True)
    MOD_b = Buf("mod", True, True)

    AT = nc.alloc_sbuf_tensor_at("AT", [128, 16, T], BF16, offset=BASE + 12288)
    AT_b = [Buf("AT%d" % i, True) for i in range(6)]
    WB_OFF = BASE + 12288 + 98304
    WB = [nc.alloc_sbuf_tensor_at("WB%d" % i, [128, 16, 512], BF16, offset=WB_OFF + i * 16384) for i in range(2)]
    WB_b = [Buf("WB%d" % i, True, True) for i in range(2)]
    PH_OFF = WB_OFF + 32768
    ph = Arena(nc, PH_OFF, LIMIT - 7 * 8192)

    wcount = [0]

    def load_w(src_ap_pcn):
        i = wcount[0] % 2
        wcount[0] += 1
        kc, n = src_ap_pcn.shape[1], src_ap_pcn.shape[2]
        S.dma('pool', WB[i][:, 0:kc, 0:n], src_ap_pcn, IN_b, WB_b[i])
        return WB[i], WB_b[i]

    def wview(W2d, n0, n):
        return W2d[:, n0:n0 + n].rearrange("(c p) n -> p c n", p=128)

    pcount = [0]

    def next_bank(nb=8, base=0):
        i = base + pcount[0] % nb
        pcount[0] += 1
        return i

    S.dma('sp', ident_f[:], ident_d, IN_b, PERS_b)
    S.dma('pool', ident_b[:], ident_d, IN_b, PERS_b)
    S.dma('sp', ngT[:], ngT_d, IN_b, PERS_b)
    S.dma('sp', cw[:], cwT_d, IN_b, PERS_b)
    S.dma('sp', cb[:], cbT_d, IN_b, PERS_b)
    S.dma('sp', condT[:], condT_d.rearrange("(c p) k -> p c k", p=128), IN_b, PERS_b)
    S.dma('sp', adab[:], ada_bT_d.rearrange("l p c -> p l c"), IN_b, PERS_b)
    S.op('dve', lambda h: h.memset(ones_b[:], 1.0), writes=[PERS_b])
    SC_b = Buf("scT", True, True)
    S.op('act', lambda h: h.activation(out=scT[:], in_=condT[:], func=AF.Silu), reads=[PERS_b], writes=[SC_b])

    ADA_PS = 7
    NAS = 7
    ADA_TOP = LIMIT - NAS * 8192
    ADAS = [nc.alloc_sbuf_tensor_at("adas%d" % i, [128, 16, 256], BF16, offset=ADA_TOP + i * 8192) for i in range(NAS)]
    ADAS_b = [Buf("adas%d" % i, True, True) for i in range(NAS)]
    ada_n = [0]

    def ada_block(l, blk):
        i = ada_n[0] % NAS
        ada_n[0] += 1
        S.dma('pool', ADAS[i][:], ada_w_d[l][:, blk * 256:(blk + 1) * 256].rearrange("(c p) n -> p c n", p=128), IN_b, ADAS_b[i])
        for j in range(2):
            ch = blk * 2 + j
            for kc in range(KC):
                S.op('pe', (lambda h, i=i, j=j, kc=kc, ch=ch, l=l: h.matmul(
                    PS[:, ADA_PS * 512 + (l * 96 + ch) * 2: ADA_PS * 512 + (l * 96 + ch) * 2 + 2],
                    ADAS[i][:, kc, j * 128:(j + 1) * 128], scT[:, kc, :], start=(kc == 0), stop=(kc == KC - 1))),
                    reads=[ADAS_b[i], SC_b], writes=[PSB[ADA_PS]], sig=(kc == KC - 1))
        if blk % 8 == 7:
            v = blk // 8
            S.op('dve', (lambda h, l=l, v=v: h.tensor_tensor(
                out=modT[:, l, v * 16:(v + 1) * 16, :],
                in0=PS[:, ADA_PS * 512 + (l * 96 + v * 16) * 2: ADA_PS * 512 + (l * 96 + v * 16 + 16) * 2].rearrange("p (c k) -> p c k", k=2),
                in1=adab[:, l, v * 16:(v + 1) * 16].unsqueeze(2).to_broadcast([128, 16, 2]), op=ALU.add)),
                reads=[PSB[ADA_PS], PERS_b], writes=[MOD_b])
            if v in (1, 4):
                w_ = 0 if v == 1 else 1
                S.op('dve', (lambda h, l=l, w_=w_, v=v: h.scalar_tensor_tensor(
                    out=gs[:, l, w_], in0=modT[:, l, v * 16:(v + 1) * 16, :], scalar=1.0,
                    in1=ngT[:, w_ * 2 + l].unsqueeze(2).to_broadcast([128, 16, 2]), op0=ALU.add, op1=ALU.mult)),
                    reads=[MOD_b, PERS_b], writes=[MOD_b])

    ada_todo = [(0, blk) for blk in range(16, 48)] + [(1, blk) for blk in range(48)]

    def ada_pull(n):
        for _ in range(n):
            if ada_todo:
                l_, blk_ = ada_todo.pop(0)
                ada_block(l_, blk_)

    for blk in range(16):
        ada_block(0, blk)

    def mod_ap(l, v, dc, cond):
        return modT[:, l, v * 16 + dc, cond:cond + 1]

    def cond_of_tile(t512):
        return 0 if t512 < 4 else 1

    if stop_after == 'in':
        S.finish(); S.emit(); st.close(); return nc
    if stop_after == 'ada':
        pass

    def norm_stats(cols, ar, pre=None):
        nr = len(ar['rstd'])
        rstd = ar['rstd'][ar['k'] % nr]
        rstd_b = ar['rstd_b'][ar['k'] % nr]
        if pre is not None:
            xi, xi_b, ntot = pre
            cols = []
        else:
            ntot = sum(n for _, n in cols)
            xi = ar['xi'][ar['k'] % ar['nxi']]
            xi_b = ar['xi_b'][ar['k'] % ar['nxi']]
        ar['k'] += 1
        o = 0
        for c0, n in cols:
            kw = {'allow_slow_non_contiguous': True} if n == 1 else {}
            S.dma('sp', xi[:, :, o:o + n], XT_d[:, :, c0:c0 + n].rearrange("c p t -> p c t"), XT_b, xi_b, **kw)
            o += n
        b = next_bank(4)
        for dc in range(KC):
            sq = ar['sq'][dc % ar['nsq']]
            sq_b = ar['sq_b'][dc % ar['nsq']]
            S.op('act', (lambda h, sq=sq, dc=dc, xi=xi: h.activation(out=sq[:, 0:ntot], in_=xi[:, dc, 0:ntot], func=AF.Square)),
                 reads=[xi_b], writes=[sq_b])
            S.op('pe', (lambda h, b=b, sq=sq, dc=dc: h.matmul(PS[:, b * 512:b * 512 + ntot], ones_b[:], sq[:, 0:ntot],
                                                             start=(dc == 0), stop=(dc == KC - 1))),
                 reads=[sq_b, PERS_b], writes=[PSB[b]], sig=True)
        S.op('dve', (lambda h, b=b: h.tensor_scalar(out=rstd[:, 0:ntot], in0=PS[:, b * 512:b * 512 + ntot], scalar1=1.0 / D, scalar2=EPS,
                                                    op0=ALU.mult, op1=ALU.add)), reads=[PSB[b]], writes=[rstd_b])
        S.op('act', (lambda h: h.activation(out=rstd[:, 0:ntot], in_=rstd[:, 0:ntot], func=AF.Sqrt)),
             reads=[rstd_b], writes=[rstd_b])
        S.op('dve', (lambda h: h.reciprocal(out=rstd[:, 0:ntot], in_=rstd[:, 0:ntot])),
             reads=[rstd_b], writes=[rstd_b])
        return xi, xi_b, rstd, rstd_b

    def norm_mod(l, which, cols, dst_fn, dst_bufs, cond, ar, pre=None):
        ntot = pre[2] if pre is not None else sum(n for _, n in cols)
        xi, xi_b, rstd, rstd_b = norm_stats(cols, ar, pre)
        for dc in range(KC):
            tmp = ar['tmp'][dc % ar['ntmp']]
            tmp_b = ar['tmp_b'][dc % ar['ntmp']]
            S.op('dve', (lambda h, tmp=tmp, dc=dc, xi=xi: h.scalar_tensor_tensor(
                out=tmp[:, 0:ntot], in0=xi[:, dc, 0:ntot], scalar=gs[:, l, which, dc, cond:cond + 1], in1=rstd[:, 0:ntot],
                op0=ALU.mult, op1=ALU.mult)), reads=[xi_b, rstd_b, MOD_b], writes=[tmp_b])
            S.op('act', (lambda h, tmp=tmp, dc=dc: h.activation(
                out=dst_fn(dc), in_=tmp[:, 0:ntot], func=AF.Identity, bias=mod_ap(l, which * 3, dc, cond), scale=1.0)),
                reads=[tmp_b, MOD_b], writes=dst_bufs)

    def norm_arena(ncol, nxi=2, nsq=6, ntmp=4):
        ar = {'k': 0, 'nxi': nxi, 'nsq': nsq, 'ntmp': ntmp}
        ar['xi'] = [ph.alloc("nxi", [128, 16, ncol], F32) for _ in range(nxi)]
        ar['xi_b'] = [Buf("nxi%d" % i, True) for i in range(nxi)]
        ar['sq'] = [ph.alloc("nsq", [128, ncol], BF16) for _ in range(nsq)]
        ar['sq_b'] = [Buf("nsq%d" % i, True) for i in range(nsq)]
        ar['rstd'] = [ph.alloc("nrstd", [128, ncol], F32) for _ in range(max(nxi, 1))]
        ar['rstd_b'] = [Buf("nrstd%d" % i, True) for i in range(max(nxi, 1))]
        ar['tmp'] = [ph.alloc("ntmp", [128, ncol], F32) for _ in range(ntmp)]
        ar['tmp_b'] = [Buf("ntmp%d" % i, True) for i in range(ntmp)]
        return ar

    def in_phase():
        ph.reset(WB_OFF)
        xin = [ph.alloc("xin", [128, D], F32) for _ in range(2)]
        xin_b = [Buf("xin%d" % i, True) for i in range(2)]
        xts = [ph.alloc("xts", [128, 16, 128], F32) for _ in range(3)]
        xts_b = [Buf("xts%d" % i, True) for i in range(3)]
        ar = norm_arena(128, nxi=0, nsq=6, ntmp=3)

        def tpart(tt):
            i = tt % 2
            x3 = tt % 3
            src = xs_d[tt * 128:(tt + 1) * 128, :] if tt < 16 else xp_d[(tt - 16) * 128:(tt - 15) * 128, :]
            S.dma('sp', xin[i][:], src, IN_b, xin_b[i])
            for q4 in range(4):
                b = next_bank(4)
                for j in range(4):
                    dc = q4 * 4 + j
                    S.op('pe', (lambda h, b=b, j=j, dc=dc, i=i: h.transpose(
                        PS[:, b * 512 + j * 128: b * 512 + (j + 1) * 128], xin[i][:, dc * 128:(dc + 1) * 128], ident_f[:])),
                        reads=[xin_b[i], PERS_b], writes=[PSB[b]], sig=(j == 3))
                if q4 % 2 == 0:
                    S.op('act', (lambda h, b=b, q4=q4, x3=x3: h.activation(
                        out=xts[x3][:, q4 * 4:(q4 + 1) * 4, :], in_=bank(b).rearrange("p (c t) -> p c t", t=128), func=AF.Copy)),
                        reads=[PSB[b]], writes=[xts_b[x3]])
                else:
                    S.op('dve', (lambda h, b=b, q4=q4, x3=x3: h.tensor_copy(
                        out=xts[x3][:, q4 * 4:(q4 + 1) * 4, :], in_=bank(b).rearrange("p (c t) -> p c t", t=128))),
                        reads=[PSB[b]], writes=[xts_b[x3]])
            S.dma('sp', XT_d[:, :, tt * 128:(tt + 1) * 128].rearrange("c p t -> p c t"), xts[x3][:], xts_b[x3], XT_b)

        def npart(tt):
            x3 = tt % 3
            norm_mod(0, 0, None, (lambda dc, tt=tt: AT[:, dc, tt * 128:(tt + 1) * 128]), [AT_b[tt // 4]], (0 if tt < 16 else 1), ar,
                     pre=(xts[x3], xts_b[x3], 128))

        for tt in range(25):
            if tt < 24:
                tpart(tt)
            if tt >= 1:
                npart(tt - 1)
        S.barrier()
        S.recycle()

    def norm1_phase(l):
        ph.reset(WB_OFF)
        lim = ph.limit
        ph.limit = LIMIT
        ar = norm_arena(512, nxi=2, nsq=6, ntmp=4)
        ph.limit = lim
        for t5 in range(6):
            norm_mod(l, 0, [(t5 * 512, 512)], (lambda dc, t5=t5: AT[:, dc, t5 * 512:(t5 + 1) * 512]), [AT_b[t5]], cond_of_tile(t5), ar)
        S.barrier()
        S.recycle()

    def mm_fm(wt, wb, j, t5, b, kcn=KC, src=None, src_b=None):
        for kc in range(kcn):
            S.op('pe', (lambda h, wt=wt, j=j, kc=kc, t5=t5, b=b: h.matmul(
                bank(b), wt[:, kc, j * 128:(j + 1) * 128], AT[:, kc, t5 * 512:(t5 + 1) * 512],
                start=(kc == 0), stop=(kc == kcn - 1))), reads=[wb, AT_b[t5]], writes=[PSB[b]], sig=(kc == kcn - 1))

    def mm_tm(wt, wb, tt, b, ncols=512):
        for kc in range(KC):
            S.op('pe', (lambda h, wt=wt, kc=kc, tt=tt, b=b: h.matmul(
                PS[:, b * 512:b * 512 + ncols], AT[:, kc, tt * 128:(tt + 1) * 128], wt[:, kc, 0:ncols],
                start=(kc == 0), stop=(kc == KC - 1))), reads=[wb, AT_b[tt // 4]], writes=[PSB[b]], sig=(kc == KC - 1))

    def g1_even():
        ph.reset()
        sgb = [ph.alloc("sgb", [128, 512], BF16) for _ in range(3)]
        sgb_b = [Buf("sgb%d" % i, True) for i in range(3)]
        sgf = [ph.alloc("sgf", [128, 512], F32) for _ in range(3)]
        sgf_b = [Buf("sgf%d" % i, True) for i in range(3)]
        cb_ = [0]
        cf_ = [0]

        def nb():
            i = cb_[0] % 3
            cb_[0] += 1
            return sgb[i], sgb_b[i]

        def nf():
            i = cf_[0] % 3
            cf_[0] += 1
            return sgf[i], sgf_b[i]

        wq = {0: load_w(wview(ev_w_in_d, 0, 512))}
        for blk in range(10):
            wt, wb = wq[blk]
            if blk + 1 < 10:
                wq[blk + 1] = load_w(wview(ev_w_in_d, (blk + 1) * 512, 512))
            kind = blk // 3
            if kind in (0, 1, 3):
                for j in range(4):
                    ch = (blk % 3) * 4 + j if kind < 3 else j
                    for t5 in range(6):
                        b = next_bank(6)
                        mm_fm(wt, wb, j, t5, b)
                        if kind == 3:
                            sg, sg_b = nf()
                            S.op('act', (lambda h, sg=sg, b=b: h.activation(out=sg[:], in_=bank(b), func=AF.Copy)),
                                 reads=[PSB[b]], writes=[sg_b])
                            S.dma('sp', XB_d[ch, :, t5 * 512:(t5 + 1) * 512], sg[:], sg_b, XB_b)
                        else:
                            sg, sg_b = nb()
                            eng = 'act' if (t5 % 2 == 0) else 'dve'
                            if eng == 'act':
                                S.op('act', (lambda h, sg=sg, b=b: h.activation(out=sg[:], in_=bank(b), func=AF.Copy)),
                                     reads=[PSB[b]], writes=[sg_b])
                            else:
                                S.op('dve', (lambda h, sg=sg, b=b: h.tensor_copy(out=sg[:], in_=bank(b))),
                                     reads=[PSB[b]], writes=[sg_b])
                            dst, dst_b = (QT_d, QT_b) if kind == 0 else (KT_d, KT_b)
                            S.dma('sp', dst[ch, :, t5 * 512:(t5 + 1) * 512], sg[:], sg_b, dst_b)
                if kind == 1:
                    for tt in range(16, 24):
                        b = next_bank(6)
                        mm_tm(wt, wb, tt, b)
                        sg, sg_b = nf()
                        S.op('act', (lambda h, sg=sg, b=b: h.activation(out=sg[:], in_=bank(b), func=AF.Copy)),
                             reads=[PSB[b]], writes=[sg_b])
                        S.dma('sp', nak_d[(tt - 16) * 128:(tt - 15) * 128, (blk - 3) * 512:(blk - 2) * 512], sg[:], sg_b, OUT_b)
            else:
                for tt in range(24):
                    b = next_bank(6)
                    mm_tm(wt, wb, tt, b)
                    sg, sg_b = nb()
                    if tt < 16:
                        S.op('dve', (lambda h, sg=sg, b=b: h.tensor_copy(out=sg[:], in_=bank(b))), reads=[PSB[b]], writes=[sg_b])
                    else:
                        sf, sf_b = nf()
                        S.op('act', (lambda h, sf=sf, b=b: h.activation(out=sf[:], in_=bank(b), func=AF.Copy)),
                             reads=[PSB[b]], writes=[sf_b])
                        S.op('dve', (lambda h, sg=sg, sf=sf: h.tensor_copy(out=sg[:], in_=sf[:])), reads=[sf_b], writes=[sg_b])
                        S.dma('sp', nav_d[(tt - 16) * 128:(tt - 15) * 128, (blk - 6) * 512:(blk - 5) * 512], sf[:], sf_b, OUT_b)
                    S.dma('sp', V_d[tt * 128:(tt + 1) * 128, (blk - 6) * 512:(blk - 5) * 512], sg[:], sg_b, V_b)
            ada_pull(8)
        ada_pull(1000)
        ph.limit = LIMIT
        S.barrier()
        S.recycle()

    PSbf = PS[:, 6 * 512:7 * 512].bitcast(BF16)
    acnt = [0]

    def pipeline2(stages):
        prev = None
        for s1, s2 in stages:
            s1()
            if prev is not None:
                prev()
            prev = s2
        if prev is not None:
            prev()

    def block_attn_prompt(hq, kch, vcol0, out_chunk, ar, stages):
        i = ar['k'] % 2
        ar['k'] += 1
        qp, kp, vp = ar['qp'][i], ar['kp'][i], ar['vp'][i]
        qp_b, kp_b, vp_b = ar['qp_b'][i], ar['kp_b'][i], ar['vp_b'][i]

        def loads():
            S.dma('sp', qp[:], QT_d[hq, :, TS:T], QT_b, qp_b)
            S.dma('sp', kp[:], KT_d[kch, :, TS:T], KT_b, kp_b)
            S.dma('sp', vp[:], V_d[TS:T, vcol0:vcol0 + 128].rearrange("(c p) d -> p c d", p=128), V_b, vp_b)

        for s in range(4):
            c = acnt[0]
            acnt[0] += 1
            sb = c % 4
            ob = 4 + c % 2
            pt = ar['pt'][c % 2]
            pt_b = ar['pt_b'][c % 2]
            rec = ar['rec'][c % 2]
            rec_b = ar['rec_b'][c % 2]

            def st1(s=s, sb=sb, pt=pt, pt_b=pt_b):
                if s == 0:
                    loads()
                for kc in range(2):
                    S.op('pe', (lambda h, kc=kc: h.matmul(
                        PS[:, sb * 512 + kc * 256: sb * 512 + (kc + 1) * 256], kp[:, s * 256 + kc * 128: s * 256 + (kc + 1) * 128],
                        qp[:, s * 256:(s + 1) * 256], start=True, stop=True)), reads=[kp_b, qp_b], writes=[PSB[sb]], sig=(kc == 1))
                S.op('act', (lambda h: h.activation(out=pt[:, 0:512], in_=bank(sb), func=AF.Exp, scale=ISQ128)),
                     reads=[PSB[sb]], writes=[pt_b])

            def st2(s=s, ob=ob, pt=pt, pt_b=pt_b, rec=rec, rec_b=rec_b):
                for kc in range(2):
                    S.op('pe', (lambda h, kc=kc: h.matmul(
                        PS[:, ob * 512: ob * 512 + 256], vp[:, s * 2 + kc, :], pt[:, kc * 256:(kc + 1) * 256],
                        start=(kc == 0), stop=(kc == 1))), reads=[vp_b, pt_b], writes=[PSB[ob]], sig=False)
                for kc in range(2):
                    S.op('pe', (lambda h, kc=kc: h.matmul(
                        PS[:, ob * 512 + 256: ob * 512 + 512], ones_b[:], pt[:, kc * 256:(kc + 1) * 256],
                        start=(kc == 0), stop=(kc == 1))), reads=[pt_b, PERS_b], writes=[PSB[ob]], sig=(kc == 1))
                S.op('dve', (lambda h: h.reciprocal(out=rec[:, 0:256], in_=PS[:, ob * 512 + 256: ob * 512 + 512])),
                     reads=[PSB[ob]], writes=[rec_b])
                S.op('dve', (lambda h: h.tensor_tensor(
                    out=AT[:, out_chunk, TS + s * 256: TS + (s + 1) * 256], in0=PS[:, ob * 512: ob * 512 + 256], in1=rec[:, 0:256], op=ALU.mult)),
                    reads=[PSB[ob], rec_b], writes=[AT_b[4 + s // 2]])

            stages.append((st1, st2))

    def attn_arena():
        ar = {'k': 0}
        ar['qp'] = [ph.alloc("qp", [128, TP], BF16) for _ in range(2)]
        ar['kp'] = [ph.alloc("kp", [128, TP], BF16) for _ in range(2)]
        ar['vp'] = [ph.alloc("vp", [128, 8, 128], BF16) for _ in range(2)]
        for n in ('qp', 'kp', 'vp'):
            ar[n + '_b'] = [Buf(n + "%d" % i, True) for i in range(2)]
        ar['pt'] = [ph.alloc("pt", [128, 1024], BF16) for _ in range(2)]
        ar['pt_b'] = [Buf("pt%d" % i, True) for i in range(2)]
        ar['rec'] = [ph.alloc("rec", [128, 512], F32) for _ in range(2)]
        ar['rec_b'] = [Buf("rec%d" % i, True) for i in range(2)]
        return ar

    def na_chunks(j):
        r0s = [min(max(r - 4, 0), 24) for r in (2 * j, 2 * j + 1)]
        return list(range(min(r0s) // 2, (max(r0s) + 7) // 2 + 1))

    def na_type(j, m):
        if 2 <= j <= 13:
            return (m - j) + 2
        base = {0: 5, 1: 9, 14: 13, 15: 17}[j]
        return base + (m - (0 if j < 2 else 12))

    def attn_even():
        ph.reset(WB_OFF)
        ar = attn_arena()
        identS = ph.alloc("identS", [128, 128], BF16)
        IDS_b = Buf("identS", True)
        S.op('act', (lambda h: h.activation(out=identS[:], in_=ident_f[:], func=AF.Copy, scale=SQ128)), reads=[PERS_b], writes=[IDS_b])
        kctok = ph.alloc("kctok", [128, 2, 1536], BF16)
        vc = ph.alloc("vc", [128, 2, 1536], BF16)
        kcT = ph.alloc("kcT", [128, 12, 256], BF16)
        KTOK_b, VC_b, KCT_b = Buf("kctok", True), Buf("vc", True), Buf("kcT", True)
        S.dma('pool', kctok[:], cnk_d.rearrange("(c p) d -> p c d", p=128), IN_b, KTOK_b)
        S.dma('pool', vc[:], cnv_d.rearrange("(c p) d -> p c d", p=128), IN_b, VC_b)
        for hh in range(12):
            for c in range(2):
                S.op('pe', (lambda h, hh=hh, c=c: h.transpose(PSbf[:, c * 128:(c + 1) * 128], kctok[:, c, hh * 128:(hh + 1) * 128], ident_b[:])),
                     reads=[KTOK_b, PERS_b], writes=[PSB[6]], sig=(c == 1))
            S.op('dve', (lambda h, hh=hh: h.tensor_copy(out=kcT[:, hh, :], in_=PSbf[:, 0:256])), reads=[PSB[6]], writes=[KCT_b])
        qth = [ph.alloc("qth", [128, TS], BF16) for _ in range(2)]
        kth = [ph.alloc("kth", [128, TS], BF16) for _ in range(2)]
        vh = [ph.alloc("vh", [128, 16, 128], BF16) for _ in range(2)]
        bth = [ph.alloc("bth", [128, 21, 128], BF16) for _ in range(2)]
        qth_b = [Buf("qth%d" % i, True) for i in range(2)]
        kth_b = [Buf("kth%d" % i, True) for i in range(2)]
        vh_b = [Buf("vh%d" % i, True) for i in range(2)]
        bth_b = [Buf("bth%d" % i, True) for i in range(2)]
        cnt = 0
        stages = []
        for hh in range(12):
            i = hh % 2

            def head_loads(hh=hh, i=i):
                S.dma('sp', qth[i][:], QT_d[hh, :, 0:TS], QT_b, qth_b[i])
                S.dma('sp', kth[i][:], KT_d[hh, :, 0:TS], KT_b, kth_b[i])
                S.dma('sp', vh[i][:], V_d[0:TS, hh * 128:(hh + 1) * 128].rearrange("(c p) d -> p c d", p=128), V_b, vh_b[i])
                S.dma('pool', bth[i][:], bt_d[hh].rearrange("t k q -> k t q"), IN_b, bth_b[i])

            for j in range(16):
                ms = na_chunks(j)
                nl = len(ms)
                ncol = (nl + 2) * 128
                sb = cnt % 2
                ob = 4 + cnt % 2
                pt = ar['pt'][cnt % 2]
                pt_b = ar['pt_b'][cnt % 2]
                rec = ar['rec'][cnt % 2]
                rec_b = ar['rec_b'][cnt % 2]
                cnt += 1
                sbufs = [PSB[2 * sb], PSB[2 * sb + 1]]
                base = sb * 1024

                def st1(hh=hh, i=i, j=j, ms=ms, nl=nl, ncol=ncol, pt=pt, pt_b=pt_b, sbufs=sbufs, base=base, head_loads=head_loads):
                    if j == 0:
                        head_loads()
                    for idx, m in enumerate(ms):
                        ty = na_type(j, m)
                        S.op('pe', (lambda h, idx=idx, m=m: h.matmul(
                            PS[:, base + idx * 128: base + (idx + 1) * 128], kth[i][:, m * 128:(m + 1) * 128], qth[i][:, j * 128:(j + 1) * 128],
                            start=True, stop=False)), reads=[kth_b[i], qth_b[i]], writes=sbufs, sig=False)
                        S.op('pe', (lambda h, idx=idx, ty=ty: h.matmul(
                            PS[:, base + idx * 128: base + (idx + 1) * 128], identS[:], bth[i][:, ty, :],
                            start=False, stop=True)), reads=[bth_b[i], IDS_b], writes=sbufs, sig=False)
                    for c in range(2):
                        idx = nl + c
                        S.op('pe', (lambda h, idx=idx, c=c: h.matmul(
                            PS[:, base + idx * 128: base + (idx + 1) * 128], kcT[:, hh, c * 128:(c + 1) * 128], qth[i][:, j * 128:(j + 1) * 128],
                            start=True, stop=True)), reads=[KCT_b, qth_b[i]], writes=sbufs, sig=(c == 1))
                    S.op('act', (lambda h: h.activation(out=pt[:, 0:ncol], in_=PS[:, base:base + ncol], func=AF.Exp, scale=ISQ128)),
                         reads=sbufs, writes=[pt_b])

                def st2(hh=hh, i=i, j=j, ms=ms, nl=nl, ob=ob, pt=pt, pt_b=pt_b, rec=rec, rec_b=rec_b):
                    for idx in range(nl + 2):
                        if idx < nl:
                            lhs = (lambda m=ms[idx]: vh[i][:, m, :])
                            rb = vh_b[i]
                        else:
                            lhs = (lambda c=idx - nl: vc[:, c, hh * 128:(hh + 1) * 128])
                            rb = VC_b
                        S.op('pe', (lambda h, lhs=lhs, idx=idx: h.matmul(
                            PS[:, ob * 512: ob * 512 + 128], lhs(), pt[:, idx * 128:(idx + 1) * 128], start=(idx == 0), stop=(idx == nl + 1))),
                            reads=[rb, pt_b], writes=[PSB[ob]], sig=False)
                    for idx in range(nl + 2):
                        S.op('pe', (lambda h, idx=idx: h.matmul(
                            PS[:, ob * 512 + 128: ob * 512 + 256], ones_b[:], pt[:, idx * 128:(idx + 1) * 128], start=(idx == 0), stop=(idx == nl + 1))),
                            reads=[pt_b, PERS_b], writes=[PSB[ob]], sig=(idx == nl + 1))
                    S.op('dve', (lambda h: h.reciprocal(out=rec[:, 0:128], in_=PS[:, ob * 512 + 128: ob * 512 + 256])),
                         reads=[PSB[ob]], writes=[rec_b])
                    S.op('dve', (lambda h: h.tensor_tensor(
                        out=AT[:, hh, j * 128:(j + 1) * 128], in0=PS[:, ob * 512: ob * 512 + 128], in1=rec[:, 0:128], op=ALU.mult)),
                        reads=[PSB[ob], rec_b], writes=[AT_b[j // 4]])

                stages.append((st1, st2))
        pipeline2(stages)
        ada_pull(1000)
        ph.limit = LIMIT
        stages = []
        for hh in range(12):
            block_attn_prompt(hh, hh, hh * 128, hh, ar, stages)
        pipeline2(stages)
        S.barrier()
        S.recycle()

    def fnet_phase():
        for gh in range(2):
            fnet_half(gh)

    def fnet_half(gh):
        ph.reset()
        xbt = ph.alloc("xbt", [128, 2, T], BF16)
        XBT_b = Buf("xbt", True)
        S.dma('pool', xbt[:], XB_d[gh * 2:gh * 2 + 2].rearrange("g p t -> p g t"), XB_b, XBT_b)
        cc = ph.alloc("cc", [128, 256], BF16)
        fw = ph.alloc("fw", [128, 4, 128], BF16)
        c256 = ph.alloc("c256", [128, 2, 256], BF16)
        s256 = ph.alloc("s256", [128, 2, 256], BF16)
        FC_b = Buf("fconst", True)
        S.dma('pool', cc[:], cc_d, IN_b, FC_b)
        S.dma('pool', fw[:], fw_d.rearrange("g c d -> c g d"), IN_b, FC_b)
        S.dma('pool', c256[:], cn256_d.rearrange("(c p) k -> p c k", p=128), IN_b, FC_b)
        S.dma('pool', s256[:], sn256_d.rearrange("(c p) k -> p c k", p=128), IN_b, FC_b)
        U = ph.alloc("U", [128, 24, 2, 256], BF16)
        U_b = Buf("U", True)
        yb = [ph.alloc("yb", [128, 512], BF16) for _ in range(2)]
        yb_b = [Buf("yb%d" % i, True) for i in range(2)]
        WQ = [nc.alloc_sbuf_tensor_at("wq%d_%d" % (gh, i), [128, 16, 256], BF16, offset=WB_OFF + i * 8192) for i in range(4)]
        WQ_b = [Buf("wq%d" % i, True) for i in range(4)]
        for tt in range(24):
            for g2 in range(1):
                b = next_bank(6)
                for gg in range(2):
                    g = gg
                    S.op('pe', (lambda h, b=b, gg=gg, g=g, tt=tt: h.matmul(
                        PS[:, b * 512 + gg * 256: b * 512 + (gg + 1) * 256], xbt[:, g, tt * 128:(tt + 1) * 128], cc[:],
                        start=True, stop=True)), reads=[XBT_b, FC_b], writes=[PSB[b]], sig=(gg == 1))
                if tt % 2 == 0:
                    S.op('act', (lambda h, b=b, tt=tt, g2=g2: h.activation(
                        out=U[:, tt, g2 * 2:(g2 + 1) * 2, :], in_=bank(b).rearrange("p (g k) -> p g k", k=256), func=AF.Copy)),
                        reads=[PSB[b]], writes=[U_b])
                else:
                    S.op('dve', (lambda h, b=b, tt=tt, g2=g2: h.tensor_copy(
                        out=U[:, tt, g2 * 2:(g2 + 1) * 2, :], in_=bank(b).rearrange("p (g k) -> p g k", k=256))),
                        reads=[PSB[b]], writes=[U_b])

        def step23(g, mm_list, ncols, dst_ap, dst_b):
            b = next_bank(6)
            n = len(mm_list)
            for q, (lhs, rhs, rb) in enumerate(mm_list):
                S.op('pe', (lambda h, b=b, lhs=lhs, rhs=rhs, q=q, n=n: h.matmul(
                    PS[:, b * 512: b * 512 + ncols], lhs(), rhs(), start=(q == 0), stop=(q == n - 1))),
                    reads=[U_b, rb], writes=[PSB[b]], sig=(q == n - 1))
            k = b % 2
            S.op('act', (lambda h, b=b, k=k: h.activation(out=yb[k][:, 0:ncols], in_=PS[:, b * 512: b * 512 + ncols], func=AF.Copy)),
                 reads=[PSB[b]], writes=[yb_b[k]])
            b2 = next_bank(6)
            S.op('pe', (lambda h, b2=b2, k=k, g=g: h.matmul(PS[:, b2 * 512: b2 * 512 + ncols], fw[:, gh * 2 + g, :], yb[k][:, 0:ncols], start=True, stop=True)),
                 reads=[yb_b[k], FC_b], writes=[PSB[b2]], sig=True)
            S.op('dve', (lambda h, b2=b2: h.tensor_copy(out=dst_ap, in_=PS[:, b2 * 512: b2 * 512 + ncols])), reads=[PSB[b2]], writes=[dst_b])

        for kt in range(8):
            q0 = (kt % 2) * 2
            cn, cn_b = WQ[q0], WQ_b[q0]
            sn, sn_b = WQ[q0 + 1], WQ_b[q0 + 1]
            S.dma('pool', cn[:], cn2k_d[:, kt * 256:(kt + 1) * 256].rearrange("(c p) n -> p c n", p=128), IN_b, cn_b)
            S.dma('pool', sn[:], sn2k_d[:, kt * 256:(kt + 1) * 256].rearrange("(c p) n -> p c n", p=128), IN_b, sn_b)
            for g in range(2):
                mm = []
                for n_ in range(16):
                    mm.append(((lambda n_=n_, g=g: U[:, n_, g, 0:128]), (lambda n_=n_, cn=cn: cn[:, n_, :]), cn_b))
                    mm.append(((lambda n_=n_, g=g: U[:, n_, g, 128:256]), (lambda n_=n_, sn=sn: sn[:, n_, :]), sn_b))
                step23(g, mm, 256, AT[:, 12 + gh * 2 + g, kt * 256:(kt + 1) * 256], AT_b[kt // 2])
        for s in range(4):
            for g in range(2):
                mm = []
                for n_ in range(2):
                    mm.append(((lambda n_=n_, g=g, s=s: U[:, 16 + s * 2 + n_, g, 0:128]), (lambda n_=n_: c256[:, n_, :]), FC_b))
                    mm.append(((lambda n_=n_, g=g, s=s: U[:, 16 + s * 2 + n_, g, 128:256]), (lambda n_=n_: s256[:, n_, :]), FC_b))
                step23(g, mm, 256, AT[:, 12 + gh * 2 + g, TS + s * 256: TS + (s + 1) * 256], AT_b[4 + s // 2])
        S.barrier()
        S.recycle()

    def g2_phase(l, w_out_d):
        ph.reset()
        NS = 4
        xi = [ph.alloc("g2xi", [128, 512], F32) for _ in range(NS)]
        xi_b = [Buf("g2xi%d" % i, True) for i in range(NS)]
        xo = [ph.alloc("g2xo", [128, 512], F32) for _ in range(NS)]
        xo_b = [Buf("g2xo%d" % i, True) for i in range(NS)]
        its = [(blk, j, t5) for blk in range(4) for j in range(4) for t5 in range(6)]

        def load(c):
            blk, j, t5 = its[c]
            S.dma('sp', xi[c % NS][:], XT_d[blk * 4 + j, :, t5 * 512:(t5 + 1) * 512], XT_b, xi_b[c % NS])

        load(0)
        load(1)
        wt = wb = None
        for c, (blk, j, t5) in enumerate(its):
            if j == 0 and t5 == 0:
                wt, wb = load_w(wview(w_out_d, blk * 512, 512))
            if c + 2 < len(its):
                load(c + 2)
            k = c % NS
            dc = blk * 4 + j
            b = next_bank(6)
            mm_fm(wt, wb, j, t5, b)
            cond = cond_of_tile(t5)
            S.op('dve', (lambda h, k=k, b=b, dc=dc, cond=cond: h.scalar_tensor_tensor(
                out=xo[k][:], in0=bank(b), scalar=mod_ap(l, 2, dc, cond), in1=xi[k][:], op0=ALU.mult, op1=ALU.add)),
                reads=[PSB[b], xi_b[k], MOD_b], writes=[xo_b[k]])
            S.dma('sp', XT_d[dc, :, t5 * 512:(t5 + 1) * 512], xo[k][:], xo_b[k], XT_b)
        S.barrier()
        S.recycle()

    FFN_TILES = [
        (0, 0, [(0, 1024)], False, True),
        (1024, 0, [(0, 1024)], True, False),
        (2048, 1, [(0, 256), (256, 512), (512, 768), (768, 1024)], False, False),
    ]

    def ffn_phase(l):
        GT = nc.alloc_sbuf_tensor_at("GT%d" % l, [128, 44, 1024], BF16, offset=BASE + 12288)
        GT_b = [Buf("GT%d" % i, True) for i in range(2)]
        ph.reset(WB_OFF)
        ar0 = norm_arena(342, nxi=1)
        norm_mod(l, 1, [(1023, 2)], (lambda dc: hsave[:, dc, 0:2]), [HS_b], 0, ar0)
        S.barrier()
        S.recycle()
        for (t0, cond, segs, lh, rh) in FFN_TILES:
            ph.reset(WB_OFF)
            H2T = ph.alloc("H2T", [128, 16, 1026], BF16)
            H2T_b = Buf("H2T", True)
            wup = [ph.alloc("wup", [128, 16, 256], BF16) for _ in range(4)]
            wup_b = [Buf("wup%d" % i, True) for i in range(4)]
            d_off = ph.off

            def wup_load(pr_, half_, k_):
                col0 = half_ * DFF + pr_ * 256
                S.dma('pool', wup[k_][:], w_up_d[l][:, col0:col0 + 256].rearrange("(c p) n -> p c n", p=128), IN_b, wup_b[k_])

            for pr_ in range(2):
                for half_ in range(2):
                    wup_load(pr_, half_, pr_ * 2 + half_)
            ar = norm_arena(342, nxi=1)
            c0 = t0
            while c0 < t0 + 1024:
                n = min(342, t0 + 1024 - c0)
                hc = 2 + c0 - t0
                norm_mod(l, 1, [(c0, n)], (lambda dc, hc=hc, n=n: H2T[:, dc, hc:hc + n]), [H2T_b], cond, ar)
                c0 += n
            S.barrier()
            ph.reset(d_off)
            acc = [ph.alloc("acc", [128, 1024], F32) for _ in range(4)]
            acc_b = [Buf("acc%d" % i, True) for i in range(4)]
            xi = [ph.alloc("fxi", [128, 512], F32) for _ in range(2)]
            xi_b = [Buf("fxi%d" % i, True) for i in range(2)]
            xo = [ph.alloc("fxo", [128, 512], F32) for _ in range(2)]
            xo_b = [Buf("fxo%d" % i, True) for i in range(2)]
            hal = [ph.alloc("hal", [128, 2], F32) for _ in range(2)]
            hal_b = [Buf("hal%d" % i, True) for i in range(2)]
            ucnt = 0
            wcnt = 0
            for pr in range(22):
                slots = []
                for half in range(2):
                    k = wcnt % 4
                    wcnt += 1
                    if pr >= 2:
                        wup_load(pr, half, k)
                    slots.append(k)
                for cc_ in range(2):
                    fc = pr * 2 + cc_
                    accs = []
                    for half in range(2):
                        k = slots[half]
                        u = ucnt % 3
                        hs = ucnt % 4
                        ucnt += 1
                        a = acc[(fc % 2) * 2 + half]
                        a_b = acc_b[(fc % 2) * 2 + half]
                        ubufs = [PSB[2 * u], PSB[2 * u + 1]]
                        fidx = half * 44 + fc
                        for hf in range(2):
                            for kc in range(KC):
                                S.op('pe', (lambda h, u=u, hf=hf, kc=kc, k=k, cc_=cc_: h.matmul(
                                    PS[:, u * 1024 + hf * 512: u * 1024 + (hf + 1) * 512], wup[k][:, kc, cc_ * 128:(cc_ + 1) * 128],
                                    H2T[:, kc, 2 + hf * 512: 2 + (hf + 1) * 512], start=(kc == 0), stop=(kc == KC - 1))),
                                    reads=[wup_b[k], H2T_b], writes=[ubufs[hf]], sig=(kc == KC - 1))
                        if lh or rh:
                            for kc in range(KC):
                                S.op('pe', (lambda h, hs=hs, kc=kc, k=k, cc_=cc_: h.matmul(
                                    PS[:, 6 * 512 + hs * 2: 6 * 512 + hs * 2 + 2], wup[k][:, kc, cc_ * 128:(cc_ + 1) * 128],
                                    hsave[:, kc, 0:2], start=(kc == 0), stop=(kc == KC - 1))),
                                    reads=[wup_b[k], HS_b], writes=[PSB[6]], sig=(kc == KC - 1))
                        S.op('act', (lambda h, a=a, u=u, fidx=fidx: h.activation(
                            out=a[:], in_=PS[:, u * 1024:(u + 1) * 1024], func=AF.Identity, scale=cw[:, l, fidx, 1:2], bias=cb[:, l, fidx:fidx + 1])),
                            reads=ubufs + [PERS_b], writes=[a_b])
                        for (sa, sbb) in segs:
                            S.op('dve', (lambda h, a=a, u=u, sa=sa, sbb=sbb, fidx=fidx: h.scalar_tensor_tensor(
                                out=a[:, sa + 1:sbb], in0=PS[:, u * 1024 + sa: u * 1024 + sbb - 1], scalar=cw[:, l, fidx, 0:1], in1=a[:, sa + 1:sbb],
                                op0=ALU.mult, op1=ALU.add)), reads=ubufs + [a_b, PERS_b], writes=[a_b])
                            S.op('dve', (lambda h, a=a, u=u, sa=sa, sbb=sbb, fidx=fidx: h.scalar_tensor_tensor(
                                out=a[:, sa:sbb - 1], in0=PS[:, u * 1024 + sa + 1: u * 1024 + sbb], scalar=cw[:, l, fidx, 2:3], in1=a[:, sa:sbb - 1],
                                op0=ALU.mult, op1=ALU.add)), reads=ubufs + [a_b, PERS_b], writes=[a_b])
                        if lh or rh:
                            hl = hal[hs % 2]
                            hl_b = hal_b[hs % 2]
                            S.op('act', (lambda h, hl=hl, hs=hs: h.activation(out=hl[:], in_=PS[:, 6 * 512 + hs * 2: 6 * 512 + hs * 2 + 2], func=AF.Copy)),
                                 reads=[PSB[6]], writes=[hl_b])
                        if lh:
                            S.op('dve', (lambda h, a=a, hl=hl, fidx=fidx: h.scalar_tensor_tensor(
                                out=a[:, 0:1], in0=hl[:, 0:1], scalar=cw[:, l, fidx, 0:1], in1=a[:, 0:1],
                                op0=ALU.mult, op1=ALU.add)), reads=[hl_b, a_b, PERS_b], writes=[a_b])
                        if rh:
                            S.op('dve', (lambda h, a=a, hl=hl, fidx=fidx: h.scalar_tensor_tensor(
                                out=a[:, 1023:1024], in0=hl[:, 1:2], scalar=cw[:, l, fidx, 2:3], in1=a[:, 1023:1024],
                                op0=ALU.mult, op1=ALU.add)), reads=[hl_b, a_b, PERS_b], writes=[a_b])
                        accs.append((a, a_b))
                    (av, av_b), (ag, ag_b) = accs
                    S.op('act', (lambda h, ag=ag: h.activation(out=ag[:], in_=ag[:], func=AF.Silu)), reads=[ag_b], writes=[ag_b])
                    S.op('dve', (lambda h, ag=ag, av=av, fc=fc: h.tensor_tensor(out=GT[:, fc, :], in0=ag[:], in1=av[:], op=ALU.mult)),
                         reads=[ag_b, av_b], writes=[GT_b[fc % 2]])
            S.barrier()
            wdn = [nc.alloc_sbuf_tensor_at("wdn%d_%d_%d" % (l, t0, i), [128, 44, 256], BF16, offset=WB_OFF + i * 22528) for i in range(2)]
            wdn_b = [Buf("wdn%d" % i, True) for i in range(2)]
            assert WB_OFF + 2 * 22528 <= d_off
            c = 0
            for nb in range(8):
                k = nb % 2
                S.dma('pool', wdn[k][:], w_down_d[l][:, nb * 256:(nb + 1) * 256].rearrange("(c p) n -> p c n", p=128), IN_b, wdn_b[k])
                for dj in range(2):
                    dc = nb * 2 + dj
                    for hf in range(2):
                        q = c % 2
                        c += 1
                        col = t0 + hf * 512
                        S.dma('sp', xi[q][:], XT_d[dc, :, col:col + 512], XT_b, xi_b[q])
                        b = next_bank(6)
                        for fc in range(44):
                            S.op('pe', (lambda h, b=b, k=k, fc=fc, dj=dj, hf=hf: h.matmul(
                                bank(b), wdn[k][:, fc, dj * 128:(dj + 1) * 128], GT[:, fc, hf * 512:(hf + 1) * 512],
                                start=(fc == 0), stop=(fc == 43))), reads=[wdn_b[k], GT_b[0], GT_b[1]], writes=[PSB[b]], sig=(fc == 43))
                        S.op('dve', (lambda h, q=q, b=b, dc=dc, cond=cond, xo=xo, xi=xi: h.scalar_tensor_tensor(
                            out=xo[q][:], in0=bank(b), scalar=mod_ap(l, 5, dc, cond), in1=xi[q][:], op0=ALU.mult, op1=ALU.add)),
                            reads=[PSB[b], xi_b[q], MOD_b], writes=[xo_b[q]])
                        S.dma('sp', XT_d[dc, :, col:col + 512], xo[q][:], xo_b[q], XT_b)
            S.barrier()
            S.recycle()

    def g1_odd():
        ph.reset()
        sgb = [ph.alloc("sgb", [128, 512], BF16) for _ in range(3)]
        sgb_b = [Buf("osgb%d" % i, True) for i in range(3)]
        sgf = [ph.alloc("sgf", [128, 512], F32) for _ in range(3)]
        sgf_b = [Buf("osgf%d" % i, True) for i in range(3)]
        cnts = {'b': 0, 'f': 0, 't': 0}

        def nb():
            i = cnts['b'] % 3
            cnts['b'] += 1
            return sgb[i], sgb_b[i]

        def nf():
            i = cnts['f'] % 3
            cnts['f'] += 1
            return sgf[i], sgf_b[i]

        names = ('qf', 'sq', 'qn', 'qg', 't1', 't2')
        tm = {n: [ph.alloc(n, [128, 512], F32) for _ in range(3)] for n in names}
        tm_b = {n: [Buf("o%s%d" % (n, i), True) for i in range(3)] for n in names}
        stages3 = []
        ssb = [ph.alloc("ss", [128, 4], F32) for _ in range(3)]
        ss_b = [Buf("oss%d" % i, True) for i in range(3)]
        qbb = [ph.alloc("qb", [128, 512], BF16) for _ in range(3)]
        qb_b = [Buf("oqb%d" % i, True) for i in range(3)]
        tst = [ph.alloc("tst", [128, 4, 128], BF16) for _ in range(3)]
        tst_b = [Buf("otst%d" % i, True) for i in range(3)]
        ctb = [ph.alloc("ct", [128, 128], F32) for _ in range(3)]
        stb = [ph.alloc("stt", [128, 128], F32) for _ in range(3)]
        ct_b = [Buf("oct%d" % i, True) for i in range(3)]
        st_b = [Buf("ost%d" % i, True) for i in range(3)]
        g4 = ph.alloc("g4", [128, 2, 512], F32)
        G4_b = Buf("g4", True)
        S.dma('sp', g4[:, 0, :], gq4_d.partition_broadcast(128), IN_b, G4_b)
        S.dma('sp', g4[:, 1, :], gk4_d.partition_broadcast(128), IN_b, G4_b)

        wref = {}
        for blk in range(6):
            if blk in (0, 5):
                wt, wb = load_w(wview(od_w_in_d, blk * 512, 512))
            if blk == 0:
                for j in range(4):
                    for t5 in range(6):
                        b = next_bank(6)
                        mm_fm(wt, wb, j, t5, b)
                        sg, sg_b = nf()
                        S.op('act', (lambda h, sg=sg, b=b: h.activation(out=sg[:], in_=bank(b), func=AF.Copy)), reads=[PSB[b]], writes=[sg_b])
                        S.dma('sp', XB_d[j, :, t5 * 512:(t5 + 1) * 512], sg[:], sg_b, XB_b)
            elif blk == 5:
                for tt in range(24):
                    b = next_bank(6)
                    mm_tm(wt, wb, tt, b)
                    sg, sg_b = nb()
                    if tt < 16:
                        S.op('dve', (lambda h, sg=sg, b=b: h.tensor_copy(out=sg[:], in_=bank(b))), reads=[PSB[b]], writes=[sg_b])
                    else:
                        sf, sf_b = nf()
                        S.op('act', (lambda h, sf=sf, b=b: h.activation(out=sf[:], in_=bank(b), func=AF.Copy)), reads=[PSB[b]], writes=[sf_b])
                        S.op('dve', (lambda h, sg=sg, sf=sf: h.tensor_copy(out=sg[:], in_=sf[:])), reads=[sf_b], writes=[sg_b])
                        S.dma('sp', gqv_d[(tt - 16) * 128:(tt - 15) * 128, :], sf[:], sf_b, OUT_b)
                    S.dma('sp', V_d[tt * 128:(tt + 1) * 128, 0:512], sg[:], sg_b, V_b)
            else:
                isk = (blk == 4)
                for tt in range(24):
                    k = cnts['t'] % 3
                    hf = cnts['t'] % 2
                    cnts['t'] += 1
                    qf, sq, qn, qg, t1, t2 = (tm[n][k] for n in names)
                    qf_b, sq_b, qn_b, qg_b, t1_b, t2_b = (tm_b[n][k] for n in names)
                    ss, ss_bb = ssb[k], ss_b[k]
                    qb, qb_bb = qbb[k], qb_b[k]
                    ct, stt = ctb[k], stb[k]
                    ts_, ts_b = tst[k], tst_b[k]
                    gi = 1 if isk else 0

                    def st1(blk=blk, tt=tt, qf=qf, qf_b=qf_b, sq=sq, sq_b=sq_b, ss=ss, ss_bb=ss_bb):
                        if tt == 0:
                            wref[blk] = load_w(wview(od_w_in_d, blk * 512, 512))
                        wt, wb = wref[blk]
                        b = next_bank(6)
                        mm_tm(wt, wb, tt, b)
                        S.op('act', (lambda h: h.activation(out=qf[:], in_=bank(b), func=AF.Copy)), reads=[PSB[b]], writes=[qf_b])
                        S.op('act', (lambda h: h.activation(out=sq[:], in_=qf[:], func=AF.Square)), reads=[qf_b], writes=[sq_b])
                        S.op('dve', (lambda h: h.tensor_reduce(out=ss[:], in_=sq[:].rearrange("p (h d) -> p h d", d=128), axis=AX.X, op=ALU.add)),
                             reads=[sq_b], writes=[ss_bb])
                        S.op('dve', (lambda h: h.tensor_scalar(out=ss[:], in0=ss[:], scalar1=1.0 / 128.0, scalar2=EPS, op0=ALU.mult, op1=ALU.add)),
                             reads=[ss_bb], writes=[ss_bb])
                        S.op('act', (lambda h: h.activation(out=ss[:], in_=ss[:], func=AF.Sqrt)), reads=[ss_bb], writes=[ss_bb])

                    def st2(tt=tt, k=k, isk=isk, gi=gi, qf=qf, qf_b=qf_b, qn=qn, qn_b=qn_b, qg=qg, qg_b=qg_b, t1=t1, t1_b=t1_b, t2=t2, t2_b=t2_b,
                            ss=ss, ss_bb=ss_bb, qb=qb, qb_bb=qb_bb, ct=ct, stt=stt):
                        S.op('dve', (lambda h: h.reciprocal(out=ss[:], in_=ss[:])), reads=[ss_bb], writes=[ss_bb])
                        S.op('dve', (lambda h: h.tensor_tensor(
                            out=qn[:].rearrange("p (h d) -> p h d", d=128), in0=qf[:].rearrange("p (h d) -> p h d", d=128),
                            in1=ss[:].unsqueeze(2).to_broadcast([128, 4, 128]), op=ALU.mult)), reads=[qf_b, ss_bb], writes=[qn_b])
                        S.op('dve', (lambda h: h.tensor_tensor(out=qg[:], in0=qn[:], in1=g4[:, gi, :], op=ALU.mult)),
                             reads=[qn_b, G4_b], writes=[qg_b])
                        if tt >= 16:
                            if isk:
                                S.dma('sp', gqk_d[(tt - 16) * 128:(tt - 15) * 128, :], qg[:], qg_b, OUT_b)
                            S.op('act', (lambda h: h.activation(out=qb[:], in_=qg[:], func=AF.Copy)), reads=[qg_b], writes=[qb_bb])
                        else:
                            S.dma('sp', ct[:], ropeC_d[tt * 128:(tt + 1) * 128, :], IN_b, ct_b[k])
                            S.dma('sp', stt[:], ropeS_d[tt * 128:(tt + 1) * 128, :], IN_b, st_b[k])
                            S.op('dve', (lambda h: h.tensor_tensor(
                                out=t1[:].rearrange("p (h d) -> p h d", d=128), in0=qg[:].rearrange("p (h d) -> p h d", d=128),
                                in1=ct[:].unsqueeze(1).to_broadcast([128, 4, 128]), op=ALU.mult)), reads=[qg_b, ct_b[k]], writes=[t1_b])
                            for s_ in range(2):
                                S.op('dve', (lambda h, s_=s_: h.tensor_tensor(
                                    out=t2[:].rearrange("p (h r s i) -> p h r s i", h=4, r=2, s=2)[:, :, :, s_, :],
                                    in0=qg[:].rearrange("p (h r s i) -> p h r s i", h=4, r=2, s=2)[:, :, :, 1 - s_, :],
                                    in1=stt[:].rearrange("p (r s i) -> p r s i", r=2, s=2)[:, :, s_, :].unsqueeze(1).to_broadcast([128, 4, 2, 32]),
                                    op=ALU.mult)), reads=[qg_b, st_b[k]], writes=[t2_b])
                            S.op('dve', (lambda h: h.tensor_tensor(out=qb[:], in0=t1[:], in1=t2[:], op=ALU.add)),
                                 reads=[t1_b, t2_b], writes=[qb_bb])

                    def st3(tt=tt, blk=blk, isk=isk, hf=hf, qb=qb, qb_bb=qb_bb, ts_=ts_, ts_b=ts_b):
                        for hh in range(4):
                            S.op('pe', (lambda h, hh=hh: h.transpose(PSbf[:, hf * 512 + hh * 128: hf * 512 + (hh + 1) * 128],
                                                                     qb[:, hh * 128:(hh + 1) * 128], ident_b[:])),
                                 reads=[qb_bb, PERS_b], writes=[PSB[6]], sig=(hh == 3))
                        S.op('act', (lambda h: h.activation(out=ts_[:], in_=PSbf[:, hf * 512:(hf + 1) * 512].rearrange("p (h t) -> p h t", t=128), func=AF.Copy)),
                             reads=[PSB[6]], writes=[ts_b])
                        if isk:
                            S.dma('sp', KT_d[0:4, :, tt * 128:(tt + 1) * 128].rearrange("h p t -> p h t"), ts_[:], ts_b, KT_b)
                        else:
                            h0 = (blk - 1) * 4
                            S.dma('sp', QT_d[h0:h0 + 4, :, tt * 128:(tt + 1) * 128].rearrange("h p t -> p h t"), ts_[:], ts_b, QT_b)

                    stages3.append((st1, st2, st3))
        n3 = len(stages3)
        for i in range(n3 + 2):
            if i < n3:
                stages3[i][0]()
            if 0 <= i - 1 < n3:
                stages3[i - 1][1]()
            if 0 <= i - 2 < n3:
                stages3[i - 2][2]()
        S.barrier()
        S.recycle()

    def attn_odd():
        ph.reset(WB_OFF)
        ar = attn_arena()
        kctok = ph.alloc("gkctok", [128, 2, 512], BF16)
        vc = ph.alloc("gvc", [128, 2, 512], BF16)
        kcT = ph.alloc("gkcT", [128, 4, 256], BF16)
        KTOK_b, VC_b, KCT_b = Buf("gkctok", True), Buf("gvc", True), Buf("gkcT", True)
        S.dma('pool', kctok[:], cgk_d.rearrange("(c p) d -> p c d", p=128), IN_b, KTOK_b)
        S.dma('pool', vc[:], cgv_d.rearrange("(c p) d -> p c d", p=128), IN_b, VC_b)
        for kv in range(4):
            for c in range(2):
                S.op('pe', (lambda h, kv=kv, c=c: h.transpose(PSbf[:, c * 128:(c + 1) * 128], kctok[:, c, kv * 128:(kv + 1) * 128], ident_b[:])),
                     reads=[KTOK_b, PERS_b], writes=[PSB[6]], sig=(c == 1))
            S.op('dve', (lambda h, kv=kv: h.tensor_copy(out=kcT[:, kv, :], in_=PSbf[:, 0:256])), reads=[PSB[6]], writes=[KCT_b])
        qs = [ph.alloc("gqs", [128, TS], BF16) for _ in range(2)]
        ks = [ph.alloc("gks", [128, TS], BF16) for _ in range(2)]
        vs = [ph.alloc("gvs", [128, 16, 128], BF16) for _ in range(2)]
        qs_b = [Buf("gqs%d" % i, True) for i in range(2)]
        ks_b = [Buf("gks%d" % i, True) for i in range(2)]
        vs_b = [Buf("gvs%d" % i, True) for i in range(2)]
        ptb = [ph.alloc("gpt", [128, 18, 512], BF16) for _ in range(2)]
        ptb_b = [Buf("gpt%d" % i, True) for i in range(2)]
        recb = ph.alloc("grec", [128, 512], F32)
        recb_b = Buf("grec", True)
        c2 = 0
        scn = [0]
        stages = []
        for hq in range(12):
            kv = hq // 3
            i = hq % 2

            def head_loads(hq=hq, kv=kv, i=i):
                S.dma('sp', qs[i][:], QT_d[hq, :, 0:TS], QT_b, qs_b[i])
                S.dma('sp', ks[i][:], KT_d[kv, :, 0:TS], KT_b, ks_b[i])
                S.dma('sp', vs[i][:], V_d[0:TS, kv * 128:(kv + 1) * 128].rearrange("(c p) d -> p c d", p=128), V_b, vs_b[i])

            for qt in range(4):
                pt = ptb[c2 % 2]
                pt_b = ptb_b[c2 % 2]
                c2 += 1

                def st1(hq=hq, kv=kv, i=i, qt=qt, pt=pt, pt_b=pt_b, head_loads=head_loads):
                    if qt == 0:
                        head_loads()
                    for kc in range(18):
                        sb = scn[0] % 4
                        scn[0] += 1
                        if kc < 2:
                            lhs = (lambda kc=kc: kcT[:, kv, kc * 128:(kc + 1) * 128])
                            rb = KCT_b
                        else:
                            lhs = (lambda kc=kc: ks[i][:, (kc - 2) * 128:(kc - 1) * 128])
                            rb = ks_b[i]
                        S.op('pe', (lambda h, sb=sb, lhs=lhs: h.matmul(bank(sb), lhs(), qs[i][:, qt * 512:(qt + 1) * 512], start=True, stop=True)),
                             reads=[rb, qs_b[i]], writes=[PSB[sb]], sig=True)
                        S.op('act', (lambda h, kc=kc, sb=sb: h.activation(out=pt[:, kc, :], in_=bank(sb), func=AF.Exp, scale=ISQ128)),
                             reads=[PSB[sb]], writes=[pt_b])

                def st2(hq=hq, kv=kv, i=i, qt=qt, pt=pt, pt_b=pt_b):
                    for kc in range(18):
                        if kc < 2:
                            lhs = (lambda kc=kc: vc[:, kc, kv * 128:(kv + 1) * 128])
                            rb = VC_b
                        else:
                            lhs = (lambda kc=kc: vs[i][:, kc - 2, :])
                            rb = vs_b[i]
                        S.op('pe', (lambda h, lhs=lhs, kc=kc: h.matmul(bank(4), lhs(), pt[:, kc, :], start=(kc == 0), stop=(kc == 17))),
                             reads=[rb, pt_b], writes=[PSB[4]], sig=(kc == 17))
                    for kc in range(18):
                        S.op('pe', (lambda h, kc=kc: h.matmul(bank(5), ones_b[:], pt[:, kc, :], start=(kc == 0), stop=(kc == 17))),
                             reads=[pt_b, PERS_b], writes=[PSB[5]], sig=(kc == 17))
                    S.op('dve', (lambda h: h.reciprocal(out=recb[:], in_=bank(5))), reads=[PSB[5]], writes=[recb_b])
                    S.op('dve', (lambda h: h.tensor_tensor(out=AT[:, 4 + hq, qt * 512:(qt + 1) * 512], in0=bank(4), in1=recb[:], op=ALU.mult)),
                         reads=[PSB[4], recb_b], writes=[AT_b[qt]])

                stages.append((st1, st2))
        pipeline2(stages)
        stages = []
        for hq in range(12):
            block_attn_prompt(hq, hq // 3, (hq // 3) * 128, 4 + hq, ar, stages)
        pipeline2(stages)
        S.barrier()
        S.recycle()

    def pool_phase():
        ph.reset(WB_OFF)
        pw = ph.alloc("pw", [128, 4, 128], BF16)
        psc = ph.alloc("psc", [128, 4], F32)
        PC_b = Buf("pconst", True)
        S.dma('pool', pw[:], pw_d.rearrange("g c d -> c g d"), IN_b, PC_b)
        S.dma('sp', psc[:], pscT_d, IN_b, PC_b)
        L = TS + 32
        X0 = ph.alloc("px0", [128, L], F32)
        A = ph.alloc("pA", [128, L], F32)
        Bb = ph.alloc("pB", [128, L], F32)
        invc = ph.alloc("pinvc", [128, TS], F32)
        tmp = ph.alloc("ptmp", [128, TS], F32)
        pooled = ph.alloc("ppooled", [128, TS], BF16)
        X0_b, A_b, B_b, I_b, T_b, P_b = (Buf(n, True) for n in ("px0", "pA", "pB", "pinvc", "ptmp", "ppooled"))
        S.op('dve', (lambda h: h.memset(X0[:], 0.0)), writes=[X0_b])

        def chain(levels, Lx):
            cur, cur_b = X0, X0_b
            outs = [(A, A_b), (Bb, B_b)]
            for lev in range(1, levels + 1):
                o, o_b = outs[lev % 2]
                if lev == 1:
                    S.op('dve', (lambda h, o=o, cur=cur: h.tensor_tensor(out=o[:, 1:Lx], in0=cur[:, 0:Lx - 1], in1=cur[:, 1:Lx], op=ALU.add)),
                         reads=[cur_b], writes=[o_b])
                else:
                    d = 2 ** (lev - 2)
                    S.op('dve', (lambda h, o=o, cur=cur, d=d: h.tensor_tensor(out=o[:, d:Lx - d], in0=cur[:, 0:Lx - 2 * d], in1=cur[:, 2 * d:Lx], op=ALU.add)),
                         reads=[cur_b], writes=[o_b])
                cur, cur_b = o, o_b
            return cur, cur_b

        for g in range(4):
            S.dma('sp', X0[:, 16:16 + TS], XB_d[g, :, 0:TS], XB_b, X0_b)
            S.dma('sp', invc[:], invc_s_d[g].partition_broadcast(128), IN_b, I_b)
            sw, sw_b = chain(g + 1, L)
            S.op('dve', (lambda h, sw=sw: h.tensor_tensor(out=tmp[:], in0=sw[:, 16:16 + TS], in1=invc[:], op=ALU.mult)), reads=[sw_b, I_b], writes=[T_b])
            S.op('dve', (lambda h: h.tensor_tensor(out=pooled[:], in0=tmp[:], in1=X0[:, 16:16 + TS], op=ALU.subtract)), reads=[T_b, X0_b], writes=[P_b])
            for t5 in range(4):
                b = next_bank(6)
                S.op('pe', (lambda h, b=b, g=g, t5=t5: h.matmul(bank(b), pw[:, g, :], pooled[:, t5 * 512:(t5 + 1) * 512], start=True, stop=True)),
                     reads=[P_b, PC_b], writes=[PSB[b]], sig=True)
                S.op('act', (lambda h, b=b, g=g, t5=t5: h.activation(out=AT[:, g, t5 * 512:(t5 + 1) * 512], in_=bank(b), func=AF.Copy, scale=psc[:, g:g + 1])),
                     reads=[PSB[b], PC_b], writes=[AT_b[t5]])
        S.op('dve', (lambda h: h.memset(X0[:], 0.0)), writes=[X0_b])
        Lp = 256 + 32
        for g in range(4):
            S.dma('sp', invc[:, 0:256], invc_p_d[g].partition_broadcast(128), IN_b, I_b)
            for s in range(4):
                S.dma('sp', X0[:, 16:16 + 256], XB_d[g, :, TS + s * 256: TS + (s + 1) * 256], XB_b, X0_b)
                sw, sw_b = chain(g + 1, Lp)
                S.op('dve', (lambda h, sw=sw: h.tensor_tensor(out=tmp[:, 0:256], in0=sw[:, 16:16 + 256], in1=invc[:, 0:256], op=ALU.mult)),
                     reads=[sw_b, I_b], writes=[T_b])
                S.op('dve', (lambda h, s=s: h.tensor_tensor(out=pooled[:, s * 256:(s + 1) * 256], in0=tmp[:, 0:256], in1=X0[:, 16:16 + 256], op=ALU.subtract)),
                     reads=[T_b, X0_b], writes=[P_b])
            for t5 in range(2):
                b = next_bank(6)
                S.op('pe', (lambda h, b=b, g=g, t5=t5: h.matmul(bank(b), pw[:, g, :], pooled[:, t5 * 512:(t5 + 1) * 512], start=True, stop=True)),
                     reads=[P_b, PC_b], writes=[PSB[b]], sig=True)
                S.op('act', (lambda h, b=b, g=g, t5=t5: h.activation(out=AT[:, g, TS + t5 * 512: TS + (t5 + 1) * 512], in_=bank(b), func=AF.Copy, scale=psc[:, g:g + 1])),
                     reads=[PSB[b], PC_b], writes=[AT_b[4 + t5]])
        S.barrier()
        S.recycle()

    def final_phase():
        ph.reset(WB_OFF)
        ar = norm_arena(512, nxi=2, nsq=4, ntmp=1)
        yT = [ph.alloc("yT", [128, 4, 512], F32) for _ in range(2)]
        yT_b = [Buf("yT%d" % i, True) for i in range(2)]
        ytok = [ph.alloc("ytok", [128, 512], F32) for _ in range(4)]
        ytok_b = [Buf("ytok%d" % i, True) for i in range(4)]
        cnt = {'y': 0, 'q': 0}

        def stage_b(t5, xi, xi_b, rstd, rstd_b):
            for dq in range(4):
                y = yT[cnt['q'] % 2]
                y_b = yT_b[cnt['q'] % 2]
                cnt['q'] += 1
                for j in range(4):
                    dc = dq * 4 + j
                    S.op('dve', (lambda h, y=y, j=j, dc=dc: h.scalar_tensor_tensor(
                        out=y[:, j, :], in0=xi[:, dc, 0:512], scalar=ngT[:, 4, dc:dc + 1], in1=rstd[:, 0:512], op0=ALU.mult, op1=ALU.mult)),
                        reads=[xi_b, rstd_b, PERS_b], writes=[y_b])
                for ts in range(4):
                    b = next_bank(6)
                    for j in range(4):
                        S.op('pe', (lambda h, b=b, j=j, y=y, ts=ts: h.transpose(
                            PS[:, b * 512 + j * 128: b * 512 + (j + 1) * 128], y[:, j, ts * 128:(ts + 1) * 128], ident_f[:])),
                            reads=[y_b, PERS_b], writes=[PSB[b]], sig=(j == 3))
                    k = cnt['y'] % 4
                    cnt['y'] += 1
                    S.op('act', (lambda h, b=b, k=k: h.activation(out=ytok[k][:], in_=bank(b), func=AF.Copy)),
                         reads=[PSB[b]], writes=[ytok_b[k]])
                    tok0 = t5 * 512 + ts * 128
                    if tok0 < TS:
                        S.dma('sp', ys_d[tok0:tok0 + 128, dq * 512:(dq + 1) * 512], ytok[k][:], ytok_b[k], OUT_b)
                    else:
                        S.dma('sp', yp_d[tok0 - TS:tok0 - TS + 128, dq * 512:(dq + 1) * 512], ytok[k][:], ytok_b[k], OUT_b)

        prev = None
        for t5 in range(6):
            st = norm_stats([(t5 * 512, 512)], ar)
            if prev is not None:
                stage_b(*prev)
            prev = (t5,) + tuple(st)
        stage_b(*prev)
        S.barrier()
        S.recycle()

    STAGES = ['n1_0', 'g1_0', 'attn_0', 'fnet_0', 'mix0', 'ffn0', 'n1_1', 'g1_1', 'attn_1', 'pool_1', 'mix1', 'ffn1', 'final']
    nstage = len(STAGES) if stop_after is None else STAGES.index(stop_after) + 1
    fns = [in_phase, g1_even, attn_even, fnet_phase, lambda: g2_phase(0, ev_w_out_d), lambda: ffn_phase(0),
           lambda: norm1_phase(1), g1_odd, attn_odd, pool_phase, lambda: g2_phase(1, od_w_out_d), lambda: ffn_phase(1), final_phase]
    for fn in fns[:nstage]:
        fn()
    S.finish()
    S.emit()
    st.close()
    return nc


def _build_bt(bias):
    NEG = np.float32(-30000.0)
    kc = np.arange(64)[:, None]
    qc = np.arange(64)[None, :]
    qstart = np.clip(qc - 8, 0, 48)
    colvalid = (kc >= qstart) & (kc < qstart + 16)
    dcidx = np.clip(kc - qc, -15, 15) + 15
    types = ([(5, 5 + dm) for dm in range(-2, 3)] + [(0, m) for m in range(4)] + [(1, m) for m in range(4)]
             + [(14, m) for m in range(12, 16)] + [(15, m) for m in range(12, 16)])
    bt = np.full((12, 21, 128, 128), NEG, np.float32)
    for ti, (j, m) in enumerate(types):
        for a in range(2):
            for b in range(2):
                kr = 2 * m + a
                r = 2 * j + b
                r0 = min(max(r - 4, 0), 24)
                if not (r0 <= kr < r0 + 8):
                    continue
                blk = bias[:, kr - r + 7][:, dcidx]
                bt[:, ti, a * 64:(a + 1) * 64, b * 64:(b + 1) * 64] = np.where(colvalid[None], blk, NEG)
    return bt


def _dft(n):
    k = np.arange(n, dtype=np.int64)
    ang = 2.0 * np.pi * ((k[:, None] * k[None, :]) % n).astype(np.float64) / n
    return (np.cos(ang) / np.sqrt(n)).astype(np.float32), (np.sin(ang) / np.sqrt(n)).astype(np.float32)


_CONST = {}


def _consts():
    if _CONST:
        return _CONST
    c2k, s2k = _dft(2048)
    c256, s256 = _dft(256)
    c128, s128 = _dft(128)
    t = np.arange(TS)
    pos = np.stack([(t // 64), (t % 64)], 1).astype(np.float32)
    inv = (10000.0 ** (-np.arange(0, 64, 2, dtype=np.float32) / 64.0)).astype(np.float32)
    ang = pos[:, :, None] * inv[None, None, :]
    cs = np.cos(ang).astype(np.float32)
    sn = np.sin(ang).astype(np.float32)
    ropeC = np.stack([cs, cs], 2).reshape(TS, 128)
    ropeS = np.stack([-sn, sn], 2).reshape(TS, 128)

    def invc(n):
        tt = np.arange(n)
        out = []
        for w in (2, 4, 8, 16):
            lo = np.clip(tt - w // 2, 0, n)
            hi = np.clip(tt + w // 2, 0, n)
            out.append(1.0 / (hi - lo).astype(np.float32))
        return np.stack(out, 0).astype(np.float32)

    _CONST.update(cn2k=c2k, sn2k=s2k, cn256=c256, sn256=s256, cc=np.ascontiguousarray(np.concatenate([c128, -s128], 1)),
                  ropeC=np.ascontiguousarray(ropeC), ropeS=np.ascontiguousarray(ropeS), invc_s=invc(TS), invc_p=invc(256),
                  ident=np.eye(128, dtype=np.float32))
    return _CONST


def make_in_maps(inp):
    f = lambda a: np.ascontiguousarray(np.asarray(a, dtype=np.float32))
    shared = {
        "ada_w": f(inp["ada_w"]),
        "ada_bT": f(np.asarray(inp["ada_b"]).reshape(2, 96, 128).transpose(0, 2, 1)),
        "ngT": f(np.stack([inp["norm1_g"][0], inp["norm1_g"][1], inp["norm2_g"][0], inp["norm2_g"][1], inp["final_norm_g"]], 0)
                 .reshape(5, 16, 128).transpose(2, 0, 1)),
        "ev_w_in": f(inp["ev_w_in"][0]),
        "ev_w_out": f(inp["ev_w_out"][0]),
        "od_w_in": f(inp["od_w_in"][0]),
        "od_w_out": f(inp["od_w_out"][0]),
        "w_up": f(inp["ffn_w_up"]),
        "w_down": f(inp["ffn_w_down"]),
        "cwT": f(np.asarray(inp["ffn_conv_w"]).reshape(2, 3, 88, 128).transpose(3, 0, 2, 1)),
        "cbT": f(np.asarray(inp["ffn_conv_b"]).reshape(2, 88, 128).transpose(2, 0, 1)),
        "fin_g": f(inp["final_norm_g"]),
        "bt": _build_bt(f(inp["ev_na_bias"][0])),
        "fw": f(inp["ev_fnet_w"][0]),
        "pw": f(inp["od_pool_w"][0]),
        "pscT": f(np.asarray(inp["od_pool_scale"][0]).reshape(4, 128).T),
        "gq4": f(np.tile(np.asarray(inp["od_q_norm_g"][0]), 4)),
        "gk4": f(np.tile(np.asarray(inp["od_k_norm_g"][0]), 4)),
    }
    shared.update(_consts())
    maps = []
    for c in range(NCORES):
        m = dict(shared)
        m["xs"] = f(inp["x_sample"][c])
        m["xp"] = f(np.asarray(inp["x_prompt"][4 * c:4 * c + 4]).reshape(TP, D))
        m["condT"] = f(np.stack([inp["c"][c], inp["c_ctx"]], axis=1))
        m["cnk"] = f(np.asarray(inp["cache_na_k"][c, 0]).reshape(256, 1536))
        m["cnv"] = f(np.asarray(inp["cache_na_v"][c, 0]).reshape(256, 1536))
        m["cgk"] = f(np.asarray(inp["cache_gqa_k"][c, 0]).reshape(256, 512))
        m["cgv"] = f(np.asarray(inp["cache_gqa_v"][c, 0]).reshape(256, 512))
        maps.append(m)
    return maps


_NC_CACHE = {}


def kernel(**inputs):
    if "nc" not in _NC_CACHE:
        _NC_CACHE["nc"] = build()
    nc = _NC_CACHE["nc"]
    maps = make_in_maps(inputs)
    res = run_bass_kernel_spmd(nc, maps, core_ids=list(range(NCORES)))
    r = res.results
    y_prompt = np.concatenate([r[c]["yp"].reshape(4, 256, D) for c in range(NCORES)], 0).astype(np.float32)
    y_sample = np.stack([r[c]["ys"] for c in range(NCORES)], 0).astype(np.float32)
    nak = np.concatenate([r[c]["nak"].reshape(4, 1, 256, 12, 128) for c in range(NCORES)], 0).astype(np.float32)
    nav = np.concatenate([r[c]["nav"].reshape(4, 1, 256, 12, 128) for c in range(NCORES)], 0).astype(np.float32)
    gqk = np.concatenate([r[c]["gqk"].reshape(4, 1, 256, 4, 128) for c in range(NCORES)], 0).astype(np.float32)
    gqv = np.concatenate([r[c]["gqv"].reshape(4, 1, 256, 4, 128) for c in range(NCORES)], 0).astype(np.float32)
    return (y_prompt, y_sample, nak, nav, gqk, gqv)
```

```python
import math
from contextlib import ExitStack

import numpy as np
import concourse.bass as bass
import concourse.mybir as mybir
from concourse.bass_utils import run_bass_kernel_spmd

F32 = mybir.dt.float32
BF16 = mybir.dt.bfloat16
AF = mybir.ActivationFunctionType
ALU = mybir.AluOpType
AX = mybir.AxisListType

T = 3072
TS = 2048
TP = 1024
D = 2048
KC = 16
DFF = 5632
EPS = 1e-6
NCORES = 8
SQ128 = math.sqrt(128.0)
ISQ128 = 1.0 / SQ128


class Buf:
    def __init__(self, name, is_sbuf=False, persistent=False, loose=False):
        self.name = name
        self.is_sbuf = is_sbuf
        self.persistent = persistent
        self.loose = loose
        self.w = {}
        self.r = {}
        self.wsem = {}
        self.rsem = {}


class SemC:
    def __init__(self, sem):
        self.sem = sem
        self.count = 0


class Eng:
    def __init__(self, name):
        self.name = name
        self.is_pe = name == 'pe'
        self.semc = None
        self.ops = []
        self.seen = {}
        self.pending_unsig = False


class Sched:
    def __init__(self, nc, stack):
        self.nc = nc
        self.stack = stack
        self.engs = {n: Eng(n) for n in ('pe', 'act', 'dve', 'pool', 'sp')}
        self.sems = []
        self.free_semcs = {}
        self.phase_semcs = []
        self.nsem = 0
        for n, e in self.engs.items():
            e.semc = self.newsem('prog_' + n, True)

    def newsem(self, name, persistent=False, cls='eng'):
        fl = self.free_semcs.setdefault(cls, [])
        if not persistent and fl:
            s = fl.pop()
        else:
            self.nsem += 1
            s = SemC(self.stack.enter_context(self.nc.semaphore('%s_%d' % (name, self.nsem))))
            s.cls = cls
            self.sems.append(s)
        if not persistent:
            self.phase_semcs.append(s)
        return s

    def recycle(self):
        for sc in self.phase_semcs:
            self.free_semcs.setdefault(sc.cls, []).append(sc)
        self.phase_semcs = []

    def _collect(self, e, reads, writes):
        need = {}
        for b in reads:
            for k, v in b.w.items():
                if need.get(k, 0) < v:
                    need[k] = v
        for b in writes:
            for d in (b.w, b.r):
                for k, v in d.items():
                    if need.get(k, 0) < v:
                        need[k] = v
        waits = []
        for k, v in need.items():
            if k is e.semc and e.is_pe:
                continue
            if e.seen.get(k, 0) >= v:
                continue
            e.seen[k] = v
            waits.append((k.sem, v))
        return waits

    def op(self, eng, fn, reads=(), writes=(), sig=True):
        e = self.engs[eng]
        waits = self._collect(e, reads, writes)
        if sig:
            e.semc.count += 1
            val = e.semc.count
            inc = (e.semc.sem, 1)
            e.pending_unsig = False
        else:
            val = e.semc.count + 1
            inc = None
            e.pending_unsig = True
        e.ops.append((waits, fn, inc))
        for b in reads:
            if b.r.get(e.semc, 0) < val:
                b.r[e.semc] = val
        for b in writes:
            b.w = {e.semc: val}
            b.r = {}

    def dma(self, queue, out_ap, in_ap, src, dst, **kw):
        e = self.engs[queue]
        waits = self._collect(e, [] if src.loose else [src], [] if dst.loose else [dst])
        if dst.is_sbuf:
            if queue not in dst.wsem:
                dst.wsem[queue] = self.newsem('w%s_%s' % (queue, dst.name), dst.persistent, queue)
            sc = dst.wsem[queue]
        else:
            if queue not in src.rsem:
                src.rsem[queue] = self.newsem('r%s_%s' % (queue, src.name), src.persistent, queue)
            sc = src.rsem[queue]
        sc.count += 16
        val = sc.count
        e.ops.append((waits, (lambda h, o=out_ap, i=in_ap, kw=kw: h.dma_start(out=o, in_=i, **kw)), (sc.sem, 16)))
        if not src.loose and src.r.get(sc, 0) < val:
            src.r[sc] = val
        if dst.is_sbuf:
            dst.w = {sc: val}
            dst.r = {}
        elif not dst.loose:
            dst.w = {**dst.w, sc: val}

    def barrier(self):
        for n, e in self.engs.items():
            if e.pending_unsig:
                raise RuntimeError('unsignalled op pending on ' + n)
        for n, e in self.engs.items():
            waits = []
            for sc in self.sems:
                v = sc.count
                if v == 0 or e.seen.get(sc, 0) >= v:
                    continue
                if sc is e.semc and e.is_pe:
                    continue
                e.seen[sc] = v
                waits.append((sc.sem, v))
            if waits:
                e.ops.append((waits, None, None))

    def finish(self):
        self.barrier()

    def emit(self):
        nc = self.nc
        with nc.Block() as block:
            def run(h, e):
                for waits, fn, inc in e.ops:
                    for s, v in waits:
                        h.wait_ge(s, v)
                    if fn is None:
                        continue
                    ins = fn(h)
                    if inc is not None:
                        ins.then_inc(inc[0], inc[1])

            @block.tensor
            def _(h):
                run(h, self.engs['pe'])

            @block.scalar
            def _(h):
                run(h, self.engs['act'])

            @block.vector
            def _(h):
                run(h, self.engs['dve'])

            @block.gpsimd
            def _(h):
                run(h, self.engs['pool'])

            @block.sync
            def _(h):
                run(h, self.engs['sp'])


class Arena:
    def __init__(self, nc, base, limit):
        self.nc = nc
        self.base = base
        self.off = base
        self.limit = limit
        self.n = 0

    def reset(self, off=None):
        self.off = self.base if off is None else off

    def alloc(self, name, shape, dtype):
        self.n += 1
        esz = 2 if dtype == BF16 else 4
        size = esz
        for s in shape[1:]:
            size *= s
        off = (max(self.off, 0) + 31) // 32 * 32
        assert off + size <= self.limit, (name, off, size, self.limit)
        t = self.nc.alloc_sbuf_tensor_at('%s_%d' % (name, self.n), list(shape), dtype, offset=off)
        self.off = off + size
        return t


def build(debug=False, stop_after=None):
    nc = bass.Bass("TRN2", target_bir_lowering=False)
    st = ExitStack()
    S = Sched(nc, st)

    def din(name, shape, dt=F32):
        return nc.dram_tensor(name, list(shape), dt, kind="ExternalInput").ap()

    def dout(name, shape, dt=F32):
        return nc.dram_tensor(name, list(shape), dt, kind="ExternalOutput").ap()

    def dscr(name, shape, dt=F32):
        if dt == BF16:
            shp = list(shape[:-1]) + [shape[-1] // 2]
            return nc.dram_tensor(name, shp, F32, kind="ExternalOutput").ap().bitcast(BF16)
        return nc.dram_tensor(name, list(shape), F32, kind="ExternalOutput").ap()

    xs_d = din("xs", [TS, D])
    xp_d = din("xp", [TP, D])
    condT_d = din("condT", [D, 2])
    cnk_d = din("cnk", [256, 1536])
    cnv_d = din("cnv", [256, 1536])
    cgk_d = din("cgk", [256, 512])
    cgv_d = din("cgv", [256, 512])
    ada_w_d = din("ada_w", [2, D, 6 * D])
    ada_bT_d = din("ada_bT", [2, 128, 96])
    ngT_d = din("ngT", [128, 5, 16])
    ev_w_in_d = din("ev_w_in", [D, 5120])
    ev_w_out_d = din("ev_w_out", [D, D])
    od_w_in_d = din("od_w_in", [D, 3072])
    od_w_out_d = din("od_w_out", [D, D])
    w_up_d = din("w_up", [2, D, 2 * DFF])
    w_down_d = din("w_down", [2, DFF, D])
    cwT_d = din("cwT", [128, 2, 88, 3])
    cbT_d = din("cbT", [128, 2, 88])
    ident_d = din("ident", [128, 128])
    fin_g_d = din("fin_g", [D])
    bt_d = din("bt", [12, 21, 128, 128])
    cn2k_d = din("cn2k", [2048, 2048])
    sn2k_d = din("sn2k", [2048, 2048])
    cn256_d = din("cn256", [256, 256])
    sn256_d = din("sn256", [256, 256])
    cc_d = din("cc", [128, 256])
    fw_d = din("fw", [4, 128, 128])
    pw_d = din("pw", [4, 128, 128])
    pscT_d = din("pscT", [128, 4])
    gq4_d = din("gq4", [512])
    gk4_d = din("gk4", [512])
    ropeC_d = din("ropeC", [TS, 128])
    ropeS_d = din("ropeS", [TS, 128])
    invc_s_d = din("invc_s", [4, TS])
    invc_p_d = din("invc_p", [4, 256])
    yp_d = dout("yp", [TP, D])
    ys_d = dout("ys", [TS, D])
    nak_d = dout("nak", [TP, 1536])
    nav_d = dout("nav", [TP, 1536])
    gqk_d = dout("gqk", [TP, 512])
    gqv_d = dout("gqv", [TP, 512])
    XT_d = dscr("XT", [16, 128, T])
    QT_d = dscr("QT", [12, 128, T], BF16)
    KT_d = dscr("KT", [12, 128, T], BF16)
    V_d = dscr("Vs", [T, 1536], BF16)
    XB_d = dscr("XB", [4, 128, T], F32)
    XT_b = Buf("XT", loose=True)
    QT_b = Buf("QT", loose=True)
    KT_b = Buf("KT", loose=True)
    V_b = Buf("V", loose=True)
    XB_b = Buf("XB", loose=True)
    IN_b = Buf("inputs", loose=True)
    OUT_b = Buf("outputs", loose=True)

    PS = nc.alloc_psum_tensor("psall", [128, 4096], F32)
    PSB = [Buf("psb%d" % i) for i in range(8)]

    def bank(i):
        return PS[:, i * 512:(i + 1) * 512]

    BASE = 16512
    LIMIT = BASE + 212800
    pers = Arena(nc, BASE, BASE + 12288)
    ident_f = pers.alloc("ident_f", [128, 128], F32)
    ident_b = pers.alloc("ident_b", [128, 128], BF16)
    ones_b = pers.alloc("ones_b", [128, 128], BF16)
    modT = pers.alloc("modT", [128, 2, 96, 2], F32)
    gs = pers.alloc("gs", [128, 2, 2, 16, 2], F32)
    ngT = pers.alloc("ngT", [128, 5, 16], F32)
    cw = pers.alloc("cw", [128, 2, 88, 3], F32)
    cb = pers.alloc("cb", [128, 2, 88], F32)
    scT = pers.alloc("scT", [128, 16, 2], BF16)
    condT = pers.alloc("condT", [128, 16, 2], F32)
    adab = pers.alloc("adab", [128, 2, 96], F32)
    hsave = pers.alloc("hsave", [128, 16, 2], BF16)
    HS_b = Buf("hsave", True, True)
    PERS_b = Buf("pers", True, True)
    MOD_b = Buf("mod", True, True)

    AT = nc.alloc_sbuf_tensor_at("AT", [128, 16, T], BF16, offset=BASE + 12288)
    AT_b = [Buf("AT%d" % i, True) for i in range(6)]
    WB_OFF = BASE + 12288 + 98304
    WB = [nc.alloc_sbuf_tensor_at("WB%d" % i, [128, 16, 512], BF16, offset=WB_OFF + i * 16384) for i in range(2)]
    WB_b = [Buf("WB%d" % i, True, True) for i in range(2)]
    PH_OFF = WB_OFF + 32768
    ph = Arena(nc, PH_OFF, LIMIT - 7 * 8192)

    wcount = [0]

    def load_w(src_ap_pcn):
        i = wcount[0] % 2
        wcount[0] += 1
        kc, n = src_ap_pcn.shape[1], src_ap_pcn.shape[2]
        S.dma('pool', WB[i][:, 0:kc, 0:n], src_ap_pcn, IN_b, WB_b[i])
        return WB[i], WB_b[i]

    def wview(W2d, n0, n):
        return W2d[:, n0:n0 + n].rearrange("(c p) n -> p c n", p=128)

    pcount = [0]

    def next_bank(nb=8, base=0):
        i = base + pcount[0] % nb
        pcount[0] += 1
        return i

    S.dma('sp', ident_f[:], ident_d, IN_b, PERS_b)
    S.dma('pool', ident_b[:], ident_d, IN_b, PERS_b)
    S.dma('sp', ngT[:], ngT_d, IN_b, PERS_b)
    S.dma('sp', cw[:], cwT_d, IN_b, PERS_b)
    S.dma('sp', cb[:], cbT_d, IN_b, PERS_b)
    S.dma('sp', condT[:], condT_d.rearrange("(c p) k -> p c k", p=128), IN_b, PERS_b)
    S.dma('sp', adab[:], ada_bT_d.rearrange("l p c -> p l c"), IN_b, PERS_b)
    S.op('dve', lambda h: h.memset(ones_b[:], 1.0), writes=[PERS_b])
    SC_b = Buf("scT", True, True)
    S.op('act', lambda h: h.activation(out=scT[:], in_=condT[:], func=AF.Silu), reads=[PERS_b], writes=[SC_b])

    ADA_PS = 7
    NAS = 7
    ADA_TOP = LIMIT - NAS * 8192
    ADAS = [nc.alloc_sbuf_tensor_at("adas%d" % i, [128, 16, 256], BF16, offset=ADA_TOP + i * 8192) for i in range(NAS)]
    ADAS_b = [Buf("adas%d" % i, True, True) for i in range(NAS)]
    ada_n = [0]

    def ada_block(l, blk):
        i = ada_n[0] % NAS
        ada_n[0] += 1
        S.dma('pool', ADAS[i][:], ada_w_d[l][:, blk * 256:(blk + 1) * 256].rearrange("(c p) n -> p c n", p=128), IN_b, ADAS_b[i])
        for j in range(2):
            ch = blk * 2 + j
            for kc in range(KC):
                S.op('pe', (lambda h, i=i, j=j, kc=kc, ch=ch, l=l: h.matmul(
                    PS[:, ADA_PS * 512 + (l * 96 + ch) * 2: ADA_PS * 512 + (l * 96 + ch) * 2 + 2],
                    ADAS[i][:, kc, j * 128:(j + 1) * 128], scT[:, kc, :], start=(kc == 0), stop=(kc == KC - 1))),
                    reads=[ADAS_b[i], SC_b], writes=[PSB[ADA_PS]], sig=(kc == KC - 1))
        if blk % 8 == 7:
            v = blk // 8
            S.op('dve', (lambda h, l=l, v=v: h.tensor_tensor(
                out=modT[:, l, v * 16:(v + 1) * 16, :],
                in0=PS[:, ADA_PS * 512 + (l * 96 + v * 16) * 2: ADA_PS * 512 + (l * 96 + v * 16 + 16) * 2].rearrange("p (c k) -> p c k", k=2),
                in1=adab[:, l, v * 16:(v + 1) * 16].unsqueeze(2).to_broadcast([128, 16, 2]), op=ALU.add)),
                reads=[PSB[ADA_PS], PERS_b], writes=[MOD_b])
            if v in (1, 4):
                w_ = 0 if v == 1 else 1
                S.op('dve', (lambda h, l=l, w_=w_, v=v: h.scalar_tensor_tensor(
                    out=gs[:, l, w_], in0=modT[:, l, v * 16:(v + 1) * 16, :], scalar=1.0,
                    in1=ngT[:, w_ * 2 + l].unsqueeze(2).to_broadcast([128, 16, 2]), op0=ALU.add, op1=ALU.mult)),
                    reads=[MOD_b, PERS_b], writes=[MOD_b])

    ada_todo = [(0, blk) for blk in range(16, 48)] + [(1, blk) for blk in range(48)]

    def ada_pull(n):
        for _ in range(n):
            if ada_todo:
                l_, blk_ = ada_todo.pop(0)
                ada_block(l_, blk_)

    for blk in range(16):
        ada_block(0, blk)

    def mod_ap(l, v, dc, cond):
        return modT[:, l, v * 16 + dc, cond:cond + 1]

    def cond_of_tile(t512):
        return 0 if t512 < 4 else 1

    if stop_after == 'in':
        S.finish(); S.emit(); st.close(); return nc
    if stop_after == 'ada':
        pass

    def norm_stats(cols, ar, pre=None):
        nr = len(ar['rstd'])
        rstd = ar['rstd'][ar['k'] % nr]
        rstd_b = ar['rstd_b'][ar['k'] % nr]
        if pre is not None:
            xi, xi_b, ntot = pre
            cols = []
        else:
            ntot = sum(n for _, n in cols)
            xi = ar['xi'][ar['k'] % ar['nxi']]
            xi_b = ar['xi_b'][ar['k'] % ar['nxi']]
        ar['k'] += 1
        o = 0
        for c0, n in cols:
            kw = {'allow_slow_non_contiguous': True} if n == 1 else {}
            S.dma('sp', xi[:, :, o:o + n], XT_d[:, :, c0:c0 + n].rearrange("c p t -> p c t"), XT_b, xi_b, **kw)
            o += n
        b = next_bank(4)
        for dc in range(KC):
            sq = ar['sq'][dc % ar['nsq']]
            sq_b = ar['sq_b'][dc % ar['nsq']]
            S.op('act', (lambda h, sq=sq, dc=dc, xi=xi: h.activation(out=sq[:, 0:ntot], in_=xi[:, dc, 0:ntot], func=AF.Square)),
                 reads=[xi_b], writes=[sq_b])
            S.op('pe', (lambda h, b=b, sq=sq, dc=dc: h.matmul(PS[:, b * 512:b * 512 + ntot], ones_b[:], sq[:, 0:ntot],
                                                             start=(dc == 0), stop=(dc == KC - 1))),
                 reads=[sq_b, PERS_b], writes=[PSB[b]], sig=True)
        S.op('dve', (lambda h, b=b: h.tensor_scalar(out=rstd[:, 0:ntot], in0=PS[:, b * 512:b * 512 + ntot], scalar1=1.0 / D, scalar2=EPS,
                                                    op0=ALU.mult, op1=ALU.add)), reads=[PSB[b]], writes=[rstd_b])
        S.op('act', (lambda h: h.activation(out=rstd[:, 0:ntot], in_=rstd[:, 0:ntot], func=AF.Sqrt)),
             reads=[rstd_b], writes=[rstd_b])
        S.op('dve', (lambda h: h.reciprocal(out=rstd[:, 0:ntot], in_=rstd[:, 0:ntot])),
             reads=[rstd_b], writes=[rstd_b])
        return xi, xi_b, rstd, rstd_b

    def norm_mod(l, which, cols, dst_fn, dst_bufs, cond, ar, pre=None):
        ntot = pre[2] if pre is not None else sum(n for _, n in cols)
        xi, xi_b, rstd, rstd_b = norm_stats(cols, ar, pre)
        for dc in range(KC):
            tmp = ar['tmp'][dc % ar['ntmp']]
            tmp_b = ar['tmp_b'][dc % ar['ntmp']]
            S.op('dve', (lambda h, tmp=tmp, dc=dc, xi=xi: h.scalar_tensor_tensor(
                out=tmp[:, 0:ntot], in0=xi[:, dc, 0:ntot], scalar=gs[:, l, which, dc, cond:cond + 1], in1=rstd[:, 0:ntot],
                op0=ALU.mult, op1=ALU.mult)), reads=[xi_b, rstd_b, MOD_b], writes=[tmp_b])
            S.op('act', (lambda h, tmp=tmp, dc=dc: h.activation(
                out=dst_fn(dc), in_=tmp[:, 0:ntot], func=AF.Identity, bias=mod_ap(l, which * 3, dc, cond), scale=1.0)),
                reads=[tmp_b, MOD_b], writes=dst_bufs)

    def norm_arena(ncol, nxi=2, nsq=6, ntmp=4):
        ar = {'k': 0, 'nxi': nxi, 'nsq': nsq, 'ntmp': ntmp}
        ar['xi'] = [ph.alloc("nxi", [128, 16, ncol], F32) for _ in range(nxi)]
        ar['xi_b'] = [Buf("nxi%d" % i, True) for i in range(nxi)]
        ar['sq'] = [ph.alloc("nsq", [128, ncol], BF16) for _ in range(nsq)]
        ar['sq_b'] = [Buf("nsq%d" % i, True) for i in range(nsq)]
        ar['rstd'] = [ph.alloc("nrstd", [128, ncol], F32) for _ in range(max(nxi, 1))]
        ar['rstd_b'] = [Buf("nrstd%d" % i, True) for i in range(max(nxi, 1))]
        ar['tmp'] = [ph.alloc("ntmp", [128, ncol], F32) for _ in range(ntmp)]
        ar['tmp_b'] = [Buf("ntmp%d" % i, True) for i in range(ntmp)]
        return ar

    def in_phase():
        ph.reset(WB_OFF)
        xin = [ph.alloc("xin", [128, D], F32) for _ in range(2)]
        xin_b = [Buf("xin%d" % i, True) for i in range(2)]
        xts = [ph.alloc("xts", [128, 16, 128], F32) for _ in range(3)]
        xts_b = [Buf("xts%d" % i, True) for i in range(3)]
        ar = norm_arena(128, nxi=0, nsq=6, ntmp=3)

        def tpart(tt):
            i = tt % 2
            x3 = tt % 3
            src = xs_d[tt * 128:(tt + 1) * 128, :] if tt < 16 else xp_d[(tt - 16) * 128:(tt - 15) * 128, :]
            S.dma('sp', xin[i][:], src, IN_b, xin_b[i])
            for q4 in range(4):
                b = next_bank(4)
                for j in range(4):
                    dc = q4 * 4 + j
                    S.op('pe', (lambda h, b=b, j=j, dc=dc, i=i: h.transpose(
                        PS[:, b * 512 + j * 128: b * 512 + (j + 1) * 128], xin[i][:, dc * 128:(dc + 1) * 128], ident_f[:])),
                        reads=[xin_b[i], PERS_b], writes=[PSB[b]], sig=(j == 3))
                if q4 % 2 == 0:
                    S.op('act', (lambda h, b=b, q4=q4, x3=x3: h.activation(
                        out=xts[x3][:, q4 * 4:(q4 + 1) * 4, :], in_=bank(b).rearrange("p (c t) -> p c t", t=128), func=AF.Copy)),
                        reads=[PSB[b]], writes=[xts_b[x3]])
                else:
                    S.op('dve', (lambda h, b=b, q4=q4, x3=x3: h.tensor_copy(
                        out=xts[x3][:, q4 * 4:(q4 + 1) * 4, :], in_=bank(b).rearrange("p (c t) -> p c t", t=128))),
                        reads=[PSB[b]], writes=[xts_b[x3]])
            S.dma('sp', XT_d[:, :, tt * 128:(tt + 1) * 128].rearrange("c p t -> p c t"), xts[x3][:], xts_b[x3], XT_b)

        def npart(tt):
            x3 = tt % 3
            norm_mod(0, 0, None, (lambda dc, tt=tt: AT[:, dc, tt * 128:(tt + 1) * 128]), [AT_b[tt // 4]], (0 if tt < 16 else 1), ar,
                     pre=(xts[x3], xts_b[x3], 128))

        for tt in range(25):
            if tt < 24:
                tpart(tt)
            if tt >= 1:
                npart(tt - 1)
        S.barrier()
        S.recycle()

    def norm1_phase(l):
        ph.reset(WB_OFF)
        lim = ph.limit
        ph.limit = LIMIT
        ar = norm_arena(512, nxi=2, nsq=6, ntmp=4)
        ph.limit = lim
        for t5 in range(6):
            norm_mod(l, 0, [(t5 * 512, 512)], (lambda dc, t5=t5: AT[:, dc, t5 * 512:(t5 + 1) * 512]), [AT_b[t5]], cond_of_tile(t5), ar)
        S.barrier()
        S.recycle()

    def mm_fm(wt, wb, j, t5, b, kcn=KC, src=None, src_b=None):
        for kc in range(kcn):
            S.op('pe', (lambda h, wt=wt, j=j, kc=kc, t5=t5, b=b: h.matmul(
                bank(b), wt[:, kc, j * 128:(j + 1) * 128], AT[:, kc, t5 * 512:(t5 + 1) * 512],
                start=(kc == 0), stop=(kc == kcn - 1))), reads=[wb, AT_b[t5]], writes=[PSB[b]], sig=(kc == kcn - 1))

    def mm_tm(wt, wb, tt, b, ncols=512):
        for kc in range(KC):
            S.op('pe', (lambda h, wt=wt, kc=kc, tt=tt, b=b: h.matmul(
                PS[:, b * 512:b * 512 + ncols], AT[:, kc, tt * 128:(tt + 1) * 128], wt[:, kc, 0:ncols],
                start=(kc == 0), stop=(kc == KC - 1))), reads=[wb, AT_b[tt // 4]], writes=[PSB[b]], sig=(kc == KC - 1))

    def g1_even():
        ph.reset()
        sgb = [ph.alloc("sgb", [128, 512], BF16) for _ in range(3)]
        sgb_b = [Buf("sgb%d" % i, True) for i in range(3)]
        sgf = [ph.alloc("sgf", [128, 512], F32) for _ in range(3)]
        sgf_b = [Buf("sgf%d" % i, True) for i in range(3)]
        cb_ = [0]
        cf_ = [0]

        def nb():
            i = cb_[0] % 3
            cb_[0] += 1
            return sgb[i], sgb_b[i]

        def nf():
            i = cf_[0] % 3
            cf_[0] += 1
            return sgf[i], sgf_b[i]

        wq = {0: load_w(wview(ev_w_in_d, 0, 512))}
        for blk in range(10):
            wt, wb = wq[blk]
            if blk + 1 < 10:
                wq[blk + 1] = load_w(wview(ev_w_in_d, (blk + 1) * 512, 512))
            kind = blk // 3
            if kind in (0, 1, 3):
                for j in range(4):
                    ch = (blk % 3) * 4 + j if kind < 3 else j
                    for t5 in range(6):
                        b = next_bank(6)
                        mm_fm(wt, wb, j, t5, b)
                        if kind == 3:
                            sg, sg_b = nf()
                            S.op('act', (lambda h, sg=sg, b=b: h.activation(out=sg[:], in_=bank(b), func=AF.Copy)),
                                 reads=[PSB[b]], writes=[sg_b])
                            S.dma('sp', XB_d[ch, :, t5 * 512:(t5 + 1) * 512], sg[:], sg_b, XB_b)
                        else:
                            sg, sg_b = nb()
                            eng = 'act' if (t5 % 2 == 0) else 'dve'
                            if eng == 'act':
                                S.op('act', (lambda h, sg=sg, b=b: h.activation(out=sg[:], in_=bank(b), func=AF.Copy)),
                                     reads=[PSB[b]], writes=[sg_b])
                            else:
                                S.op('dve', (lambda h, sg=sg, b=b: h.tensor_copy(out=sg[:], in_=bank(b))),
                                     reads=[PSB[b]], writes=[sg_b])
                            dst, dst_b = (QT_d, QT_b) if kind == 0 else (KT_d, KT_b)
                            S.dma('sp', dst[ch, :, t5 * 512:(t5 + 1) * 512], sg[:], sg_b, dst_b)
                if kind == 1:
                    for tt in range(16, 24):
                        b = next_bank(6)
                        mm_tm(wt, wb, tt, b)
                        sg, sg_b = nf()
                        S.op('act', (lambda h, sg=sg, b=b: h.activation(out=sg[:], in_=bank(b), func=AF.Copy)),
                             reads=[PSB[b]], writes=[sg_b])
                        S.dma('sp', nak_d[(tt - 16) * 128:(tt - 15) * 128, (blk - 3) * 512:(blk - 2) * 512], sg[:], sg_b, OUT_b)
            else:
                for tt in range(24):
                    b = next_bank(6)
                    mm_tm(wt, wb, tt, b)
                    sg, sg_b = nb()
                    if tt < 16:
                        S.op('dve', (lambda h, sg=sg, b=b: h.tensor_copy(out=sg[:], in_=bank(b))), reads=[PSB[b]], writes=[sg_b])
                    else:
                        sf, sf_b = nf()
                        S.op('act', (lambda h, sf=sf, b=b: h.activation(out=sf[:], in_=bank(b), func=AF.Copy)),
                             reads=[PSB[b]], writes=[sf_b])
                        S.op('dve', (lambda h, sg=sg, sf=sf: h.tensor_copy(out=sg[:], in_=sf[:])), reads=[sf_b], writes=[sg_b])
                        S.dma('sp', nav_d[(tt - 16) * 128:(tt - 15) * 128, (blk - 6) * 512:(blk - 5) * 512], sf[:], sf_b, OUT_b)
                    S.dma('sp', V_d[tt * 128:(tt + 1) * 128, (blk - 6) * 512:(blk - 5) * 512], sg[:], sg_b, V_b)
            ada_pull(8)
        ada_pull(1000)
        ph.limit = LIMIT
        S.barrier()
        S.recycle()

    PSbf = PS[:, 6 * 512:7 * 512].bitcast(BF16)
    acnt = [0]

    def pipeline2(stages):
        prev = None
        for s1, s2 in stages:
            s1()
            if prev is not None:
                prev()
            prev = s2
        if prev is not None:
            prev()

    def block_attn_prompt(hq, kch, vcol0, out_chunk, ar, stages):
        i = ar['k'] % 2
        ar['k'] += 1
        qp, kp, vp = ar['qp'][i], ar['kp'][i], ar['vp'][i]
        qp_b, kp_b, vp_b = ar['qp_b'][i], ar['kp_b'][i], ar['vp_b'][i]

        def loads():
            S.dma('sp', qp[:], QT_d[hq, :, TS:T], QT_b, qp_b)
            S.dma('sp', kp[:], KT_d[kch, :, TS:T], KT_b, kp_b)
            S.dma('sp', vp[:], V_d[TS:T, vcol0:vcol0 + 128].rearrange("(c p) d -> p c d", p=128), V_b, vp_b)

        for s in range(4):
            c = acnt[0]
            acnt[0] += 1
            sb = c % 4
            ob = 4 + c % 2
            pt = ar['pt'][c % 2]
            pt_b = ar['pt_b'][c % 2]
            rec = ar['rec'][c % 2]
            rec_b = ar['rec_b'][c % 2]

            def st1(s=s, sb=sb, pt=pt, pt_b=pt_b):
                if s == 0:
                    loads()
                for kc in range(2):
                    S.op('pe', (lambda h, kc=kc: h.matmul(
                        PS[:, sb * 512 + kc * 256: sb * 512 + (kc + 1) * 256], kp[:, s * 256 + kc * 128: s * 256 + (kc + 1) * 128],
                        qp[:, s * 256:(s + 1) * 256], start=True, stop=True)), reads=[kp_b, qp_b], writes=[PSB[sb]], sig=(kc == 1))
                S.op('act', (lambda h: h.activation(out=pt[:, 0:512], in_=bank(sb), func=AF.Exp, scale=ISQ128)),
                     reads=[PSB[sb]], writes=[pt_b])

            def st2(s=s, ob=ob, pt=pt, pt_b=pt_b, rec=rec, rec_b=rec_b):
                for kc in range(2):
                    S.op('pe', (lambda h, kc=kc: h.matmul(
                        PS[:, ob * 512: ob * 512 + 256], vp[:, s * 2 + kc, :], pt[:, kc * 256:(kc + 1) * 256],
                        start=(kc == 0), stop=(kc == 1))), reads=[vp_b, pt_b], writes=[PSB[ob]], sig=False)
                for kc in range(2):
                    S.op('pe', (lambda h, kc=kc: h.matmul(
                        PS[:, ob * 512 + 256: ob * 512 + 512], ones_b[:], pt[:, kc * 256:(kc + 1) * 256],
                        start=(kc == 0), stop=(kc == 1))), reads=[pt_b, PERS_b], writes=[PSB[ob]], sig=(kc == 1))
                S.op('dve', (lambda h: h.reciprocal(out=rec[:, 0:256], in_=PS[:, ob * 512 + 256: ob * 512 + 512])),
                     reads=[PSB[ob]], writes=[rec_b])
                S.op('dve', (lambda h: h.tensor_tensor(
                    out=AT[:, out_chunk, TS + s * 256: TS + (s + 1) * 256], in0=PS[:, ob * 512: ob * 512 + 256], in1=rec[:, 0:256], op=ALU.mult)),
                    reads=[PSB[ob], rec_b], writes=[AT_b[4 + s // 2]])

            stages.append((st1, st2))

    def attn_arena():
        ar = {'k': 0}
        ar['qp'] = [ph.alloc("qp", [128, TP], BF16) for _ in range(2)]
        ar['kp'] = [ph.alloc("kp", [128, TP], BF16) for _ in range(2)]
        ar['vp'] = [ph.alloc("vp", [128, 8, 128], BF16) for _ in range(2)]
        for n in ('qp', 'kp', 'vp'):
            ar[n + '_b'] = [Buf(n + "%d" % i, True) for i in range(2)]
        ar['pt'] = [ph.alloc("pt", [128, 1024], BF16) for _ in range(2)]
        ar['pt_b'] = [Buf("pt%d" % i, True) for i in range(2)]
        ar['rec'] = [ph.alloc("rec", [128, 512], F32) for _ in range(2)]
        ar['rec_b'] = [Buf("rec%d" % i, True) for i in range(2)]
        return ar

    def na_chunks(j):
        r0s = [min(max(r - 4, 0), 24) for r in (2 * j, 2 * j + 1)]
        return list(range(min(r0s) // 2, (max(r0s) + 7) // 2 + 1))

    def na_type(j, m):
        if 2 <= j <= 13:
            return (m - j) + 2
        base = {0: 5, 1: 9, 14: 13, 15: 17}[j]
        return base + (m - (0 if j < 2 else 12))

    def attn_even():
        ph.reset(WB_OFF)
        ar = attn_arena()
        identS = ph.alloc("identS", [128, 128], BF16)
        IDS_b = Buf("identS", True)
        S.op('act', (lambda h: h.activation(out=identS[:], in_=ident_f[:], func=AF.Copy, scale=SQ128)), reads=[PERS_b], writes=[IDS_b])
        kctok = ph.alloc("kctok", [128, 2, 1536], BF16)
        vc = ph.alloc("vc", [128, 2, 1536], BF16)
        kcT = ph.alloc("kcT", [128, 12, 256], BF16)
        KTOK_b, VC_b, KCT_b = Buf("kctok", True), Buf("vc", True), Buf("kcT", True)
        S.dma('pool', kctok[:], cnk_d.rearrange("(c p) d -> p c d", p=128), IN_b, KTOK_b)
        S.dma('pool', vc[:], cnv_d.rearrange("(c p) d -> p c d", p=128), IN_b, VC_b)
        for hh in range(12):
            for c in range(2):
                S.op('pe', (lambda h, hh=hh, c=c: h.transpose(PSbf[:, c * 128:(c + 1) * 128], kctok[:, c, hh * 128:(hh + 1) * 128], ident_b[:])),
                     reads=[KTOK_b, PERS_b], writes=[PSB[6]], sig=(c == 1))
            S.op('dve', (lambda h, hh=hh: h.tensor_copy(out=kcT[:, hh, :], in_=PSbf[:, 0:256])), reads=[PSB[6]], writes=[KCT_b])
        qth = [ph.alloc("qth", [128, TS], BF16) for _ in range(2)]
        kth = [ph.alloc("kth", [128, TS], BF16) for _ in range(2)]
        vh = [ph.alloc("vh", [128, 16, 128], BF16) for _ in range(2)]
        bth = [ph.alloc("bth", [128, 21, 128], BF16) for _ in range(2)]
        qth_b = [Buf("qth%d" % i, True) for i in range(2)]
        kth_b = [Buf("kth%d" % i, True) for i in range(2)]
        vh_b = [Buf("vh%d" % i, True) for i in range(2)]
        bth_b = [Buf("bth%d" % i, True) for i in range(2)]
        cnt = 0
        stages = []
        for hh in range(12):
            i = hh % 2

            def head_loads(hh=hh, i=i):
                S.dma('sp', qth[i][:], QT_d[hh, :, 0:TS], QT_b, qth_b[i])
                S.dma('sp', kth[i][:], KT_d[hh, :, 0:TS], KT_b, kth_b[i])
                S.dma('sp', vh[i][:], V_d[0:TS, hh * 128:(hh + 1) * 128].rearrange("(c p) d -> p c d", p=128), V_b, vh_b[i])
                S.dma('pool', bth[i][:], bt_d[hh].rearrange("t k q -> k t q"), IN_b, bth_b[i])

            for j in range(16):
                ms = na_chunks(j)
                nl = len(ms)
                ncol = (nl + 2) * 128
                sb = cnt % 2
                ob = 4 + cnt % 2
                pt = ar['pt'][cnt % 2]
                pt_b = ar['pt_b'][cnt % 2]
                rec = ar['rec'][cnt % 2]
                rec_b = ar['rec_b'][cnt % 2]
                cnt += 1
                sbufs = [PSB[2 * sb], PSB[2 * sb + 1]]
                base = sb * 1024

                def st1(hh=hh, i=i, j=j, ms=ms, nl=nl, ncol=ncol, pt=pt, pt_b=pt_b, sbufs=sbufs, base=base, head_loads=head_loads):
                    if j == 0:
                        head_loads()
                    for idx, m in enumerate(ms):
                        ty = na_type(j, m)
                        S.op('pe', (lambda h, idx=idx, m=m: h.matmul(
                            PS[:, base + idx * 128: base + (idx + 1) * 128], kth[i][:, m * 128:(m + 1) * 128], qth[i][:, j * 128:(j + 1) * 128],
                            start=True, stop=False)), reads=[kth_b[i], qth_b[i]], writes=sbufs, sig=False)
                        S.op('pe', (lambda h, idx=idx, ty=ty: h.matmul(
                            PS[:, base + idx * 128: base + (idx + 1) * 128], identS[:], bth[i][:, ty, :],
                            start=False, stop=True)), reads=[bth_b[i], IDS_b], writes=sbufs, sig=False)
                    for c in range(2):
                        idx = nl + c
                        S.op('pe', (lambda h, idx=idx, c=c: h.matmul(
                            PS[:, base + idx * 128: base + (idx + 1) * 128], kcT[:, hh, c * 128:(c + 1) * 128], qth[i][:, j * 128:(j + 1) * 128],
                            start=True, stop=True)), reads=[KCT_b, qth_b[i]], writes=sbufs, sig=(c == 1))
                    S.op('act', (lambda h: h.activation(out=pt[:, 0:ncol], in_=PS[:, base:base + ncol], func=AF.Exp, scale=ISQ128)),
                         reads=sbufs, writes=[pt_b])

                def st2(hh=hh, i=i, j=j, ms=ms, nl=nl, ob=ob, pt=pt, pt_b=pt_b, rec=rec, rec_b=rec_b):
                    for idx in range(nl + 2):
                        if idx < nl:
                            lhs = (lambda m=ms[idx]: vh[i][:, m, :])
                            rb = vh_b[i]
                        else:
                            lhs = (lambda c=idx - nl: vc[:, c, hh * 128:(hh + 1) * 128])
                            rb = VC_b
                        S.op('pe', (lambda h, lhs=lhs, idx=idx: h.matmul(
                            PS[:, ob * 512: ob * 512 + 128], lhs(), pt[:, idx * 128:(idx + 1) * 128], start=(idx == 0), stop=(idx == nl + 1))),
                            reads=[rb, pt_b], writes=[PSB[ob]], sig=False)
                    for idx in range(nl + 2):
                        S.op('pe', (lambda h, idx=idx: h.matmul(
                            PS[:, ob * 512 + 128: ob * 512 + 256], ones_b[:], pt[:, idx * 128:(idx + 1) * 128], start=(idx == 0), stop=(idx == nl + 1))),
                            reads=[pt_b, PERS_b], writes=[PSB[ob]], sig=(idx == nl + 1))
                    S.op('dve', (lambda h: h.reciprocal(out=rec[:, 0:128], in_=PS[:, ob * 512 + 128: ob * 512 + 256])),
                         reads=[PSB[ob]], writes=[rec_b])
                    S.op('dve', (lambda h: h.tensor_tensor(
                        out=AT[:, hh, j * 128:(j + 1) * 128], in0=PS[:, ob * 512: ob * 512 + 128], in1=rec[:, 0:128], op=ALU.mult)),
                        reads=[PSB[ob], rec_b], writes=[AT_b[j // 4]])

                stages.append((st1, st2))
        pipeline2(stages)
        ada_pull(1000)
        ph.limit = LIMIT
        stages = []
        for hh in range(12):
            block_attn_prompt(hh, hh, hh * 128, hh, ar, stages)
        pipeline2(stages)
        S.barrier()
        S.recycle()

    def fnet_phase():
        for gh in range(2):
            fnet_half(gh)

    def fnet_half(gh):
        ph.reset()
        xbt = ph.alloc("xbt", [128, 2, T], BF16)
        XBT_b = Buf("xbt", True)
        S.dma('pool', xbt[:], XB_d[gh * 2:gh * 2 + 2].rearrange("g p t -> p g t"), XB_b, XBT_b)
        cc = ph.alloc("cc", [128, 256], BF16)
        fw = ph.alloc("fw", [128, 4, 128], BF16)
        c256 = ph.alloc("c256", [128, 2, 256], BF16)
        s256 = ph.alloc("s256", [128, 2, 256], BF16)
        FC_b = Buf("fconst", True)
        S.dma('pool', cc[:], cc_d, IN_b, FC_b)
        S.dma('pool', fw[:], fw_d.rearrange("g c d -> c g d"), IN_b, FC_b)
        S.dma('pool', c256[:], cn256_d.rearrange("(c p) k -> p c k", p=128), IN_b, FC_b)
        S.dma('pool', s256[:], sn256_d.rearrange("(c p) k -> p c k", p=128), IN_b, FC_b)
        U = ph.alloc("U", [128, 24, 2, 256], BF16)
        U_b = Buf("U", True)
        yb = [ph.alloc("yb", [128, 512], BF16) for _ in range(2)]
        yb_b = [Buf("yb%d" % i, True) for i in range(2)]
        WQ = [nc.alloc_sbuf_tensor_at("wq%d_%d" % (gh, i), [128, 16, 256], BF16, offset=WB_OFF + i * 8192) for i in range(4)]
        WQ_b = [Buf("wq%d" % i, True) for i in range(4)]
        for tt in range(24):
            for g2 in range(1):
                b = next_bank(6)
                for gg in range(2):
                    g = gg
                    S.op('pe', (lambda h, b=b, gg=gg, g=g, tt=tt: h.matmul(
                        PS[:, b * 512 + gg * 256: b * 512 + (gg + 1) * 256], xbt[:, g, tt * 128:(tt + 1) * 128], cc[:],
                        start=True, stop=True)), reads=[XBT_b, FC_b], writes=[PSB[b]], sig=(gg == 1))
                if tt % 2 == 0:
                    S.op('act', (lambda h, b=b, tt=tt, g2=g2: h.activation(
                        out=U[:, tt, g2 * 2:(g2 + 1) * 2, :], in_=bank(b).rearrange("p (g k) -> p g k", k=256), func=AF.Copy)),
                        reads=[PSB[b]], writes=[U_b])
                else:
                    S.op('dve', (lambda h, b=b, tt=tt, g2=g2: h.tensor_copy(
                        out=U[:, tt, g2 * 2:(g2 + 1) * 2, :], in_=bank(b).rearrange("p (g k) -> p g k", k=256))),
                        reads=[PSB[b]], writes=[U_b])

        def step23(g, mm_list, ncols, dst_ap, dst_b):
            b = next_bank(6)
            n = len(mm_list)
            for q, (lhs, rhs, rb) in enumerate(mm_list):
                S.op('pe', (lambda h, b=b, lhs=lhs, rhs=rhs, q=q, n=n: h.matmul(
                    PS[:, b * 512: b * 512 + ncols], lhs(), rhs(), start=(q == 0), stop=(q == n - 1))),
                    reads=[U_b, rb], writes=[PSB[b]], sig=(q == n - 1))
            k = b % 2
            S.op('act', (lambda h, b=b, k=k: h.activation(out=yb[k][:, 0:ncols], in_=PS[:, b * 512: b * 512 + ncols], func=AF.Copy)),
                 reads=[PSB[b]], writes=[yb_b[k]])
            b2 = next_bank(6)
            S.op('pe', (lambda h, b2=b2, k=k, g=g: h.matmul(PS[:, b2 * 512: b2 * 512 + ncols], fw[:, gh * 2 + g, :], yb[k][:, 0:ncols], start=True, stop=True)),
                 reads=[yb_b[k], FC_b], writes=[PSB[b2]], sig=True)
            S.op('dve', (lambda h, b2=b2: h.tensor_copy(out=dst_ap, in_=PS[:, b2 * 512: b2 * 512 + ncols])), reads=[PSB[b2]], writes=[dst_b])

        for kt in range(8):
            q0 = (kt % 2) * 2
            cn, cn_b = WQ[q0], WQ_b[q0]
            sn, sn_b = WQ[q0 + 1], WQ_b[q0 + 1]
            S.dma('pool', cn[:], cn2k_d[:, kt * 256:(kt + 1) * 256].rearrange("(c p) n -> p c n", p=128), IN_b, cn_b)
            S.dma('pool', sn[:], sn2k_d[:, kt * 256:(kt + 1) * 256].rearrange("(c p) n -> p c n", p=128), IN_b, sn_b)
            for g in range(2):
                mm = []
                for n_ in range(16):
                    mm.append(((lambda n_=n_, g=g: U[:, n_, g, 0:128]), (lambda n_=n_, cn=cn: cn[:, n_, :]), cn_b))
                    mm.append(((lambda n_=n_, g=g: U[:, n_, g, 128:256]), (lambda n_=n_, sn=sn: sn[:, n_, :]), sn_b))
                step23(g, mm, 256, AT[:, 12 + gh * 2 + g, kt * 256:(kt + 1) * 256], AT_b[kt // 2])
        for s in range(4):
            for g in range(2):
                mm = []
                for n_ in range(2):
                    mm.append(((lambda n_=n_, g=g, s=s: U[:, 16 + s * 2 + n_, g, 0:128]), (lambda n_=n_: c256[:, n_, :]), FC_b))
                    mm.append(((lambda n_=n_, g=g, s=s: U[:, 16 + s * 2 + n_, g, 128:256]), (lambda n_=n_: s256[:, n_, :]), FC_b))
                step23(g, mm, 256, AT[:, 12 + gh * 2 + g, TS + s * 256: TS + (s + 1) * 256], AT_b[4 + s // 2])
        S.barrier()
        S.recycle()

    def g2_phase(l, w_out_d):
        ph.reset()
        NS = 4
        xi = [ph.alloc("g2xi", [128, 512], F32) for _ in range(NS)]
        xi_b = [Buf("g2xi%d" % i, True) for i in range(NS)]
        xo = [ph.alloc("g2xo", [128, 512], F32) for _ in range(NS)]
        xo_b = [Buf("g2xo%d" % i, True) for i in range(NS)]
        its = [(blk, j, t5) for blk in range(4) for j in range(4) for t5 in range(6)]

        def load(c):
            blk, j, t5 = its[c]
            S.dma('sp', xi[c % NS][:], XT_d[blk * 4 + j, :, t5 * 512:(t5 + 1) * 512], XT_b, xi_b[c % NS])

        load(0)
        load(1)
        wt = wb = None
        for c, (blk, j, t5) in enumerate(its):
            if j == 0 and t5 == 0:
                wt, wb = load_w(wview(w_out_d, blk * 512, 512))
            if c + 2 < len(its):
                load(c + 2)
            k = c % NS
            dc = blk * 4 + j
            b = next_bank(6)
            mm_fm(wt, wb, j, t5, b)
            cond = cond_of_tile(t5)
            S.op('dve', (lambda h, k=k, b=b, dc=dc, cond=cond: h.scalar_tensor_tensor(
                out=xo[k][:], in0=bank(b), scalar=mod_ap(l, 2, dc, cond), in1=xi[k][:], op0=ALU.mult, op1=ALU.add)),
                reads=[PSB[b], xi_b[k], MOD_b], writes=[xo_b[k]])
            S.dma('sp', XT_d[dc, :, t5 * 512:(t5 + 1) * 512], xo[k][:], xo_b[k], XT_b)
        S.barrier()
        S.recycle()

    FFN_TILES = [
        (0, 0, [(0, 1024)], False, True),
        (1024, 0, [(0, 1024)], True, False),
        (2048, 1, [(0, 256), (256, 512), (512, 768), (768, 1024)], False, False),
    ]

    FFN_ST = {}

    def ffn_static():
        if FFN_ST:
            return FFN_ST
        FB = WB_OFF
        st_ = FFN_ST
        st_['GT'] = nc.alloc_sbuf_tensor_at("GT", [128, 44, 1024], BF16, offset=BASE + 12288)
        st_['GT_b'] = [Buf("GT%d" % i, True) for i in range(2)]
        st_['H2T'] = nc.alloc_sbuf_tensor_at("H2T", [128, 16, 1026], BF16, offset=FB)
        st_['H2T_b'] = Buf("H2T", True)
        st_['wup'] = [nc.alloc_sbuf_tensor_at("wup%d" % k, [128, 16, 256], BF16, offset=FB + 32832 + k * 8192) for k in range(4)]
        st_['wup_b'] = [Buf("wup%d" % k, True, True) for k in range(4)]
        D0 = FB + 65600
        st_['acc'] = [nc.alloc_sbuf_tensor_at("acc%d" % k, [128, 1024], F32, offset=D0 + k * 4096) for k in range(4)]
        st_['acc_b'] = [Buf("acc%d" % k, True) for k in range(4)]
        st_['hal'] = [nc.alloc_sbuf_tensor_at("hal%d" % k, [128, 2], F32, offset=D0 + 16384 + k * 32) for k in range(2)]
        st_['hal_b'] = [Buf("hal%d" % k, True) for k in range(2)]
        st_['wdn'] = [nc.alloc_sbuf_tensor_at("wdn%d" % k, [128, 44, 256], BF16, offset=FB + 32832 + k * 22528) for k in range(2)]
        st_['wdn_b'] = [Buf("wdn%d" % k, True, True) for k in range(2)]
        N0 = FB + 77888
        st_['nxi'] = nc.alloc_sbuf_tensor_at("fnxi", [128, 16, 200], F32, offset=N0)
        st_['nxi_b'] = Buf("fnxi", True, True)
        st_['nsq'] = nc.alloc_sbuf_tensor_at("fnsq", [128, 16, 200], BF16, offset=N0 + 12800)
        st_['nsq_b'] = Buf("fnsq", True)
        st_['nrstd'] = nc.alloc_sbuf_tensor_at("fnrstd", [128, 200], F32, offset=N0 + 19200)
        st_['nrstd_b'] = Buf("fnrstd", True)
        st_['ntmp'] = [nc.alloc_sbuf_tensor_at("fntmp%d" % k, [128, 200], F32, offset=N0 + 20000 + k * 800) for k in range(3)]
        st_['ntmp_b'] = [Buf("fntmp%d" % k, True) for k in range(3)]
        assert N0 + 22400 <= LIMIT
        X0 = BASE + 12288 + 90112
        st_['xi'] = [nc.alloc_sbuf_tensor_at("fxi%d" % k, [128, 512], F32, offset=X0 + k * 2048) for k in range(2)]
        st_['xi_b'] = [Buf("fxi%d" % k, True, True) for k in range(2)]
        st_['xo'] = [nc.alloc_sbuf_tensor_at("fxo%d" % k, [128, 512], F32, offset=X0 + 4096 + k * 2048) for k in range(2)]
        st_['xo_b'] = [Buf("fxo%d" % k, True, True) for k in range(2)]
        return st_

    def ffn_phase(l):
        F = ffn_static()
        GT, GT_b, H2T, H2T_b = F['GT'], F['GT_b'], F['H2T'], F['H2T_b']
        wup, wup_b, acc, acc_b, hal, hal_b = F['wup'], F['wup_b'], F['acc'], F['acc_b'], F['hal'], F['hal_b']
        wdn, wdn_b, xi, xi_b, xo, xo_b = F['wdn'], F['wdn_b'], F['xi'], F['xi_b'], F['xo'], F['xo_b']
        nxi, nxi_b, nsq, nsq_b, nrstd, nrstd_b, ntmp, ntmp_b = (F[k] for k in ('nxi', 'nxi_b', 'nsq', 'nsq_b', 'nrstd', 'nrstd_b', 'ntmp', 'ntmp_b'))

        ph.reset(WB_OFF)
        ar0 = norm_arena(342, nxi=1)
        norm_mod(l, 1, [(1023, 2)], (lambda dc: hsave[:, dc, 0:2]), [HS_b], 0, ar0)
        S.barrier()
        S.recycle()

        def pieces(t0):
            out = []
            c0 = t0
            while c0 < t0 + 1024:
                n = min(200, t0 + 1024 - c0)
                out.append((c0, n, 2 + c0 - t0))
                c0 += n
            return out

        def norm_p1(pc):
            c0, n, hc = pc
            S.dma('sp', nxi[:, :, 0:n], XT_d[:, :, c0:c0 + n].rearrange("c p t -> p c t"), XT_b, nxi_b)
            for dc in range(KC):
                S.op('act', (lambda h, dc=dc, n=n: h.activation(out=nsq[:, dc, 0:n], in_=nxi[:, dc, 0:n], func=AF.Square)),
                     reads=[nxi_b], writes=[nsq_b])

        def norm_p2(pc, cond):
            c0, n, hc = pc
            b = next_bank(6)
            for dc in range(KC):
                S.op('pe', (lambda h, b=b, dc=dc, n=n: h.matmul(PS[:, b * 512:b * 512 + n], ones_b[:], nsq[:, dc, 0:n],
                                                                start=(dc == 0), stop=(dc == KC - 1))),
                     reads=[nsq_b, PERS_b], writes=[PSB[b]], sig=(dc == KC - 1))
            S.op('dve', (lambda h, b=b, n=n: h.tensor_scalar(out=nrstd[:, 0:n], in0=PS[:, b * 512:b * 512 + n], scalar1=1.0 / D, scalar2=EPS,
                                                             op0=ALU.mult, op1=ALU.add)), reads=[PSB[b]], writes=[nrstd_b])
            S.op('act', (lambda h, n=n: h.activation(out=nrstd[:, 0:n], in_=nrstd[:, 0:n], func=AF.Sqrt)), reads=[nrstd_b], writes=[nrstd_b])
            S.op('dve', (lambda h, n=n: h.reciprocal(out=nrstd[:, 0:n], in_=nrstd[:, 0:n])), reads=[nrstd_b], writes=[nrstd_b])
            for dc in range(KC):
                tmp, tmp_b = ntmp[dc % 3], ntmp_b[dc % 3]
                S.op('dve', (lambda h, tmp=tmp, dc=dc, n=n, cond=cond: h.scalar_tensor_tensor(
                    out=tmp[:, 0:n], in0=nxi[:, dc, 0:n], scalar=gs[:, l, 1, dc, cond:cond + 1], in1=nrstd[:, 0:n],
                    op0=ALU.mult, op1=ALU.mult)), reads=[nxi_b, nrstd_b, MOD_b], writes=[tmp_b])
                S.op('act', (lambda h, tmp=tmp, dc=dc, n=n, hc=hc, cond=cond: h.activation(
                    out=H2T[:, dc, hc:hc + n], in_=tmp[:, 0:n], func=AF.Identity, bias=mod_ap(l, 3, dc, cond), scale=1.0)),
                    reads=[tmp_b, MOD_b], writes=[H2T_b])

        def up_proj(t0, cond, segs, lh, rh):
            ucnt = 0
            wcnt = 0
            for pr in range(22):
                slots = []
                for half in range(2):
                    k = wcnt % 4
                    wcnt += 1
                    col0 = half * DFF + pr * 256
                    S.dma('pool', wup[k][:], w_up_d[l][:, col0:col0 + 256].rearrange("(c p) n -> p c n", p=128), IN_b, wup_b[k])
                    slots.append(k)
                for cc_ in range(2):
                    fc = pr * 2 + cc_
                    accs = []
                    for half in range(2):
                        k = slots[half]
                        u = ucnt % 3
                        hs = ucnt % 4
                        ucnt += 1
                        a = acc[(fc % 2) * 2 + half]
                        a_b = acc_b[(fc % 2) * 2 + half]
                        ubufs = [PSB[2 * u], PSB[2 * u + 1]]
                        fidx = half * 44 + fc
                        for hf in range(2):
                            for kc in range(KC):
                                S.op('pe', (lambda h, u=u, hf=hf, kc=kc, k=k, cc_=cc_: h.matmul(
                                    PS[:, u * 1024 + hf * 512: u * 1024 + (hf + 1) * 512], wup[k][:, kc, cc_ * 128:(cc_ + 1) * 128],
                                    H2T[:, kc, 2 + hf * 512: 2 + (hf + 1) * 512], start=(kc == 0), stop=(kc == KC - 1))),
                                    reads=[wup_b[k], H2T_b], writes=[ubufs[hf]], sig=(kc == KC - 1))
                        if lh or rh:
                            for kc in range(KC):
                                S.op('pe', (lambda h, hs=hs, kc=kc, k=k, cc_=cc_: h.matmul(
                                    PS[:, 6 * 512 + hs * 2: 6 * 512 + hs * 2 + 2], wup[k][:, kc, cc_ * 128:(cc_ + 1) * 128],
                                    hsave[:, kc, 0:2], start=(kc == 0), stop=(kc == KC - 1))),
                                    reads=[wup_b[k], HS_b], writes=[PSB[6]], sig=(kc == KC - 1))
                        S.op('act', (lambda h, a=a, u=u, fidx=fidx: h.activation(
                            out=a[:], in_=PS[:, u * 1024:(u + 1) * 1024], func=AF.Identity, scale=cw[:, l, fidx, 1:2], bias=cb[:, l, fidx:fidx + 1])),
                            reads=ubufs + [PERS_b], writes=[a_b])
                        for (sa, sbb) in segs:
                            S.op('dve', (lambda h, a=a, u=u, sa=sa, sbb=sbb, fidx=fidx: h.scalar_tensor_tensor(
                                out=a[:, sa + 1:sbb], in0=PS[:, u * 1024 + sa: u * 1024 + sbb - 1], scalar=cw[:, l, fidx, 0:1], in1=a[:, sa + 1:sbb],
                                op0=ALU.mult, op1=ALU.add)), reads=ubufs + [a_b, PERS_b], writes=[a_b])
                            S.op('dve', (lambda h, a=a, u=u, sa=sa, sbb=sbb, fidx=fidx: h.scalar_tensor_tensor(
                                out=a[:, sa:sbb - 1], in0=PS[:, u * 1024 + sa + 1: u * 1024 + sbb], scalar=cw[:, l, fidx, 2:3], in1=a[:, sa:sbb - 1],
                                op0=ALU.mult, op1=ALU.add)), reads=ubufs + [a_b, PERS_b], writes=[a_b])
                        if lh or rh:
                            hl = hal[hs % 2]
                            hl_b = hal_b[hs % 2]
                            S.op('act', (lambda h, hl=hl, hs=hs: h.activation(out=hl[:], in_=PS[:, 6 * 512 + hs * 2: 6 * 512 + hs * 2 + 2], func=AF.Copy)),
                                 reads=[PSB[6]], writes=[hl_b])
                        if lh:
                            S.op('dve', (lambda h, a=a, hl=hl, fidx=fidx: h.scalar_tensor_tensor(
                                out=a[:, 0:1], in0=hl[:, 0:1], scalar=cw[:, l, fidx, 0:1], in1=a[:, 0:1],
                                op0=ALU.mult, op1=ALU.add)), reads=[hl_b, a_b, PERS_b], writes=[a_b])
                        if rh:
                            S.op('dve', (lambda h, a=a, hl=hl, fidx=fidx: h.scalar_tensor_tensor(
                                out=a[:, 1023:1024], in0=hl[:, 1:2], scalar=cw[:, l, fidx, 2:3], in1=a[:, 1023:1024],
                                op0=ALU.mult, op1=ALU.add)), reads=[hl_b, a_b, PERS_b], writes=[a_b])
                        accs.append((a, a_b))
                    (av, av_b), (ag, ag_b) = accs
                    S.op('act', (lambda h, ag=ag: h.activation(out=ag[:], in_=ag[:], func=AF.Silu)), reads=[ag_b], writes=[ag_b])
                    S.op('dve', (lambda h, ag=ag, av=av, fc=fc: h.tensor_tensor(out=GT[:, fc, :], in0=ag[:], in1=av[:], op=ALU.mult)),
                         reads=[ag_b, av_b], writes=[GT_b[fc % 2]])

        def down_proj(t0, cond, nxt):
            pcs = pieces(nxt[0]) if nxt is not None else []
            c = 0
            for nb in range(8):
                k = nb % 2
                S.dma('pool', wdn[k][:], w_down_d[l][:, nb * 256:(nb + 1) * 256].rearrange("(c p) n -> p c n", p=128), IN_b, wdn_b[k])
                if nb < len(pcs):
                    norm_p1(pcs[nb])
                for dj in range(2):
                    dc = nb * 2 + dj
                    for hf in range(2):
                        q = c % 2
                        c += 1
                        col = t0 + hf * 512
                        S.dma('sp', xi[q][:], XT_d[dc, :, col:col + 512], XT_b, xi_b[q])
                        b = next_bank(6)
                        for fc in range(44):
                            S.op('pe', (lambda h, b=b, k=k, fc=fc, dj=dj, hf=hf: h.matmul(
                                bank(b), wdn[k][:, fc, dj * 128:(dj + 1) * 128], GT[:, fc, hf * 512:(hf + 1) * 512],
                                start=(fc == 0), stop=(fc == 43))), reads=[wdn_b[k], GT_b[0], GT_b[1]], writes=[PSB[b]], sig=(fc == 43))
                        S.op('dve', (lambda h, q=q, b=b, dc=dc, cond=cond: h.scalar_tensor_tensor(
                            out=xo[q][:], in0=bank(b), scalar=mod_ap(l, 5, dc, cond), in1=xi[q][:], op0=ALU.mult, op1=ALU.add)),
                            reads=[PSB[b], xi_b[q], MOD_b], writes=[xo_b[q]])
                        S.dma('sp', XT_d[dc, :, col:col + 512], xo[q][:], xo_b[q], XT_b)
                if nb < len(pcs):
                    norm_p2(pcs[nb], nxt[1])

        for pc in pieces(FFN_TILES[0][0]):
            norm_p1(pc)
            norm_p2(pc, FFN_TILES[0][1])
        S.barrier()
        for ti, (t0, cond, segs, lh, rh) in enumerate(FFN_TILES):
            up_proj(t0, cond, segs, lh, rh)
            S.barrier()
            down_proj(t0, cond, FFN_TILES[ti + 1] if ti + 1 < len(FFN_TILES) else None)
            S.barrier()
        S.recycle()

    def g1_odd():
        ph.reset()
        sgb = [ph.alloc("sgb", [128, 512], BF16) for _ in range(3)]
        sgb_b = [Buf("osgb%d" % i, True) for i in range(3)]
        sgf = [ph.alloc("sgf", [128, 512], F32) for _ in range(3)]
        sgf_b = [Buf("osgf%d" % i, True) for i in range(3)]
        cnts = {'b': 0, 'f': 0, 't': 0}

        def nb():
            i = cnts['b'] % 3
            cnts['b'] += 1
            return sgb[i], sgb_b[i]

        def nf():
            i = cnts['f'] % 3
            cnts['f'] += 1
            return sgf[i], sgf_b[i]

        names = ('qf', 'sq', 'qn', 'qg', 't1', 't2')
        tm = {n: [ph.alloc(n, [128, 512], F32) for _ in range(3)] for n in names}
        tm_b = {n: [Buf("o%s%d" % (n, i), True) for i in range(3)] for n in names}
        stages3 = []
        ssb = [ph.alloc("ss", [128, 4], F32) for _ in range(3)]
        ss_b = [Buf("oss%d" % i, True) for i in range(3)]
        qbb = [ph.alloc("qb", [128, 512], BF16) for _ in range(3)]
        qb_b = [Buf("oqb%d" % i, True) for i in range(3)]
        tst = [ph.alloc("tst", [128, 4, 128], BF16) for _ in range(3)]
        tst_b = [Buf("otst%d" % i, True) for i in range(3)]
        ctb = [ph.alloc("ct", [128, 128], F32) for _ in range(3)]
        stb = [ph.alloc("stt", [128, 128], F32) for _ in range(3)]
        ct_b = [Buf("oct%d" % i, True) for i in range(3)]
        st_b = [Buf("ost%d" % i, True) for i in range(3)]
        g4 = ph.alloc("g4", [128, 2, 512], F32)
        G4_b = Buf("g4", True)
        S.dma('sp', g4[:, 0, :], gq4_d.partition_broadcast(128), IN_b, G4_b)
        S.dma('sp', g4[:, 1, :], gk4_d.partition_broadcast(128), IN_b, G4_b)

        wref = {}
        for blk in range(6):
            if blk in (0, 5):
                wt, wb = load_w(wview(od_w_in_d, blk * 512, 512))
            if blk == 0:
                for j in range(4):
                    for t5 in range(6):
                        b = next_bank(6)
                        mm_fm(wt, wb, j, t5, b)
                        sg, sg_b = nf()
                        S.op('act', (lambda h, sg=sg, b=b: h.activation(out=sg[:], in_=bank(b), func=AF.Copy)), reads=[PSB[b]], writes=[sg_b])
                        S.dma('sp', XB_d[j, :, t5 * 512:(t5 + 1) * 512], sg[:], sg_b, XB_b)
            elif blk == 5:
                for tt in range(24):
                    b = next_bank(6)
                    mm_tm(wt, wb, tt, b)
                    sg, sg_b = nb()
                    if tt < 16:
                        S.op('dve', (lambda h, sg=sg, b=b: h.tensor_copy(out=sg[:], in_=bank(b))), reads=[PSB[b]], writes=[sg_b])
                    else:
                        sf, sf_b = nf()
                        S.op('act', (lambda h, sf=sf, b=b: h.activation(out=sf[:], in_=bank(b), func=AF.Copy)), reads=[PSB[b]], writes=[sf_b])
                        S.op('dve', (lambda h, sg=sg, sf=sf: h.tensor_copy(out=sg[:], in_=sf[:])), reads=[sf_b], writes=[sg_b])
                        S.dma('sp', gqv_d[(tt - 16) * 128:(tt - 15) * 128, :], sf[:], sf_b, OUT_b)
                    S.dma('sp', V_d[tt * 128:(tt + 1) * 128, 0:512], sg[:], sg_b, V_b)
            else:
                isk = (blk == 4)
                for tt in range(24):
                    k = cnts['t'] % 3
                    hf = cnts['t'] % 2
                    cnts['t'] += 1
                    qf, sq, qn, qg, t1, t2 = (tm[n][k] for n in names)
                    qf_b, sq_b, qn_b, qg_b, t1_b, t2_b = (tm_b[n][k] for n in names)
                    ss, ss_bb = ssb[k], ss_b[k]
                    qb, qb_bb = qbb[k], qb_b[k]
                    ct, stt = ctb[k], stb[k]
                    ts_, ts_b = tst[k], tst_b[k]
                    gi = 1 if isk else 0

                    def st1(blk=blk, tt=tt, qf=qf, qf_b=qf_b, sq=sq, sq_b=sq_b, ss=ss, ss_bb=ss_bb):
                        if tt == 0:
                            wref[blk] = load_w(wview(od_w_in_d, blk * 512, 512))
                        wt, wb = wref[blk]
                        b = next_bank(6)
                        mm_tm(wt, wb, tt, b)
                        S.op('act', (lambda h: h.activation(out=qf[:], in_=bank(b), func=AF.Copy)), reads=[PSB[b]], writes=[qf_b])
                        S.op('act', (lambda h: h.activation(out=sq[:], in_=qf[:], func=AF.Square)), reads=[qf_b], writes=[sq_b])
                        S.op('dve', (lambda h: h.tensor_reduce(out=ss[:], in_=sq[:].rearrange("p (h d) -> p h d", d=128), axis=AX.X, op=ALU.add)),
                             reads=[sq_b], writes=[ss_bb])
                        S.op('dve', (lambda h: h.tensor_scalar(out=ss[:], in0=ss[:], scalar1=1.0 / 128.0, scalar2=EPS, op0=ALU.mult, op1=ALU.add)),
                             reads=[ss_bb], writes=[ss_bb])
                        S.op('act', (lambda h: h.activation(out=ss[:], in_=ss[:], func=AF.Sqrt)), reads=[ss_bb], writes=[ss_bb])

                    def st2(tt=tt, k=k, isk=isk, gi=gi, qf=qf, qf_b=qf_b, qn=qn, qn_b=qn_b, qg=qg, qg_b=qg_b, t1=t1, t1_b=t1_b, t2=t2, t2_b=t2_b,
                            ss=ss, ss_bb=ss_bb, qb=qb, qb_bb=qb_bb, ct=ct, stt=stt):
                        S.op('dve', (lambda h: h.reciprocal(out=ss[:], in_=ss[:])), reads=[ss_bb], writes=[ss_bb])
                        S.op('dve', (lambda h: h.tensor_tensor(
                            out=qn[:].rearrange("p (h d) -> p h d", d=128), in0=qf[:].rearrange("p (h d) -> p h d", d=128),
                            in1=ss[:].unsqueeze(2).to_broadcast([128, 4, 128]), op=ALU.mult)), reads=[qf_b, ss_bb], writes=[qn_b])
                        S.op('dve', (lambda h: h.tensor_tensor(out=qg[:], in0=qn[:], in1=g4[:, gi, :], op=ALU.mult)),
                             reads=[qn_b, G4_b], writes=[qg_b])
                        if tt >= 16:
                            if isk:
                                S.dma('sp', gqk_d[(tt - 16) * 128:(tt - 15) * 128, :], qg[:], qg_b, OUT_b)
                            S.op('act', (lambda h: h.activation(out=qb[:], in_=qg[:], func=AF.Copy)), reads=[qg_b], writes=[qb_bb])
                        else:
                            S.dma('sp', ct[:], ropeC_d[tt * 128:(tt + 1) * 128, :], IN_b, ct_b[k])
                            S.dma('sp', stt[:], ropeS_d[tt * 128:(tt + 1) * 128, :], IN_b, st_b[k])
                            S.op('dve', (lambda h: h.tensor_tensor(
                                out=t1[:].rearrange("p (h d) -> p h d", d=128), in0=qg[:].rearrange("p (h d) -> p h d", d=128),
                                in1=ct[:].unsqueeze(1).to_broadcast([128, 4, 128]), op=ALU.mult)), reads=[qg_b, ct_b[k]], writes=[t1_b])
                            for s_ in range(2):
                                S.op('dve', (lambda h, s_=s_: h.tensor_tensor(
                                    out=t2[:].rearrange("p (h r s i) -> p h r s i", h=4, r=2, s=2)[:, :, :, s_, :],
                                    in0=qg[:].rearrange("p (h r s i) -> p h r s i", h=4, r=2, s=2)[:, :, :, 1 - s_, :],
                                    in1=stt[:].rearrange("p (r s i) -> p r s i", r=2, s=2)[:, :, s_, :].unsqueeze(1).to_broadcast([128, 4, 2, 32]),
                                    op=ALU.mult)), reads=[qg_b, st_b[k]], writes=[t2_b])
                            S.op('dve', (lambda h: h.tensor_tensor(out=qb[:], in0=t1[:], in1=t2[:], op=ALU.add)),
                                 reads=[t1_b, t2_b], writes=[qb_bb])

                    def st3(tt=tt, blk=blk, isk=isk, hf=hf, qb=qb, qb_bb=qb_bb, ts_=ts_, ts_b=ts_b):
                        for hh in range(4):
                            S.op('pe', (lambda h, hh=hh: h.transpose(PSbf[:, hf * 512 + hh * 128: hf * 512 + (hh + 1) * 128],
                                                                     qb[:, hh * 128:(hh + 1) * 128], ident_b[:])),
                                 reads=[qb_bb, PERS_b], writes=[PSB[6]], sig=(hh == 3))
                        S.op('act', (lambda h: h.activation(out=ts_[:], in_=PSbf[:, hf * 512:(hf + 1) * 512].rearrange("p (h t) -> p h t", t=128), func=AF.Copy)),
                             reads=[PSB[6]], writes=[ts_b])
                        if isk:
                            S.dma('sp', KT_d[0:4, :, tt * 128:(tt + 1) * 128].rearrange("h p t -> p h t"), ts_[:], ts_b, KT_b)
                        else:
                            h0 = (blk - 1) * 4
                            S.dma('sp', QT_d[h0:h0 + 4, :, tt * 128:(tt + 1) * 128].rearrange("h p t -> p h t"), ts_[:], ts_b, QT_b)

                    stages3.append((st1, st2, st3))
        n3 = len(stages3)
        for i in range(n3 + 2):
            if i < n3:
                stages3[i][0]()
            if 0 <= i - 1 < n3:
                stages3[i - 1][1]()
            if 0 <= i - 2 < n3:
                stages3[i - 2][2]()
        S.barrier()
        S.recycle()

    def attn_odd():
        ph.reset(WB_OFF)
        ar = attn_arena()
        kctok = ph.alloc("gkctok", [128, 2, 512], BF16)
        vc = ph.alloc("gvc", [128, 2, 512], BF16)
        kcT = ph.alloc("gkcT", [128, 4, 256], BF16)
        KTOK_b, VC_b, KCT_b = Buf("gkctok", True), Buf("gvc", True), Buf("gkcT", True)
        S.dma('pool', kctok[:], cgk_d.rearrange("(c p) d -> p c d", p=128), IN_b, KTOK_b)
        S.dma('pool', vc[:], cgv_d.rearrange("(c p) d -> p c d", p=128), IN_b, VC_b)
        for kv in range(4):
            for c in range(2):
                S.op('pe', (lambda h, kv=kv, c=c: h.transpose(PSbf[:, c * 128:(c + 1) * 128], kctok[:, c, kv * 128:(kv + 1) * 128], ident_b[:])),
                     reads=[KTOK_b, PERS_b], writes=[PSB[6]], sig=(c == 1))
            S.op('dve', (lambda h, kv=kv: h.tensor_copy(out=kcT[:, kv, :], in_=PSbf[:, 0:256])), reads=[PSB[6]], writes=[KCT_b])
        qs = [ph.alloc("gqs", [128, TS], BF16) for _ in range(2)]
        ks = [ph.alloc("gks", [128, TS], BF16) for _ in range(2)]
        vs = [ph.alloc("gvs", [128, 16, 128], BF16) for _ in range(2)]
        qs_b = [Buf("gqs%d" % i, True) for i in range(2)]
        ks_b = [Buf("gks%d" % i, True) for i in range(2)]
        vs_b = [Buf("gvs%d" % i, True) for i in range(2)]
        ptb = [ph.alloc("gpt", [128, 18, 512], BF16) for _ in range(2)]
        ptb_b = [Buf("gpt%d" % i, True) for i in range(2)]
        recb = ph.alloc("grec", [128, 512], F32)
        recb_b = Buf("grec", True)
        c2 = 0
        scn = [0]
        stages = []
        for hq in range(12):
            kv = hq // 3
            i = hq % 2

            def head_loads(hq=hq, kv=kv, i=i):
                S.dma('sp', qs[i][:], QT_d[hq, :, 0:TS], QT_b, qs_b[i])
                S.dma('sp', ks[i][:], KT_d[kv, :, 0:TS], KT_b, ks_b[i])
                S.dma('sp', vs[i][:], V_d[0:TS, kv * 128:(kv + 1) * 128].rearrange("(c p) d -> p c d", p=128), V_b, vs_b[i])

            for qt in range(4):
                pt = ptb[c2 % 2]
                pt_b = ptb_b[c2 % 2]
                c2 += 1

                def st1(hq=hq, kv=kv, i=i, qt=qt, pt=pt, pt_b=pt_b, head_loads=head_loads):
                    if qt == 0:
                        head_loads()
                    for kc in range(18):
                        sb = scn[0] % 4
                        scn[0] += 1
                        if kc < 2:
                            lhs = (lambda kc=kc: kcT[:, kv, kc * 128:(kc + 1) * 128])
                            rb = KCT_b
                        else:
                            lhs = (lambda kc=kc: ks[i][:, (kc - 2) * 128:(kc - 1) * 128])
                            rb = ks_b[i]
                        S.op('pe', (lambda h, sb=sb, lhs=lhs: h.matmul(bank(sb), lhs(), qs[i][:, qt * 512:(qt + 1) * 512], start=True, stop=True)),
                             reads=[rb, qs_b[i]], writes=[PSB[sb]], sig=True)
                        S.op('act', (lambda h, kc=kc, sb=sb: h.activation(out=pt[:, kc, :], in_=bank(sb), func=AF.Exp, scale=ISQ128)),
                             reads=[PSB[sb]], writes=[pt_b])

                def st2(hq=hq, kv=kv, i=i, qt=qt, pt=pt, pt_b=pt_b):
                    for kc in range(18):
                        if kc < 2:
                            lhs = (lambda kc=kc: vc[:, kc, kv * 128:(kv + 1) * 128])
                            rb = VC_b
                        else:
                            lhs = (lambda kc=kc: vs[i][:, kc - 2, :])
                            rb = vs_b[i]
                        S.op('pe', (lambda h, lhs=lhs, kc=kc: h.matmul(bank(4), lhs(), pt[:, kc, :], start=(kc == 0), stop=(kc == 17))),
                             reads=[rb, pt_b], writes=[PSB[4]], sig=(kc == 17))
                    for kc in range(18):
                        S.op('pe', (lambda h, kc=kc: h.matmul(bank(5), ones_b[:], pt[:, kc, :], start=(kc == 0), stop=(kc == 17))),
                             reads=[pt_b, PERS_b], writes=[PSB[5]], sig=(kc == 17))
                    S.op('dve', (lambda h: h.reciprocal(out=recb[:], in_=bank(5))), reads=[PSB[5]], writes=[recb_b])
                    S.op('dve', (lambda h: h.tensor_tensor(out=AT[:, 4 + hq, qt * 512:(qt + 1) * 512], in0=bank(4), in1=recb[:], op=ALU.mult)),
                         reads=[PSB[4], recb_b], writes=[AT_b[qt]])

                stages.append((st1, st2))
        pipeline2(stages)
        stages = []
        for hq in range(12):
            block_attn_prompt(hq, hq // 3, (hq // 3) * 128, 4 + hq, ar, stages)
        pipeline2(stages)
        S.barrier()
        S.recycle()

    def pool_phase():
        ph.reset(WB_OFF)
        pw = ph.alloc("pw", [128, 4, 128], BF16)
        psc = ph.alloc("psc", [128, 4], F32)
        PC_b = Buf("pconst", True)
        S.dma('pool', pw[:], pw_d.rearrange("g c d -> c g d"), IN_b, PC_b)
        S.dma('sp', psc[:], pscT_d, IN_b, PC_b)
        L = TS + 32
        X0 = ph.alloc("px0", [128, L], F32)
        A = ph.alloc("pA", [128, L], F32)
        Bb = ph.alloc("pB", [128, L], F32)
        invc = ph.alloc("pinvc", [128, TS], F32)
        tmp = ph.alloc("ptmp", [128, TS], F32)
        pooled = ph.alloc("ppooled", [128, TS], BF16)
        X0_b, A_b, B_b, I_b, T_b, P_b = (Buf(n, True) for n in ("px0", "pA", "pB", "pinvc", "ptmp", "ppooled"))
        S.op('dve', (lambda h: h.memset(X0[:], 0.0)), writes=[X0_b])

        def chain(levels, Lx):
            cur, cur_b = X0, X0_b
            outs = [(A, A_b), (Bb, B_b)]
            for lev in range(1, levels + 1):
                o, o_b = outs[lev % 2]
                if lev == 1:
                    S.op('dve', (lambda h, o=o, cur=cur: h.tensor_tensor(out=o[:, 1:Lx], in0=cur[:, 0:Lx - 1], in1=cur[:, 1:Lx], op=ALU.add)),
                         reads=[cur_b], writes=[o_b])
                else:
                    d = 2 ** (lev - 2)
                    S.op('dve', (lambda h, o=o, cur=cur, d=d: h.tensor_tensor(out=o[:, d:Lx - d], in0=cur[:, 0:Lx - 2 * d], in1=cur[:, 2 * d:Lx], op=ALU.add)),
                         reads=[cur_b], writes=[o_b])
                cur, cur_b = o, o_b
            return cur, cur_b

        for g in range(4):
            S.dma('sp', X0[:, 16:16 + TS], XB_d[g, :, 0:TS], XB_b, X0_b)
            S.dma('sp', invc[:], invc_s_d[g].partition_broadcast(128), IN_b, I_b)
            sw, sw_b = chain(g + 1, L)
            S.op('dve', (lambda h, sw=sw: h.tensor_tensor(out=tmp[:], in0=sw[:, 16:16 + TS], in1=invc[:], op=ALU.mult)), reads=[sw_b, I_b], writes=[T_b])
            S.op('dve', (lambda h: h.tensor_tensor(out=pooled[:], in0=tmp[:], in1=X0[:, 16:16 + TS], op=ALU.subtract)), reads=[T_b, X0_b], writes=[P_b])
            for t5 in range(4):
                b = next_bank(6)
                S.op('pe', (lambda h, b=b, g=g, t5=t5: h.matmul(bank(b), pw[:, g, :], pooled[:, t5 * 512:(t5 + 1) * 512], start=True, stop=True)),
                     reads=[P_b, PC_b], writes=[PSB[b]], sig=True)
                S.op('act', (lambda h, b=b, g=g, t5=t5: h.activation(out=AT[:, g, t5 * 512:(t5 + 1) * 512], in_=bank(b), func=AF.Copy, scale=psc[:, g:g + 1])),
                     reads=[PSB[b], PC_b], writes=[AT_b[t5]])
        S.op('dve', (lambda h: h.memset(X0[:], 0.0)), writes=[X0_b])
        Lp = 256 + 32
        for g in range(4):
            S.dma('sp', invc[:, 0:256], invc_p_d[g].partition_broadcast(128), IN_b, I_b)
            for s in range(4):
                S.dma('sp', X0[:, 16:16 + 256], XB_d[g, :, TS + s * 256: TS + (s + 1) * 256], XB_b, X0_b)
                sw, sw_b = chain(g + 1, Lp)
                S.op('dve', (lambda h, sw=sw: h.tensor_tensor(out=tmp[:, 0:256], in0=sw[:, 16:16 + 256], in1=invc[:, 0:256], op=ALU.mult)),
                     reads=[sw_b, I_b], writes=[T_b])
                S.op('dve', (lambda h, s=s: h.tensor_tensor(out=pooled[:, s * 256:(s + 1) * 256], in0=tmp[:, 0:256], in1=X0[:, 16:16 + 256], op=ALU.subtract)),
                     reads=[T_b, X0_b], writes=[P_b])
            for t5 in range(2):
                b = next_bank(6)
                S.op('pe', (lambda h, b=b, g=g, t5=t5: h.matmul(bank(b), pw[:, g, :], pooled[:, t5 * 512:(t5 + 1) * 512], start=True, stop=True)),
                     reads=[P_b, PC_b], writes=[PSB[b]], sig=True)
                S.op('act', (lambda h, b=b, g=g, t5=t5: h.activation(out=AT[:, g, TS + t5 * 512: TS + (t5 + 1) * 512], in_=bank(b), func=AF.Copy, scale=psc[:, g:g + 1])),
                     reads=[PSB[b], PC_b], writes=[AT_b[4 + t5]])
        S.barrier()
        S.recycle()

    def final_phase():
        ph.reset(WB_OFF)
        ar = norm_arena(512, nxi=2, nsq=4, ntmp=1)
        yT = [ph.alloc("yT", [128, 4, 512], F32) for _ in range(2)]
        yT_b = [Buf("yT%d" % i, True) for i in range(2)]
        ytok = [ph.alloc("ytok", [128, 512], F32) for _ in range(4)]
        ytok_b = [Buf("ytok%d" % i, True) for i in range(4)]
        cnt = {'y': 0, 'q': 0}

        def stage_b(t5, xi, xi_b, rstd, rstd_b):
            for dq in range(4):
                y = yT[cnt['q'] % 2]
                y_b = yT_b[cnt['q'] % 2]
                cnt['q'] += 1
                for j in range(4):
                    dc = dq * 4 + j
                    S.op('dve', (lambda h, y=y, j=j, dc=dc: h.scalar_tensor_tensor(
                        out=y[:, j, :], in0=xi[:, dc, 0:512], scalar=ngT[:, 4, dc:dc + 1], in1=rstd[:, 0:512], op0=ALU.mult, op1=ALU.mult)),
                        reads=[xi_b, rstd_b, PERS_b], writes=[y_b])
                for ts in range(4):
                    b = next_bank(6)
                    for j in range(4):
                        S.op('pe', (lambda h, b=b, j=j, y=y, ts=ts: h.transpose(
                            PS[:, b * 512 + j * 128: b * 512 + (j + 1) * 128], y[:, j, ts * 128:(ts + 1) * 128], ident_f[:])),
                            reads=[y_b, PERS_b], writes=[PSB[b]], sig=(j == 3))
                    k = cnt['y'] % 4
                    cnt['y'] += 1
                    S.op('act', (lambda h, b=b, k=k: h.activation(out=ytok[k][:], in_=bank(b), func=AF.Copy)),
                         reads=[PSB[b]], writes=[ytok_b[k]])
                    tok0 = t5 * 512 + ts * 128
                    if tok0 < TS:
                        S.dma('sp', ys_d[tok0:tok0 + 128, dq * 512:(dq + 1) * 512], ytok[k][:], ytok_b[k], OUT_b)
                    else:
                        S.dma('sp', yp_d[tok0 - TS:tok0 - TS + 128, dq * 512:(dq + 1) * 512], ytok[k][:], ytok_b[k], OUT_b)

        prev = None
        for t5 in range(6):
            st = norm_stats([(t5 * 512, 512)], ar)
            if prev is not None:
                stage_b(*prev)
            prev = (t5,) + tuple(st)
        stage_b(*prev)
        S.barrier()
        S.recycle()

    STAGES = ['n1_0', 'g1_0', 'attn_0', 'fnet_0', 'mix0', 'ffn0', 'n1_1', 'g1_1', 'attn_1', 'pool_1', 'mix1', 'ffn1', 'final']
    nstage = len(STAGES) if stop_after is None else STAGES.index(stop_after) + 1
    fns = [in_phase, g1_even, attn_even, fnet_phase, lambda: g2_phase(0, ev_w_out_d), lambda: ffn_phase(0),
           lambda: norm1_phase(1), g1_odd, attn_odd, pool_phase, lambda: g2_phase(1, od_w_out_d), lambda: ffn_phase(1), final_phase]
    for fn in fns[:nstage]:
        fn()
    S.finish()
    S.emit()
    st.close()
    return nc


def _build_bt(bias):
    NEG = np.float32(-30000.0)
    kc = np.arange(64)[:, None]
    qc = np.arange(64)[None, :]
    qstart = np.clip(qc - 8, 0, 48)
    colvalid = (kc >= qstart) & (kc < qstart + 16)
    dcidx = np.clip(kc - qc, -15, 15) + 15
    types = ([(5, 5 + dm) for dm in range(-2, 3)] + [(0, m) for m in range(4)] + [(1, m) for m in range(4)]
             + [(14, m) for m in range(12, 16)] + [(15, m) for m in range(12, 16)])
    bt = np.full((12, 21, 128, 128), NEG, np.float32)
    for ti, (j, m) in enumerate(types):
        for a in range(2):
            for b in range(2):
                kr = 2 * m + a
                r = 2 * j + b
                r0 = min(max(r - 4, 0), 24)
                if not (r0 <= kr < r0 + 8):
                    continue
                blk = bias[:, kr - r + 7][:, dcidx]
                bt[:, ti, a * 64:(a + 1) * 64, b * 64:(b + 1) * 64] = np.where(colvalid[None], blk, NEG)
    return bt


def _dft(n):
    k = np.arange(n, dtype=np.int64)
    ang = 2.0 * np.pi * ((k[:, None] * k[None, :]) % n).astype(np.float64) / n
    return (np.cos(ang) / np.sqrt(n)).astype(np.float32), (np.sin(ang) / np.sqrt(n)).astype(np.float32)


_CONST = {}


def _consts():
    if _CONST:
        return _CONST
    c2k, s2k = _dft(2048)
    c256, s256 = _dft(256)
    c128, s128 = _dft(128)
    t = np.arange(TS)
    pos = np.stack([(t // 64), (t % 64)], 1).astype(np.float32)
    inv = (10000.0 ** (-np.arange(0, 64, 2, dtype=np.float32) / 64.0)).astype(np.float32)
    ang = pos[:, :, None] * inv[None, None, :]
    cs = np.cos(ang).astype(np.float32)
    sn = np.sin(ang).astype(np.float32)
    ropeC = np.stack([cs, cs], 2).reshape(TS, 128)
    ropeS = np.stack([-sn, sn], 2).reshape(TS, 128)

    def invc(n):
        tt = np.arange(n)
        out = []
        for w in (2, 4, 8, 16):
            lo = np.clip(tt - w // 2, 0, n)
            hi = np.clip(tt + w // 2, 0, n)
            out.append(1.0 / (hi - lo).astype(np.float32))
        return np.stack(out, 0).astype(np.float32)

    _CONST.update(cn2k=c2k, sn2k=s2k, cn256=c256, sn256=s256, cc=np.ascontiguousarray(np.concatenate([c128, -s128], 1)),
                  ropeC=np.ascontiguousarray(ropeC), ropeS=np.ascontiguousarray(ropeS), invc_s=invc(TS), invc_p=invc(256),
                  ident=np.eye(128, dtype=np.float32))
    return _CONST


def make_in_maps(inp):
    f = lambda a: np.ascontiguousarray(np.asarray(a, dtype=np.float32))
    shared = {
        "ada_w": f(inp["ada_w"]),
        "ada_bT": f(np.asarray(inp["ada_b"]).reshape(2, 96, 128).transpose(0, 2, 1)),
        "ngT": f(np.stack([inp["norm1_g"][0], inp["norm1_g"][1], inp["norm2_g"][0], inp["norm2_g"][1], inp["final_norm_g"]], 0)
                 .reshape(5, 16, 128).transpose(2, 0, 1)),
        "ev_w_in": f(inp["ev_w_in"][0]),
        "ev_w_out": f(inp["ev_w_out"][0]),
        "od_w_in": f(inp["od_w_in"][0]),
        "od_w_out": f(inp["od_w_out"][0]),
        "w_up": f(inp["ffn_w_up"]),
        "w_down": f(inp["ffn_w_down"]),
        "cwT": f(np.asarray(inp["ffn_conv_w"]).reshape(2, 3, 88, 128).transpose(3, 0, 2, 1)),
        "cbT": f(np.asarray(inp["ffn_conv_b"]).reshape(2, 88, 128).transpose(2, 0, 1)),
        "fin_g": f(inp["final_norm_g"]),
        "bt": _build_bt(f(inp["ev_na_bias"][0])),
        "fw": f(inp["ev_fnet_w"][0]),
        "pw": f(inp["od_pool_w"][0]),
        "pscT": f(np.asarray(inp["od_pool_scale"][0]).reshape(4, 128).T),
        "gq4": f(np.tile(np.asarray(inp["od_q_norm_g"][0]), 4)),
        "gk4": f(np.tile(np.asarray(inp["od_k_norm_g"][0]), 4)),
    }
    shared.update(_consts())
    maps = []
    for c in range(NCORES):
        m = dict(shared)
        m["xs"] = f(inp["x_sample"][c])
        m["xp"] = f(np.asarray(inp["x_prompt"][4 * c:4 * c + 4]).reshape(TP, D))
        m["condT"] = f(np.stack([inp["c"][c], inp["c_ctx"]], axis=1))
        m["cnk"] = f(np.asarray(inp["cache_na_k"][c, 0]).reshape(256, 1536))
        m["cnv"] = f(np.asarray(inp["cache_na_v"][c, 0]).reshape(256, 1536))
        m["cgk"] = f(np.asarray(inp["cache_gqa_k"][c, 0]).reshape(256, 512))
        m["cgv"] = f(np.asarray(inp["cache_gqa_v"][c, 0]).reshape(256, 512))
        maps.append(m)
    return maps


_NC_CACHE = {}


def kernel(**inputs):
    if "nc" not in _NC_CACHE:
        _NC_CACHE["nc"] = build()
    nc = _NC_CACHE["nc"]
    maps = make_in_maps(inputs)
    res = run_bass_kernel_spmd(nc, maps, core_ids=list(range(NCORES)))
    r = res.results
    y_prompt = np.concatenate([r[c]["yp"].reshape(4, 256, D) for c in range(NCORES)], 0).astype(np.float32)
    y_sample = np.stack([r[c]["ys"] for c in range(NCORES)], 0).astype(np.float32)
    nak = np.concatenate([r[c]["nak"].reshape(4, 1, 256, 12, 128) for c in range(NCORES)], 0).astype(np.float32)
    nav = np.concatenate([r[c]["nav"].reshape(4, 1, 256, 12, 128) for c in range(NCORES)], 0).astype(np.float32)
    gqk = np.concatenate([r[c]["gqk"].reshape(4, 1, 256, 4, 128) for c in range(NCORES)], 0).astype(np.float32)
    gqv = np.concatenate([r[c]["gqv"].reshape(4, 1, 256, 4, 128) for c in range(NCORES)], 0).astype(np.float32)
    return (y_prompt, y_sample, nak, nav, gqk, gqv)
```

```python
import math
from contextlib import ExitStack

import numpy as np
import concourse.bass as bass
import concourse.mybir as mybir
from concourse.bass_utils import run_bass_kernel_spmd

F32 = mybir.dt.float32
BF16 = mybir.dt.bfloat16
AF = mybir.ActivationFunctionType
ALU = mybir.AluOpType
AX = mybir.AxisListType

T = 3072
TS = 2048
TP = 1024
D = 2048
KC = 16
DFF = 5632
EPS = 1e-6
NCORES = 8
SQ128 = math.sqrt(128.0)
ISQ128 = 1.0 / SQ128


class Buf:
    def __init__(self, name, is_sbuf=False, persistent=False, loose=False):
        self.name = name
        self.is_sbuf = is_sbuf
        self.persistent = persistent
        self.loose = loose
        self.w = {}
        self.r = {}
        self.wsem = {}
        self.rsem = {}


class SemC:
    def __init__(self, sem):
        self.sem = sem
        self.count = 0


class Eng:
    def __init__(self, name):
        self.name = name
        self.is_pe = name == 'pe'
        self.semc = None
        self.ops = []
        self.seen = {}
        self.pending_unsig = False


class Sched:
    def __init__(self, nc, stack):
        self.nc = nc
        self.stack = stack
        self.engs = {n: Eng(n) for n in ('pe', 'act', 'dve', 'pool', 'sp')}
        self.sems = []
        self.free_semcs = {}
        self.phase_semcs = []
        self.nsem = 0
        for n, e in self.engs.items():
            e.semc = self.newsem('prog_' + n, True)

    def newsem(self, name, persistent=False, cls='eng'):
        fl = self.free_semcs.setdefault(cls, [])
        if not persistent and fl:
            s = fl.pop()
        else:
            self.nsem += 1
            s = SemC(self.stack.enter_context(self.nc.semaphore('%s_%d' % (name, self.nsem))))
            s.cls = cls
            self.sems.append(s)
        if not persistent:
            self.phase_semcs.append(s)
        return s

    def recycle(self):
        for sc in self.phase_semcs:
            self.free_semcs.setdefault(sc.cls, []).append(sc)
        self.phase_semcs = []

    def _collect(self, e, reads, writes):
        need = {}
        for b in reads:
            for k, v in b.w.items():
                if need.get(k, 0) < v:
                    need[k] = v
        for b in writes:
            for d in (b.w, b.r):
                for k, v in d.items():
                    if need.get(k, 0) < v:
                        need[k] = v
        waits = []
        for k, v in need.items():
            if k is e.semc and e.is_pe:
                continue
            if e.seen.get(k, 0) >= v:
                continue
            e.seen[k] = v
            waits.append((k.sem, v))
        return waits

    def op(self, eng, fn, reads=(), writes=(), sig=True):
        e = self.engs[eng]
        waits = self._collect(e, reads, writes)
        if sig:
            e.semc.count += 1
            val = e.semc.count
            inc = (e.semc.sem, 1)
            e.pending_unsig = False
        else:
            val = e.semc.count + 1
            inc = None
            e.pending_unsig = True
        e.ops.append((waits, fn, inc))
        for b in reads:
            if b.r.get(e.semc, 0) < val:
                b.r[e.semc] = val
        for b in writes:
            b.w = {e.semc: val}
            b.r = {}

    def dma(self, queue, out_ap, in_ap, src, dst, **kw):
        e = self.engs[queue]
        waits = self._collect(e, [] if src.loose else [src], [] if dst.loose else [dst])
        if dst.is_sbuf:
            if queue not in dst.wsem:
                dst.wsem[queue] = self.newsem('w%s_%s' % (queue, dst.name), dst.persistent, queue)
            sc = dst.wsem[queue]
        else:
            if queue not in src.rsem:
                src.rsem[queue] = self.newsem('r%s_%s' % (queue, src.name), src.persistent, queue)
            sc = src.rsem[queue]
        sc.count += 16
        val = sc.count
        e.ops.append((waits, (lambda h, o=out_ap, i=in_ap, kw=kw: h.dma_start(out=o, in_=i, **kw)), (sc.sem, 16)))
        if not src.loose and src.r.get(sc, 0) < val:
            src.r[sc] = val
        if dst.is_sbuf:
            dst.w = {sc: val}
            dst.r = {}
        elif not dst.loose:
            dst.w = {**dst.w, sc: val}

    def barrier(self):
        for n, e in self.engs.items():
            if e.pending_unsig:
                raise RuntimeError('unsignalled op pending on ' + n)
        for n, e in self.engs.items():
            waits = []
            for sc in self.sems:
                v = sc.count
                if v == 0 or e.seen.get(sc, 0) >= v:
                    continue
                if sc is e.semc and e.is_pe:
                    continue
                e.seen[sc] = v
                waits.append((sc.sem, v))
            if waits:
                e.ops.append((waits, None, None))

    def finish(self):
        self.barrier()

    def emit(self):
        nc = self.nc
        with nc.Block() as block:
            def run(h, e):
                for waits, fn, inc in e.ops:
                    for s, v in waits:
                        h.wait_ge(s, v)
                    if fn is None:
                        continue
                    ins = fn(h)
                    if inc is not None:
                        ins.then_inc(inc[0], inc[1])

            @block.tensor
            def _(h):
                run(h, self.engs['pe'])

            @block.scalar
            def _(h):
                run(h, self.engs['act'])

            @block.vector
            def _(h):
                run(h, self.engs['dve'])

            @block.gpsimd
            def _(h):
                run(h, self.engs['pool'])

            @block.sync
            def _(h):
                run(h, self.engs['sp'])


class Arena:
    def __init__(self, nc, base, limit):
        self.nc = nc
        self.base = base
        self.off = base
        self.limit = limit
        self.n = 0

    def reset(self, off=None):
        self.off = self.base if off is None else off

    def alloc(self, name, shape, dtype):
        self.n += 1
        esz = 2 if dtype == BF16 else 4
        size = esz
        for s in shape[1:]:
            size *= s
        off = (max(self.off, 0) + 31) // 32 * 32
        assert off + size <= self.limit, (name, off, size, self.limit)
        t = self.nc.alloc_sbuf_tensor_at('%s_%d' % (name, self.n), list(shape), dtype, offset=off)
        self.off = off + size
        return t


def build(debug=False, stop_after=None):
    nc = bass.Bass("TRN2", target_bir_lowering=False)
    st = ExitStack()
    S = Sched(nc, st)

    def din(name, shape, dt=F32):
        return nc.dram_tensor(name, list(shape), dt, kind="ExternalInput").ap()

    def dout(name, shape, dt=F32):
        return nc.dram_tensor(name, list(shape), dt, kind="ExternalOutput").ap()

    def dscr(name, shape, dt=F32):
        if dt == BF16:
            shp = list(shape[:-1]) + [shape[-1] // 2]
            return nc.dram_tensor(name, shp, F32, kind="ExternalOutput").ap().bitcast(BF16)
        return nc.dram_tensor(name, list(shape), F32, kind="ExternalOutput").ap()

    xs_d = din("xs", [TS, D])
    xp_d = din("xp", [TP, D])
    condT_d = din("condT", [D, 2])
    cnk_d = din("cnk", [256, 1536])
    cnv_d = din("cnv", [256, 1536])
    cgk_d = din("cgk", [256, 512])
    cgv_d = din("cgv", [256, 512])
    ada_w_d = din("ada_w", [2, D, 6 * D])
    ada_bT_d = din("ada_bT", [2, 128, 96])
    ngT_d = din("ngT", [128, 5, 16])
    ev_w_in_d = din("ev_w_in", [D, 5120])
    ev_w_out_d = din("ev_w_out", [D, D])
    od_w_in_d = din("od_w_in", [D, 3072])
    od_w_out_d = din("od_w_out", [D, D])
    w_up_d = din("w_up", [2, D, 2 * DFF])
    w_down_d = din("w_down", [2, DFF, D])
    cwT_d = din("cwT", [128, 2, 88, 3])
    cbT_d = din("cbT", [128, 2, 88])
    ident_d = din("ident", [128, 128])
    fin_g_d = din("fin_g", [D])
    bt_d = din("bt", [12, 21, 128, 128])
    cn2k_d = din("cn2k", [2048, 2048])
    sn2k_d = din("sn2k", [2048, 2048])
    cn256_d = din("cn256", [256, 256])
    sn256_d = din("sn256", [256, 256])
    cc_d = din("cc", [128, 256])
    fw_d = din("fw", [4, 128, 128])
    pw_d = din("pw", [4, 128, 128])
    pscT_d = din("pscT", [128, 4])
    gq4_d = din("gq4", [512])
    gk4_d = din("gk4", [512])
    ropeC_d = din("ropeC", [TS, 128])
    ropeS_d = din("ropeS", [TS, 128])
    invc_s_d = din("invc_s", [4, TS])
    invc_p_d = din("invc_p", [4, 256])
    yp_d = dout("yp", [TP, D])
    ys_d = dout("ys", [TS, D])
    nak_d = dout("nak", [TP, 1536])
    nav_d = dout("nav", [TP, 1536])
    gqk_d = dout("gqk", [TP, 512])
    gqv_d = dout("gqv", [TP, 512])
    XT_d = dscr("XT", [16, 128, T])
    QT_d = dscr("QT", [12, 128, T], BF16)
    KT_d = dscr("KT", [12, 128, T], BF16)
    V_d = dscr("Vs", [T, 1536], BF16)
    XB_d = dscr("XB", [4, 128, T], F32)
    XT_b = Buf("XT", loose=True)
    QT_b = Buf("QT", loose=True)
    KT_b = Buf("KT", loose=True)
    V_b = Buf("V", loose=True)
    XB_b = Buf("XB", loose=True)
    IN_b = Buf("inputs", loose=True)
    OUT_b = Buf("outputs", loose=True)

    PS = nc.alloc_psum_tensor("psall", [128, 4096], F32)
    PSB = [Buf("psb%d" % i) for i in range(8)]

    def bank(i):
        return PS[:, i * 512:(i + 1) * 512]

    BASE = 16512
    LIMIT = BASE + 212800
    pers = Arena(nc, BASE, BASE + 12288)
    ident_f = pers.alloc("ident_f", [128, 128], F32)
    ident_b = pers.alloc("ident_b", [128, 128], BF16)
    ones_b = pers.alloc("ones_b", [128, 128], BF16)
    modT = pers.alloc("modT", [128, 2, 96, 2], F32)
    gs = pers.alloc("gs", [128, 2, 2, 16, 2], F32)
    ngT = pers.alloc("ngT", [128, 5, 16], F32)
    cw = pers.alloc("cw", [128, 2, 88, 3], F32)
    cb = pers.alloc("cb", [128, 2, 88], F32)
    scT = pers.alloc("scT", [128, 16, 2], BF16)
    condT = pers.alloc("condT", [128, 16, 2], F32)
    adab = pers.alloc("adab", [128, 2, 96], F32)
    hsave = pers.alloc("hsave", [128, 16, 2], BF16)
    HS_b = Buf("hsave", True, True)
    PERS_b = Buf("pers", True, True)
    MOD_b = Buf("mod", True, True)

    AT = nc.alloc_sbuf_tensor_at("AT", [128, 16, T], BF16, offset=BASE + 12288)
    AT_b = [Buf("AT%d" % i, True) for i in range(6)]
    WB_OFF = BASE + 12288 + 98304
    WB = [nc.alloc_sbuf_tensor_at("WB%d" % i, [128, 16, 512], BF16, offset=WB_OFF + i * 16384) for i in range(2)]
    WB_b = [Buf("WB%d" % i, True, True) for i in range(2)]
    PH_OFF = WB_OFF + 32768
    ph = Arena(nc, PH_OFF, LIMIT - 7 * 8192)

    wcount = [0]

    def load_w(src_ap_pcn):
        i = wcount[0] % 2
        wcount[0] += 1
        kc, n = src_ap_pcn.shape[1], src_ap_pcn.shape[2]
        S.dma('pool', WB[i][:, 0:kc, 0:n], src_ap_pcn, IN_b, WB_b[i])
        return WB[i], WB_b[i]

    def wview(W2d, n0, n):
        return W2d[:, n0:n0 + n].rearrange("(c p) n -> p c n", p=128)

    pcount = [0]

    def next_bank(nb=8, base=0):
        i = base + pcount[0] % nb
        pcount[0] += 1
        return i

    S.dma('sp', ident_f[:], ident_d, IN_b, PERS_b)
    S.dma('pool', ident_b[:], ident_d, IN_b, PERS_b)
    S.dma('sp', ngT[:], ngT_d, IN_b, PERS_b)
    S.dma('sp', cw[:], cwT_d, IN_b, PERS_b)
    S.dma('sp', cb[:], cbT_d, IN_b, PERS_b)
    S.dma('sp', condT[:], condT_d.rearrange("(c p) k -> p c k", p=128), IN_b, PERS_b)
    S.dma('sp', adab[:], ada_bT_d.rearrange("l p c -> p l c"), IN_b, PERS_b)
    S.op('dve', lambda h: h.memset(ones_b[:], 1.0), writes=[PERS_b])
    SC_b = Buf("scT", True, True)
    S.op('act', lambda h: h.activation(out=scT[:], in_=condT[:], func=AF.Silu), reads=[PERS_b], writes=[SC_b])

    ADA_PS = 7
    NAS = 7
    ADA_TOP = LIMIT - NAS * 8192
    ADAS = [nc.alloc_sbuf_tensor_at("adas%d" % i, [128, 16, 256], BF16, offset=ADA_TOP + i * 8192) for i in range(NAS)]
    ADAS_b = [Buf("adas%d" % i, True, True) for i in range(NAS)]
    ada_n = [0]

    def ada_block(l, blk):
        i = ada_n[0] % NAS
        ada_n[0] += 1
        S.dma('pool', ADAS[i][:], ada_w_d[l][:, blk * 256:(blk + 1) * 256].rearrange("(c p) n -> p c n", p=128), IN_b, ADAS_b[i])
        for j in range(2):
            ch = blk * 2 + j
            for kc in range(KC):
                S.op('pe', (lambda h, i=i, j=j, kc=kc, ch=ch, l=l: h.matmul(
                    PS[:, ADA_PS * 512 + (l * 96 + ch) * 2: ADA_PS * 512 + (l * 96 + ch) * 2 + 2],
                    ADAS[i][:, kc, j * 128:(j + 1) * 128], scT[:, kc, :], start=(kc == 0), stop=(kc == KC - 1))),
                    reads=[ADAS_b[i], SC_b], writes=[PSB[ADA_PS]], sig=(kc == KC - 1))
        if blk % 8 == 7:
            v = blk // 8
            S.op('dve', (lambda h, l=l, v=v: h.tensor_tensor(
                out=modT[:, l, v * 16:(v + 1) * 16, :],
                in0=PS[:, ADA_PS * 512 + (l * 96 + v * 16) * 2: ADA_PS * 512 + (l * 96 + v * 16 + 16) * 2].rearrange("p (c k) -> p c k", k=2),
                in1=adab[:, l, v * 16:(v + 1) * 16].unsqueeze(2).to_broadcast([128, 16, 2]), op=ALU.add)),
                reads=[PSB[ADA_PS], PERS_b], writes=[MOD_b])
            if v in (1, 4):
                w_ = 0 if v == 1 else 1
                S.op('dve', (lambda h, l=l, w_=w_, v=v: h.scalar_tensor_tensor(
                    out=gs[:, l, w_], in0=modT[:, l, v * 16:(v + 1) * 16, :], scalar=1.0,
                    in1=ngT[:, w_ * 2 + l].unsqueeze(2).to_broadcast([128, 16, 2]), op0=ALU.add, op1=ALU.mult)),
                    reads=[MOD_b, PERS_b], writes=[MOD_b])

    ada_todo = [(0, blk) for blk in range(16, 48)] + [(1, blk) for blk in range(48)]

    def ada_pull(n):
        for _ in range(n):
            if ada_todo:
                l_, blk_ = ada_todo.pop(0)
                ada_block(l_, blk_)

    for blk in range(16):
        ada_block(0, blk)

    def mod_ap(l, v, dc, cond):
        return modT[:, l, v * 16 + dc, cond:cond + 1]

    def cond_of_tile(t512):
        return 0 if t512 < 4 else 1

    if stop_after == 'in':
        S.finish(); S.emit(); st.close(); return nc
    if stop_after == 'ada':
        pass

    def norm_stats(cols, ar, pre=None):
        nr = len(ar['rstd'])
        rstd = ar['rstd'][ar['k'] % nr]
        rstd_b = ar['rstd_b'][ar['k'] % nr]
        if pre is not None:
            xi, xi_b, ntot = pre
            cols = []
        else:
            ntot = sum(n for _, n in cols)
            xi = ar['xi'][ar['k'] % ar['nxi']]
            xi_b = ar['xi_b'][ar['k'] % ar['nxi']]
        ar['k'] += 1
        o = 0
        for c0, n in cols:
            kw = {'allow_slow_non_contiguous': True} if n == 1 else {}
            S.dma('sp', xi[:, :, o:o + n], XT_d[:, :, c0:c0 + n].rearrange("c p t -> p c t"), XT_b, xi_b, **kw)
            o += n
        b = next_bank(4)
        for dc in range(KC):
            sq = ar['sq'][dc % ar['nsq']]
            sq_b = ar['sq_b'][dc % ar['nsq']]
            S.op('act', (lambda h, sq=sq, dc=dc, xi=xi: h.activation(out=sq[:, 0:ntot], in_=xi[:, dc, 0:ntot], func=AF.Square)),
                 reads=[xi_b], writes=[sq_b])
            S.op('pe', (lambda h, b=b, sq=sq, dc=dc: h.matmul(PS[:, b * 512:b * 512 + ntot], ones_b[:], sq[:, 0:ntot],
                                                             start=(dc == 0), stop=(dc == KC - 1))),
                 reads=[sq_b, PERS_b], writes=[PSB[b]], sig=True)
        S.op('dve', (lambda h, b=b: h.tensor_scalar(out=rstd[:, 0:ntot], in0=PS[:, b * 512:b * 512 + ntot], scalar1=1.0 / D, scalar2=EPS,
                                                    op0=ALU.mult, op1=ALU.add)), reads=[PSB[b]], writes=[rstd_b])
        S.op('act', (lambda h: h.activation(out=rstd[:, 0:ntot], in_=rstd[:, 0:ntot], func=AF.Sqrt)),
             reads=[rstd_b], writes=[rstd_b])
        S.op('dve', (lambda h: h.reciprocal(out=rstd[:, 0:ntot], in_=rstd[:, 0:ntot])),
             reads=[rstd_b], writes=[rstd_b])
        return xi, xi_b, rstd, rstd_b

    def norm_mod(l, which, cols, dst_fn, dst_bufs, cond, ar, pre=None):
        ntot = pre[2] if pre is not None else sum(n for _, n in cols)
        xi, xi_b, rstd, rstd_b = norm_stats(cols, ar, pre)
        for dc in range(KC):
            tmp = ar['tmp'][dc % ar['ntmp']]
            tmp_b = ar['tmp_b'][dc % ar['ntmp']]
            S.op('dve', (lambda h, tmp=tmp, dc=dc, xi=xi: h.scalar_tensor_tensor(
                out=tmp[:, 0:ntot], in0=xi[:, dc, 0:ntot], scalar=gs[:, l, which, dc, cond:cond + 1], in1=rstd[:, 0:ntot],
                op0=ALU.mult, op1=ALU.mult)), reads=[xi_b, rstd_b, MOD_b], writes=[tmp_b])
            S.op('act', (lambda h, tmp=tmp, dc=dc: h.activation(
                out=dst_fn(dc), in_=tmp[:, 0:ntot], func=AF.Identity, bias=mod_ap(l, which * 3, dc, cond), scale=1.0)),
                reads=[tmp_b, MOD_b], writes=dst_bufs)

    def norm_arena(ncol, nxi=2, nsq=6, ntmp=4):
        ar = {'k': 0, 'nxi': nxi, 'nsq': nsq, 'ntmp': ntmp}
        ar['xi'] = [ph.alloc("nxi", [128, 16, ncol], F32) for _ in range(nxi)]
        ar['xi_b'] = [Buf("nxi%d" % i, True) for i in range(nxi)]
        ar['sq'] = [ph.alloc("nsq", [128, ncol], BF16) for _ in range(nsq)]
        ar['sq_b'] = [Buf("nsq%d" % i, True) for i in range(nsq)]
        ar['rstd'] = [ph.alloc("nrstd", [128, ncol], F32) for _ in range(max(nxi, 1))]
        ar['rstd_b'] = [Buf("nrstd%d" % i, True) for i in range(max(nxi, 1))]
        ar['tmp'] = [ph.alloc("ntmp", [128, ncol], F32) for _ in range(ntmp)]
        ar['tmp_b'] = [Buf("ntmp%d" % i, True) for i in range(ntmp)]
        return ar

    def in_phase():
        ph.reset(WB_OFF)
        xin = [ph.alloc("xin", [128, D], F32) for _ in range(2)]
        xin_b = [Buf("xin%d" % i, True) for i in range(2)]
        xts = [ph.alloc("xts", [128, 16, 128], F32) for _ in range(3)]
        xts_b = [Buf("xts%d" % i, True) for i in range(3)]
        ar = norm_arena(128, nxi=0, nsq=6, ntmp=3)

        def tpart(tt):
            i = tt % 2
            x3 = tt % 3
            src = xs_d[tt * 128:(tt + 1) * 128, :] if tt < 16 else xp_d[(tt - 16) * 128:(tt - 15) * 128, :]
            S.dma('sp', xin[i][:], src, IN_b, xin_b[i])
            for q4 in range(4):
                b = next_bank(4)
                for j in range(4):
                    dc = q4 * 4 + j
                    S.op('pe', (lambda h, b=b, j=j, dc=dc, i=i: h.transpose(
                        PS[:, b * 512 + j * 128: b * 512 + (j + 1) * 128], xin[i][:, dc * 128:(dc + 1) * 128], ident_f[:])),
                        reads=[xin_b[i], PERS_b], writes=[PSB[b]], sig=(j == 3))
                if q4 % 2 == 0:
                    S.op('act', (lambda h, b=b, q4=q4, x3=x3: h.activation(
                        out=xts[x3][:, q4 * 4:(q4 + 1) * 4, :], in_=bank(b).rearrange("p (c t) -> p c t", t=128), func=AF.Copy)),
                        reads=[PSB[b]], writes=[xts_b[x3]])
                else:
                    S.op('dve', (lambda h, b=b, q4=q4, x3=x3: h.tensor_copy(
                        out=xts[x3][:, q4 * 4:(q4 + 1) * 4, :], in_=bank(b).rearrange("p (c t) -> p c t", t=128))),
                        reads=[PSB[b]], writes=[xts_b[x3]])
            S.dma('sp', XT_d[:, :, tt * 128:(tt + 1) * 128].rearrange("c p t -> p c t"), xts[x3][:], xts_b[x3], XT_b)

        def npart(tt):
            x3 = tt % 3
            norm_mod(0, 0, None, (lambda dc, tt=tt: AT[:, dc, tt * 128:(tt + 1) * 128]), [AT_b[tt // 4]], (0 if tt < 16 else 1), ar,
                     pre=(xts[x3], xts_b[x3], 128))

        for tt in range(25):
            if tt < 24:
                tpart(tt)
            if tt >= 1:
                npart(tt - 1)
        S.barrier()
        S.recycle()

    def norm1_phase(l):
        ph.reset(WB_OFF)
        lim = ph.limit
        ph.limit = LIMIT
        ar = norm_arena(512, nxi=2, nsq=6, ntmp=4)
        ph.limit = lim
        for t5 in range(6):
            norm_mod(l, 0, [(t5 * 512, 512)], (lambda dc, t5=t5: AT[:, dc, t5 * 512:(t5 + 1) * 512]), [AT_b[t5]], cond_of_tile(t5), ar)
        S.barrier()
        S.recycle()

    def mm_fm(wt, wb, j, t5, b, kcn=KC, src=None, src_b=None):
        for kc in range(kcn):
            S.op('pe', (lambda h, wt=wt, j=j, kc=kc, t5=t5, b=b: h.matmul(
                bank(b), wt[:, kc, j * 128:(j + 1) * 128], AT[:, kc, t5 * 512:(t5 + 1) * 512],
                start=(kc == 0), stop=(kc == kcn - 1))), reads=[wb, AT_b[t5]], writes=[PSB[b]], sig=(kc == kcn - 1))

    def mm_tm(wt, wb, tt, b, ncols=512):
        for kc in range(KC):
            S.op('pe', (lambda h, wt=wt, kc=kc, tt=tt, b=b: h.matmul(
                PS[:, b * 512:b * 512 + ncols], AT[:, kc, tt * 128:(tt + 1) * 128], wt[:, kc, 0:ncols],
                start=(kc == 0), stop=(kc == KC - 1))), reads=[wb, AT_b[tt // 4]], writes=[PSB[b]], sig=(kc == KC - 1))

    def g1_even():
        ph.reset()
        sgb = [ph.alloc("sgb", [128, 512], BF16) for _ in range(3)]
        sgb_b = [Buf("sgb%d" % i, True) for i in range(3)]
        sgf = [ph.alloc("sgf", [128, 512], F32) for _ in range(3)]
        sgf_b = [Buf("sgf%d" % i, True) for i in range(3)]
        cb_ = [0]
        cf_ = [0]

        def nb():
            i = cb_[0] % 3
            cb_[0] += 1
            return sgb[i], sgb_b[i]

        def nf():
            i = cf_[0] % 3
            cf_[0] += 1
            return sgf[i], sgf_b[i]

        wq = {0: load_w(wview(ev_w_in_d, 0, 512))}
        for blk in range(10):
            wt, wb = wq[blk]
            if blk + 1 < 10:
                wq[blk + 1] = load_w(wview(ev_w_in_d, (blk + 1) * 512, 512))
            kind = blk // 3
            if kind in (0, 1, 3):
                for j in range(4):
                    ch = (blk % 3) * 4 + j if kind < 3 else j
                    for t5 in range(6):
                        b = next_bank(6)
                        mm_fm(wt, wb, j, t5, b)
                        if kind == 3:
                            sg, sg_b = nf()
                            S.op('act', (lambda h, sg=sg, b=b: h.activation(out=sg[:], in_=bank(b), func=AF.Copy)),
                                 reads=[PSB[b]], writes=[sg_b])
                            S.dma('sp', XB_d[ch, :, t5 * 512:(t5 + 1) * 512], sg[:], sg_b, XB_b)
                        else:
                            sg, sg_b = nb()
                            eng = 'act' if (t5 % 2 == 0) else 'dve'
                            if eng == 'act':
                                S.op('act', (lambda h, sg=sg, b=b: h.activation(out=sg[:], in_=bank(b), func=AF.Copy)),
                                     reads=[PSB[b]], writes=[sg_b])
                            else:
                                S.op('dve', (lambda h, sg=sg, b=b: h.tensor_copy(out=sg[:], in_=bank(b))),
                                     reads=[PSB[b]], writes=[sg_b])
                            dst, dst_b = (QT_d, QT_b) if kind == 0 else (KT_d, KT_b)
                            S.dma('sp', dst[ch, :, t5 * 512:(t5 + 1) * 512], sg[:], sg_b, dst_b)
                if kind == 1:
                    for tt in range(16, 24):
                        b = next_bank(6)
                        mm_tm(wt, wb, tt, b)
                        sg, sg_b = nf()
                        S.op('act', (lambda h, sg=sg, b=b: h.activation(out=sg[:], in_=bank(b), func=AF.Copy)),
                             reads=[PSB[b]], writes=[sg_b])
                        S.dma('sp', nak_d[(tt - 16) * 128:(tt - 15) * 128, (blk - 3) * 512:(blk - 2) * 512], sg[:], sg_b, OUT_b)
            else:
                for tt in range(24):
                    b = next_bank(6)
                    mm_tm(wt, wb, tt, b)
                    sg, sg_b = nb()
                    if tt < 16:
                        S.op('dve', (lambda h, sg=sg, b=b: h.tensor_copy(out=sg[:], in_=bank(b))), reads=[PSB[b]], writes=[sg_b])
                    else:
                        sf, sf_b = nf()
                        S.op('act', (lambda h, sf=sf, b=b: h.activation(out=sf[:], in_=bank(b), func=AF.Copy)),
                             reads=[PSB[b]], writes=[sf_b])
                        S.op('dve', (lambda h, sg=sg, sf=sf: h.tensor_copy(out=sg[:], in_=sf[:])), reads=[sf_b], writes=[sg_b])
                        S.dma('sp', nav_d[(tt - 16) * 128:(tt - 15) * 128, (blk - 6) * 512:(blk - 5) * 512], sf[:], sf_b, OUT_b)
                    S.dma('sp', V_d[tt * 128:(tt + 1) * 128, (blk - 6) * 512:(blk - 5) * 512], sg[:], sg_b, V_b)
            ada_pull(8)
        ada_pull(1000)
        ph.limit = LIMIT
        S.barrier()
        S.recycle()

    PSbf = PS[:, 6 * 512:7 * 512].bitcast(BF16)
    acnt = [0]

    def pipeline2(stages):
        prev = None
        for s1, s2 in stages:
            s1()
            if prev is not None:
                prev()
            prev = s2
        if prev is not None:
            prev()

    def block_attn_prompt(hq, kch, vcol0, out_chunk, ar, stages):
        i = ar['k'] % 2
        ar['k'] += 1
        qp, kp, vp = ar['qp'][i], ar['kp'][i], ar['vp'][i]
        qp_b, kp_b, vp_b = ar['qp_b'][i], ar['kp_b'][i], ar['vp_b'][i]

        def loads():
            S.dma('sp', qp[:], QT_d[hq, :, TS:T], QT_b, qp_b)
            S.dma('sp', kp[:], KT_d[kch, :, TS:T], KT_b, kp_b)
            S.dma('sp', vp[:], V_d[TS:T, vcol0:vcol0 + 128].rearrange("(c p) d -> p c d", p=128), V_b, vp_b)

        for s in range(4):
            c = acnt[0]
            acnt[0] += 1
            sb = c % 4
            ob = 4 + c % 2
            pt = ar['pt'][c % 2]
            pt_b = ar['pt_b'][c % 2]
            rec = ar['rec'][c % 2]
            rec_b = ar['rec_b'][c % 2]

            def st1(s=s, sb=sb, pt=pt, pt_b=pt_b):
                if s == 0:
                    loads()
                for kc in range(2):
                    S.op('pe', (lambda h, kc=kc: h.matmul(
                        PS[:, sb * 512 + kc * 256: sb * 512 + (kc + 1) * 256], kp[:, s * 256 + kc * 128: s * 256 + (kc + 1) * 128],
                        qp[:, s * 256:(s + 1) * 256], start=True, stop=True)), reads=[kp_b, qp_b], writes=[PSB[sb]], sig=(kc == 1))
                S.op('act', (lambda h: h.activation(out=pt[:, 0:512], in_=bank(sb), func=AF.Exp, scale=ISQ128)),
                     reads=[PSB[sb]], writes=[pt_b])

            def st2(s=s, ob=ob, pt=pt, pt_b=pt_b, rec=rec, rec_b=rec_b):
                for kc in range(2):
                    S.op('pe', (lambda h, kc=kc: h.matmul(
                        PS[:, ob * 512: ob * 512 + 256], vp[:, s * 2 + kc, :], pt[:, kc * 256:(kc + 1) * 256],
                        start=(kc == 0), stop=(kc == 1))), reads=[vp_b, pt_b], writes=[PSB[ob]], sig=False)
                for kc in range(2):
                    S.op('pe', (lambda h, kc=kc: h.matmul(
                        PS[:, ob * 512 + 256: ob * 512 + 512], ones_b[:], pt[:, kc * 256:(kc + 1) * 256],
                        start=(kc == 0), stop=(kc == 1))), reads=[pt_b, PERS_b], writes=[PSB[ob]], sig=(kc == 1))
                S.op('dve', (lambda h: h.reciprocal(out=rec[:, 0:256], in_=PS[:, ob * 512 + 256: ob * 512 + 512])),
                     reads=[PSB[ob]], writes=[rec_b])
                S.op('dve', (lambda h: h.tensor_tensor(
                    out=AT[:, out_chunk, TS + s * 256: TS + (s + 1) * 256], in0=PS[:, ob * 512: ob * 512 + 256], in1=rec[:, 0:256], op=ALU.mult)),
                    reads=[PSB[ob], rec_b], writes=[AT_b[4 + s // 2]])

            stages.append((st1, st2))

    def attn_arena():
        ar = {'k': 0}
        ar['qp'] = [ph.alloc("qp", [128, TP], BF16) for _ in range(2)]
        ar['kp'] = [ph.alloc("kp", [128, TP], BF16) for _ in range(2)]
        ar['vp'] = [ph.alloc("vp", [128, 8, 128], BF16) for _ in range(2)]
        for n in ('qp', 'kp', 'vp'):
            ar[n + '_b'] = [Buf(n + "%d" % i, True) for i in range(2)]
        ar['pt'] = [ph.alloc("pt", [128, 1024], BF16) for _ in range(2)]
        ar['pt_b'] = [Buf("pt%d" % i, True) for i in range(2)]
        ar['rec'] = [ph.alloc("rec", [128, 512], F32) for _ in range(2)]
        ar['rec_b'] = [Buf("rec%d" % i, True) for i in range(2)]
        return ar

    def na_chunks(j):
        r0s = [min(max(r - 4, 0), 24) for r in (2 * j, 2 * j + 1)]
        return list(range(min(r0s) // 2, (max(r0s) + 7) // 2 + 1))

    def na_type(j, m):
        if 2 <= j <= 13:
            return (m - j) + 2
        base = {0: 5, 1: 9, 14: 13, 15: 17}[j]
        return base + (m - (0 if j < 2 else 12))

    def attn_even():
        ph.reset(WB_OFF)
        ar = attn_arena()
        identS = ph.alloc("identS", [128, 128], BF16)
        IDS_b = Buf("identS", True)
        S.op('act', (lambda h: h.activation(out=identS[:], in_=ident_f[:], func=AF.Copy, scale=SQ128)), reads=[PERS_b], writes=[IDS_b])
        kctok = ph.alloc("kctok", [128, 2, 1536], BF16)
        vc = ph.alloc("vc", [128, 2, 1536], BF16)
        kcT = ph.alloc("kcT", [128, 12, 256], BF16)
        KTOK_b, VC_b, KCT_b = Buf("kctok", True), Buf("vc", True), Buf("kcT", True)
        S.dma('pool', kctok[:], cnk_d.rearrange("(c p) d -> p c d", p=128), IN_b, KTOK_b)
        S.dma('pool', vc[:], cnv_d.rearrange("(c p) d -> p c d", p=128), IN_b, VC_b)
        for hh in range(12):
            for c in range(2):
                S.op('pe', (lambda h, hh=hh, c=c: h.transpose(PSbf[:, c * 128:(c + 1) * 128], kctok[:, c, hh * 128:(hh + 1) * 128], ident_b[:])),
                     reads=[KTOK_b, PERS_b], writes=[PSB[6]], sig=(c == 1))
            S.op('dve', (lambda h, hh=hh: h.tensor_copy(out=kcT[:, hh, :], in_=PSbf[:, 0:256])), reads=[PSB[6]], writes=[KCT_b])
        qth = [ph.alloc("qth", [128, TS], BF16) for _ in range(2)]
        kth = [ph.alloc("kth", [128, TS], BF16) for _ in range(2)]
        vh = [ph.alloc("vh", [128, 16, 128], BF16) for _ in range(2)]
        bth = [ph.alloc("bth", [128, 21, 128], BF16) for _ in range(2)]
        qth_b = [Buf("qth%d" % i, True) for i in range(2)]
        kth_b = [Buf("kth%d" % i, True) for i in range(2)]
        vh_b = [Buf("vh%d" % i, True) for i in range(2)]
        bth_b = [Buf("bth%d" % i, True) for i in range(2)]
        cnt = 0
        stages = []
        for hh in range(12):
            i = hh % 2

            def head_loads(hh=hh, i=i):
                S.dma('sp', qth[i][:], QT_d[hh, :, 0:TS], QT_b, qth_b[i])
                S.dma('sp', kth[i][:], KT_d[hh, :, 0:TS], KT_b, kth_b[i])
                S.dma('sp', vh[i][:], V_d[0:TS, hh * 128:(hh + 1) * 128].rearrange("(c p) d -> p c d", p=128), V_b, vh_b[i])
                S.dma('pool', bth[i][:], bt_d[hh].rearrange("t k q -> k t q"), IN_b, bth_b[i])

            for j in range(16):
                ms = na_chunks(j)
                nl = len(ms)
                ncol = (nl + 2) * 128
                sb = cnt % 2
                ob = 4 + cnt % 2
                pt = ar['pt'][cnt % 2]
                pt_b = ar['pt_b'][cnt % 2]
                rec = ar['rec'][cnt % 2]
                rec_b = ar['rec_b'][cnt % 2]
                cnt += 1
                sbufs = [PSB[2 * sb], PSB[2 * sb + 1]]
                base = sb * 1024

                def st1(hh=hh, i=i, j=j, ms=ms, nl=nl, ncol=ncol, pt=pt, pt_b=pt_b, sbufs=sbufs, base=base, head_loads=head_loads):
                    if j == 0:
                        head_loads()
                    for idx, m in enumerate(ms):
                        ty = na_type(j, m)
                        S.op('pe', (lambda h, idx=idx, m=m: h.matmul(
                            PS[:, base + idx * 128: base + (idx + 1) * 128], kth[i][:, m * 128:(m + 1) * 128], qth[i][:, j * 128:(j + 1) * 128],
                            start=True, stop=False)), reads=[kth_b[i], qth_b[i]], writes=sbufs, sig=False)
                        S.op('pe', (lambda h, idx=idx, ty=ty: h.matmul(
                            PS[:, base + idx * 128: base + (idx + 1) * 128], identS[:], bth[i][:, ty, :],
                            start=False, stop=True)), reads=[bth_b[i], IDS_b], writes=sbufs, sig=False)
                    for c in range(2):
                        idx = nl + c
                        S.op('pe', (lambda h, idx=idx, c=c: h.matmul(
                            PS[:, base + idx * 128: base + (idx + 1) * 128], kcT[:, hh, c * 128:(c + 1) * 128], qth[i][:, j * 128:(j + 1) * 128],
                            start=True, stop=True)), reads=[KCT_b, qth_b[i]], writes=sbufs, sig=(c == 1))
                    S.op('act', (lambda h: h.activation(out=pt[:, 0:ncol], in_=PS[:, base:base + ncol], func=AF.Exp, scale=ISQ128)),
                         reads=sbufs, writes=[pt_b])

                def st2(hh=hh, i=i, j=j, ms=ms, nl=nl, ob=ob, pt=pt, pt_b=pt_b, rec=rec, rec_b=rec_b):
                    for idx in range(nl + 2):
                        if idx < nl:
                            lhs = (lambda m=ms[idx]: vh[i][:, m, :])
                            rb = vh_b[i]
                        else:
                            lhs = (lambda c=idx - nl: vc[:, c, hh * 128:(hh + 1) * 128])
                            rb = VC_b
                        S.op('pe', (lambda h, lhs=lhs, idx=idx: h.matmul(
                            PS[:, ob * 512: ob * 512 + 128], lhs(), pt[:, idx * 128:(idx + 1) * 128], start=(idx == 0), stop=(idx == nl + 1))),
                            reads=[rb, pt_b], writes=[PSB[ob]], sig=False)
                    for idx in range(nl + 2):
                        S.op('pe', (lambda h, idx=idx: h.matmul(
                            PS[:, ob * 512 + 128: ob * 512 + 256], ones_b[:], pt[:, idx * 128:(idx + 1) * 128], start=(idx == 0), stop=(idx == nl + 1))),
                            reads=[pt_b, PERS_b], writes=[PSB[ob]], sig=(idx == nl + 1))
                    S.op('dve', (lambda h: h.reciprocal(out=rec[:, 0:128], in_=PS[:, ob * 512 + 128: ob * 512 + 256])),
                         reads=[PSB[ob]], writes=[rec_b])
                    S.op('dve', (lambda h: h.tensor_tensor(
                        out=AT[:, hh, j * 128:(j + 1) * 128], in0=PS[:, ob * 512: ob * 512 + 128], in1=rec[:, 0:128], op=ALU.mult)),
                        reads=[PSB[ob], rec_b], writes=[AT_b[j // 4]])

                stages.append((st1, st2))
        pipeline2(stages)
        ada_pull(1000)
        ph.limit = LIMIT
        stages = []
        for hh in range(12):
            block_attn_prompt(hh, hh, hh * 128, hh, ar, stages)
        pipeline2(stages)
        S.barrier()
        S.recycle()

    def fnet_phase():
        for gh in range(2):
            fnet_half(gh)

    def fnet_half(gh):
        ph.reset()
        xbt = ph.alloc("xbt", [128, 2, T], BF16)
        XBT_b = Buf("xbt", True)
        S.dma('pool', xbt[:], XB_d[gh * 2:gh * 2 + 2].rearrange("g p t -> p g t"), XB_b, XBT_b)
        cc = ph.alloc("cc", [128, 256], BF16)
        fw = ph.alloc("fw", [128, 4, 128], BF16)
        c256 = ph.alloc("c256", [128, 2, 256], BF16)
        s256 = ph.alloc("s256", [128, 2, 256], BF16)
        FC_b = Buf("fconst", True)
        S.dma('pool', cc[:], cc_d, IN_b, FC_b)
        S.dma('pool', fw[:], fw_d.rearrange("g c d -> c g d"), IN_b, FC_b)
        S.dma('pool', c256[:], cn256_d.rearrange("(c p) k -> p c k", p=128), IN_b, FC_b)
        S.dma('pool', s256[:], sn256_d.rearrange("(c p) k -> p c k", p=128), IN_b, FC_b)
        U = ph.alloc("U", [128, 24, 2, 256], BF16)
        U_b = Buf("U", True)
        yb = [ph.alloc("yb", [128, 512], BF16) for _ in range(2)]
        yb_b = [Buf("yb%d" % i, True) for i in range(2)]
        WQ = [nc.alloc_sbuf_tensor_at("wq%d_%d" % (gh, i), [128, 16, 256], BF16, offset=WB_OFF + i * 8192) for i in range(4)]
        WQ_b = [Buf("wq%d" % i, True) for i in range(4)]
        for tt in range(24):
            for g2 in range(1):
                b = next_bank(6)
                for gg in range(2):
                    g = gg
                    S.op('pe', (lambda h, b=b, gg=gg, g=g, tt=tt: h.matmul(
                        PS[:, b * 512 + gg * 256: b * 512 + (gg + 1) * 256], xbt[:, g, tt * 128:(tt + 1) * 128], cc[:],
                        start=True, stop=True)), reads=[XBT_b, FC_b], writes=[PSB[b]], sig=(gg == 1))
                if tt % 2 == 0:
                    S.op('act', (lambda h, b=b, tt=tt, g2=g2: h.activation(
                        out=U[:, tt, g2 * 2:(g2 + 1) * 2, :], in_=bank(b).rearrange("p (g k) -> p g k", k=256), func=AF.Copy)),
                        reads=[PSB[b]], writes=[U_b])
                else:
                    S.op('dve', (lambda h, b=b, tt=tt, g2=g2: h.tensor_copy(
                        out=U[:, tt, g2 * 2:(g2 + 1) * 2, :], in_=bank(b).rearrange("p (g k) -> p g k", k=256))),
                        reads=[PSB[b]], writes=[U_b])

        def step23(g, mm_list, ncols, dst_ap, dst_b):
            b = next_bank(6)
            n = len(mm_list)
            for q, (lhs, rhs, rb) in enumerate(mm_list):
                S.op('pe', (lambda h, b=b, lhs=lhs, rhs=rhs, q=q, n=n: h.matmul(
                    PS[:, b * 512: b * 512 + ncols], lhs(), rhs(), start=(q == 0), stop=(q == n - 1))),
                    reads=[U_b, rb], writes=[PSB[b]], sig=(q == n - 1))
            k = b % 2
            S.op('act', (lambda h, b=b, k=k: h.activation(out=yb[k][:, 0:ncols], in_=PS[:, b * 512: b * 512 + ncols], func=AF.Copy)),
                 reads=[PSB[b]], writes=[yb_b[k]])
            b2 = next_bank(6)
            S.op('pe', (lambda h, b2=b2, k=k, g=g: h.matmul(PS[:, b2 * 512: b2 * 512 + ncols], fw[:, gh * 2 + g, :], yb[k][:, 0:ncols], start=True, stop=True)),
                 reads=[yb_b[k], FC_b], writes=[PSB[b2]], sig=True)
            S.op('dve', (lambda h, b2=b2: h.tensor_copy(out=dst_ap, in_=PS[:, b2 * 512: b2 * 512 + ncols])), reads=[PSB[b2]], writes=[dst_b])

        for kt in range(8):
            q0 = (kt % 2) * 2
            cn, cn_b = WQ[q0], WQ_b[q0]
            sn, sn_b = WQ[q0 + 1], WQ_b[q0 + 1]
            S.dma('pool', cn[:], cn2k_d[:, kt * 256:(kt + 1) * 256].rearrange("(c p) n -> p c n", p=128), IN_b, cn_b)
            S.dma('pool', sn[:], sn2k_d[:, kt * 256:(kt + 1) * 256].rearrange("(c p) n -> p c n", p=128), IN_b, sn_b)
            for g in range(2):
                mm = []
                for n_ in range(16):
                    mm.append(((lambda n_=n_, g=g: U[:, n_, g, 0:128]), (lambda n_=n_, cn=cn: cn[:, n_, :]), cn_b))
                    mm.append(((lambda n_=n_, g=g: U[:, n_, g, 128:256]), (lambda n_=n_, sn=sn: sn[:, n_, :]), sn_b))
                step23(g, mm, 256, AT[:, 12 + gh * 2 + g, kt * 256:(kt + 1) * 256], AT_b[kt // 2])
        for s in range(4):
            for g in range(2):
                mm = []
                for n_ in range(2):
                    mm.append(((lambda n_=n_, g=g, s=s: U[:, 16 + s * 2 + n_, g, 0:128]), (lambda n_=n_: c256[:, n_, :]), FC_b))
                    mm.append(((lambda n_=n_, g=g, s=s: U[:, 16 + s * 2 + n_, g, 128:256]), (lambda n_=n_: s256[:, n_, :]), FC_b))
                step23(g, mm, 256, AT[:, 12 + gh * 2 + g, TS + s * 256: TS + (s + 1) * 256], AT_b[4 + s // 2])
        S.barrier()
        S.recycle()

    def g2_phase(l, w_out_d):
        ph.reset()
        NS = 4
        xi = [ph.alloc("g2xi", [128, 512], F32) for _ in range(NS)]
        xi_b = [Buf("g2xi%d" % i, True) for i in range(NS)]
        xo = [ph.alloc("g2xo", [128, 512], F32) for _ in range(NS)]
        xo_b = [Buf("g2xo%d" % i, True) for i in range(NS)]
        its = [(blk, j, t5) for blk in range(4) for j in range(4) for t5 in range(6)]

        def load(c):
            blk, j, t5 = its[c]
            S.dma('sp', xi[c % NS][:], XT_d[blk * 4 + j, :, t5 * 512:(t5 + 1) * 512], XT_b, xi_b[c % NS])

        load(0)
        load(1)
        wt = wb = None
        for c, (blk, j, t5) in enumerate(its):
            if j == 0 and t5 == 0:
                wt, wb = load_w(wview(w_out_d, blk * 512, 512))
            if c + 2 < len(its):
                load(c + 2)
            k = c % NS
            dc = blk * 4 + j
            b = next_bank(6)
            mm_fm(wt, wb, j, t5, b)
            cond = cond_of_tile(t5)
            S.op('dve', (lambda h, k=k, b=b, dc=dc, cond=cond: h.scalar_tensor_tensor(
                out=xo[k][:], in0=bank(b), scalar=mod_ap(l, 2, dc, cond), in1=xi[k][:], op0=ALU.mult, op1=ALU.add)),
                reads=[PSB[b], xi_b[k], MOD_b], writes=[xo_b[k]])
            S.dma('sp', XT_d[dc, :, t5 * 512:(t5 + 1) * 512], xo[k][:], xo_b[k], XT_b)
        S.barrier()
        S.recycle()

    FFN_TILES = [
        (0, 0, [(0, 1024)], False, True),
        (1024, 0, [(0, 1024)], True, False),
        (2048, 1, [(0, 256), (256, 512), (512, 768), (768, 1024)], False, False),
    ]

    FFN_ST = {}

    def ffn_static():
        if FFN_ST:
            return FFN_ST
        FB = WB_OFF
        st_ = FFN_ST
        st_['GT'] = nc.alloc_sbuf_tensor_at("GT", [128, 44, 1024], BF16, offset=BASE + 12288)
        st_['GT_b'] = [Buf("GT%d" % i, True) for i in range(2)]
        st_['H2T'] = nc.alloc_sbuf_tensor_at("H2T", [128, 16, 1026], BF16, offset=FB)
        st_['H2T_b'] = Buf("H2T", True)
        st_['wup'] = [nc.alloc_sbuf_tensor_at("wup%d" % k, [128, 16, 256], BF16, offset=FB + 32832 + k * 8192) for k in range(4)]
        st_['wup_b'] = [Buf("wup%d" % k, True, True) for k in range(4)]
        D0 = FB + 65600
        st_['acc'] = [nc.alloc_sbuf_tensor_at("acc%d" % k, [128, 1024], F32, offset=D0 + k * 4096) for k in range(4)]
        st_['acc_b'] = [Buf("acc%d" % k, True) for k in range(4)]
        st_['hal'] = [nc.alloc_sbuf_tensor_at("hal%d" % k, [128, 2], F32, offset=D0 + 16384 + k * 32) for k in range(2)]
        st_['hal_b'] = [Buf("hal%d" % k, True) for k in range(2)]
        st_['wdn'] = [nc.alloc_sbuf_tensor_at("wdn%d" % k, [128, 44, 256], BF16, offset=FB + 32832 + k * 22528) for k in range(2)]
        st_['wdn_b'] = [Buf("wdn%d" % k, True, True) for k in range(2)]
        N0 = FB + 77888
        st_['nxi'] = nc.alloc_sbuf_tensor_at("fnxi", [128, 16, 200], F32, offset=N0)
        st_['nxi_b'] = Buf("fnxi", True, True)
        st_['nsq'] = nc.alloc_sbuf_tensor_at("fnsq", [128, 16, 200], BF16, offset=N0 + 12800)
        st_['nsq_b'] = Buf("fnsq", True)
        st_['nrstd'] = nc.alloc_sbuf_tensor_at("fnrstd", [128, 200], F32, offset=N0 + 19200)
        st_['nrstd_b'] = Buf("fnrstd", True)
        st_['ntmp'] = [nc.alloc_sbuf_tensor_at("fntmp%d" % k, [128, 200], F32, offset=N0 + 20000 + k * 800) for k in range(3)]
        st_['ntmp_b'] = [Buf("fntmp%d" % k, True) for k in range(3)]
        assert N0 + 22400 <= LIMIT
        M0 = FB + 32832
        st_['nset'] = [
            dict(xi=st_['nxi'], xi_b=st_['nxi_b'], sq=st_['nsq'], sq_b=st_['nsq_b'], rstd=st_['nrstd'], rstd_b=st_['nrstd_b'],
                 tmp=st_['ntmp'], tmp_b=st_['ntmp_b']),
            dict(xi=nc.alloc_sbuf_tensor_at("fnxi2", [128, 16, 200], F32, offset=M0), xi_b=Buf("fnxi2", True, True),
                 sq=nc.alloc_sbuf_tensor_at("fnsq2", [128, 16, 200], BF16, offset=M0 + 12800), sq_b=Buf("fnsq2", True),
                 rstd=nc.alloc_sbuf_tensor_at("fnrstd2", [128, 200], F32, offset=M0 + 19200), rstd_b=Buf("fnrstd2", True),
                 tmp=[nc.alloc_sbuf_tensor_at("fntmp2_%d" % k, [128, 200], F32, offset=M0 + 20000 + k * 800) for k in range(3)],
                 tmp_b=[Buf("fntmp2_%d" % k, True) for k in range(3)]),
        ]
        X0 = BASE + 12288 + 90112
        st_['xi'] = [nc.alloc_sbuf_tensor_at("fxi%d" % k, [128, 512], F32, offset=X0 + k * 2048) for k in range(2)]
        st_['xi_b'] = [Buf("fxi%d" % k, True, True) for k in range(2)]
        st_['xo'] = [nc.alloc_sbuf_tensor_at("fxo%d" % k, [128, 512], F32, offset=X0 + 4096 + k * 2048) for k in range(2)]
        st_['xo_b'] = [Buf("fxo%d" % k, True, True) for k in range(2)]
        return st_

    def ffn_phase(l):
        F = ffn_static()
        GT, GT_b, H2T, H2T_b = F['GT'], F['GT_b'], F['H2T'], F['H2T_b']
        wup, wup_b, acc, acc_b, hal, hal_b = F['wup'], F['wup_b'], F['acc'], F['acc_b'], F['hal'], F['hal_b']
        wdn, wdn_b, xi, xi_b, xo, xo_b = F['wdn'], F['wdn_b'], F['xi'], F['xi_b'], F['xo'], F['xo_b']
        nxi, nxi_b, nsq, nsq_b, nrstd, nrstd_b, ntmp, ntmp_b = (F[k] for k in ('nxi', 'nxi_b', 'nsq', 'nsq_b', 'nrstd', 'nrstd_b', 'ntmp', 'ntmp_b'))

        NS = F['nset']

        def pieces(t0):
            out = []
            c0 = t0
            while c0 < t0 + 1024:
                n = min(200, t0 + 1024 - c0)
                out.append((c0, n, (lambda dc, hc=2 + c0 - t0, n=n: H2T[:, dc, hc:hc + n]), H2T_b))
                c0 += n
            return out

        def norm_p1(pc, ns):
            c0, n, dst_fn, dst_b = pc
            S.dma('sp', ns['xi'][:, :, 0:n], XT_d[:, :, c0:c0 + n].rearrange("c p t -> p c t"), XT_b, ns['xi_b'])
            for dc in range(KC):
                S.op('act', (lambda h, dc=dc, n=n, ns=ns: h.activation(out=ns['sq'][:, dc, 0:n], in_=ns['xi'][:, dc, 0:n], func=AF.Square)),
                     reads=[ns['xi_b']], writes=[ns['sq_b']])

        def norm_p2(pc, cond, ns):
            c0, n, dst_fn, dst_b = pc
            b = next_bank(6)
            rstd, rstd_b = ns['rstd'], ns['rstd_b']
            for dc in range(KC):
                S.op('pe', (lambda h, b=b, dc=dc, n=n, ns=ns: h.matmul(PS[:, b * 512:b * 512 + n], ones_b[:], ns['sq'][:, dc, 0:n],
                                                                       start=(dc == 0), stop=(dc == KC - 1))),
                     reads=[ns['sq_b'], PERS_b], writes=[PSB[b]], sig=(dc == KC - 1))
            S.op('dve', (lambda h, b=b, n=n: h.tensor_scalar(out=rstd[:, 0:n], in0=PS[:, b * 512:b * 512 + n], scalar1=1.0 / D, scalar2=EPS,
                                                             op0=ALU.mult, op1=ALU.add)), reads=[PSB[b]], writes=[rstd_b])
            S.op('act', (lambda h, n=n: h.activation(out=rstd[:, 0:n], in_=rstd[:, 0:n], func=AF.Sqrt)), reads=[rstd_b], writes=[rstd_b])
            S.op('dve', (lambda h, n=n: h.reciprocal(out=rstd[:, 0:n], in_=rstd[:, 0:n])), reads=[rstd_b], writes=[rstd_b])
            for dc in range(KC):
                tmp, tmp_b = ns['tmp'][dc % 3], ns['tmp_b'][dc % 3]
                S.op('dve', (lambda h, tmp=tmp, dc=dc, n=n, cond=cond, ns=ns: h.scalar_tensor_tensor(
                    out=tmp[:, 0:n], in0=ns['xi'][:, dc, 0:n], scalar=gs[:, l, 1, dc, cond:cond + 1], in1=rstd[:, 0:n],
                    op0=ALU.mult, op1=ALU.mult)), reads=[ns['xi_b'], rstd_b, MOD_b], writes=[tmp_b])
                S.op('act', (lambda h, tmp=tmp, dc=dc, n=n, cond=cond: h.activation(
                    out=dst_fn(dc), in_=tmp[:, 0:n], func=AF.Identity, bias=mod_ap(l, 3, dc, cond), scale=1.0)),
                    reads=[tmp_b, MOD_b], writes=[dst_b])

        def up_proj(t0, cond, segs, lh, rh):
            ucnt = 0
            wcnt = 0
            for pr in range(22):
                slots = []
                for half in range(2):
                    k = wcnt % 4
                    wcnt += 1
                    col0 = half * DFF + pr * 256
                    S.dma('pool', wup[k][:], w_up_d[l][:, col0:col0 + 256].rearrange("(c p) n -> p c n", p=128), IN_b, wup_b[k])
                    slots.append(k)
                for cc_ in range(2):
                    fc = pr * 2 + cc_
                    accs = []
                    for half in range(2):
                        k = slots[half]
                        u = ucnt % 3
                        hs = ucnt % 4
                        ucnt += 1
                        a = acc[(fc % 2) * 2 + half]
                        a_b = acc_b[(fc % 2) * 2 + half]
                        ubufs = [PSB[2 * u], PSB[2 * u + 1]]
                        fidx = half * 44 + fc
                        for hf in range(2):
                            for kc in range(KC):
                                S.op('pe', (lambda h, u=u, hf=hf, kc=kc, k=k, cc_=cc_: h.matmul(
                                    PS[:, u * 1024 + hf * 512: u * 1024 + (hf + 1) * 512], wup[k][:, kc, cc_ * 128:(cc_ + 1) * 128],
                                    H2T[:, kc, 2 + hf * 512: 2 + (hf + 1) * 512], start=(kc == 0), stop=(kc == KC - 1))),
                                    reads=[wup_b[k], H2T_b], writes=[ubufs[hf]], sig=(kc == KC - 1))
                        if lh or rh:
                            for kc in range(KC):
                                S.op('pe', (lambda h, hs=hs, kc=kc, k=k, cc_=cc_: h.matmul(
                                    PS[:, 6 * 512 + hs * 2: 6 * 512 + hs * 2 + 2], wup[k][:, kc, cc_ * 128:(cc_ + 1) * 128],
                                    hsave[:, kc, 0:2], start=(kc == 0), stop=(kc == KC - 1))),
                                    reads=[wup_b[k], HS_b], writes=[PSB[6]], sig=(kc == KC - 1))
                        S.op('act', (lambda h, a=a, u=u, fidx=fidx: h.activation(
                            out=a[:], in_=PS[:, u * 1024:(u + 1) * 1024], func=AF.Identity, scale=cw[:, l, fidx, 1:2], bias=cb[:, l, fidx:fidx + 1])),
                            reads=ubufs + [PERS_b], writes=[a_b])
                        for (sa, sbb) in segs:
                            S.op('dve', (lambda h, a=a, u=u, sa=sa, sbb=sbb, fidx=fidx: h.scalar_tensor_tensor(
                                out=a[:, sa + 1:sbb], in0=PS[:, u * 1024 + sa: u * 1024 + sbb - 1], scalar=cw[:, l, fidx, 0:1], in1=a[:, sa + 1:sbb],
                                op0=ALU.mult, op1=ALU.add)), reads=ubufs + [a_b, PERS_b], writes=[a_b])
                            S.op('dve', (lambda h, a=a, u=u, sa=sa, sbb=sbb, fidx=fidx: h.scalar_tensor_tensor(
                                out=a[:, sa:sbb - 1], in0=PS[:, u * 1024 + sa + 1: u * 1024 + sbb], scalar=cw[:, l, fidx, 2:3], in1=a[:, sa:sbb - 1],
                                op0=ALU.mult, op1=ALU.add)), reads=ubufs + [a_b, PERS_b], writes=[a_b])
                        if lh or rh:
                            hl = hal[hs % 2]
                            hl_b = hal_b[hs % 2]
                            S.op('act', (lambda h, hl=hl, hs=hs: h.activation(out=hl[:], in_=PS[:, 6 * 512 + hs * 2: 6 * 512 + hs * 2 + 2], func=AF.Copy)),
                                 reads=[PSB[6]], writes=[hl_b])
                        if lh:
                            S.op('dve', (lambda h, a=a, hl=hl, fidx=fidx: h.scalar_tensor_tensor(
                                out=a[:, 0:1], in0=hl[:, 0:1], scalar=cw[:, l, fidx, 0:1], in1=a[:, 0:1],
                                op0=ALU.mult, op1=ALU.add)), reads=[hl_b, a_b, PERS_b], writes=[a_b])
                        if rh:
                            S.op('dve', (lambda h, a=a, hl=hl, fidx=fidx: h.scalar_tensor_tensor(
                                out=a[:, 1023:1024], in0=hl[:, 1:2], scalar=cw[:, l, fidx, 2:3], in1=a[:, 1023:1024],
                                op0=ALU.mult, op1=ALU.add)), reads=[hl_b, a_b, PERS_b], writes=[a_b])
                        accs.append((a, a_b))
                    (av, av_b), (ag, ag_b) = accs
                    S.op('act', (lambda h, ag=ag: h.activation(out=ag[:], in_=ag[:], func=AF.Silu)), reads=[ag_b], writes=[ag_b])
                    S.op('dve', (lambda h, ag=ag, av=av, fc=fc: h.tensor_tensor(out=GT[:, fc, :], in0=ag[:], in1=av[:], op=ALU.mult)),
                         reads=[ag_b, av_b], writes=[GT_b[fc % 2]])

        def down_proj(t0, cond, nxt):
            pcs = pieces(nxt[0]) if nxt is not None else []
            c = 0
            for nb in range(8):
                k = nb % 2
                S.dma('pool', wdn[k][:], w_down_d[l][:, nb * 256:(nb + 1) * 256].rearrange("(c p) n -> p c n", p=128), IN_b, wdn_b[k])
                if nb < len(pcs):
                    norm_p1(pcs[nb], NS[0])
                for dj in range(2):
                    dc = nb * 2 + dj
                    for hf in range(2):
                        q = c % 2
                        c += 1
                        col = t0 + hf * 512
                        S.dma('sp', xi[q][:], XT_d[dc, :, col:col + 512], XT_b, xi_b[q])
                        b = next_bank(6)
                        for fc in range(44):
                            S.op('pe', (lambda h, b=b, k=k, fc=fc, dj=dj, hf=hf: h.matmul(
                                bank(b), wdn[k][:, fc, dj * 128:(dj + 1) * 128], GT[:, fc, hf * 512:(hf + 1) * 512],
                                start=(fc == 0), stop=(fc == 43))), reads=[wdn_b[k], GT_b[0], GT_b[1]], writes=[PSB[b]], sig=(fc == 43))
                        S.op('dve', (lambda h, q=q, b=b, dc=dc, cond=cond: h.scalar_tensor_tensor(
                            out=xo[q][:], in0=bank(b), scalar=mod_ap(l, 5, dc, cond), in1=xi[q][:], op0=ALU.mult, op1=ALU.add)),
                            reads=[PSB[b], xi_b[q], MOD_b], writes=[xo_b[q]])
                        S.dma('sp', XT_d[dc, :, col:col + 512], xo[q][:], xo_b[q], XT_b)
                if nb < len(pcs):
                    norm_p2(pcs[nb], nxt[1], NS[0])

        halo_pc = (1023, 2, (lambda dc: hsave[:, dc, 0:2]), HS_b)
        plist = [(halo_pc, 0)] + [(pc, FFN_TILES[0][1]) for pc in pieces(FFN_TILES[0][0])]
        for i in range(len(plist) + 1):
            if i < len(plist):
                norm_p1(plist[i][0], NS[i % 2])
            if i >= 1:
                norm_p2(plist[i - 1][0], plist[i - 1][1], NS[(i - 1) % 2])
        S.barrier()
        for ti, (t0, cond, segs, lh, rh) in enumerate(FFN_TILES):
            up_proj(t0, cond, segs, lh, rh)
            S.barrier()
            down_proj(t0, cond, FFN_TILES[ti + 1] if ti + 1 < len(FFN_TILES) else None)
            S.barrier()
        S.recycle()

    def g1_odd():
        ph.reset()
        sgb = [ph.alloc("sgb", [128, 512], BF16) for _ in range(3)]
        sgb_b = [Buf("osgb%d" % i, True) for i in range(3)]
        sgf = [ph.alloc("sgf", [128, 512], F32) for _ in range(3)]
        sgf_b = [Buf("osgf%d" % i, True) for i in range(3)]
        cnts = {'b': 0, 'f': 0, 't': 0}

        def nb():
            i = cnts['b'] % 3
            cnts['b'] += 1
            return sgb[i], sgb_b[i]

        def nf():
            i = cnts['f'] % 3
            cnts['f'] += 1
            return sgf[i], sgf_b[i]

        names = ('qf', 'sq', 'qn', 'qg', 't1', 't2')
        tm = {n: [ph.alloc(n, [128, 512], F32) for _ in range(3)] for n in names}
        tm_b = {n: [Buf("o%s%d" % (n, i), True) for i in range(3)] for n in names}
        stages3 = []
        ssb = [ph.alloc("ss", [128, 4], F32) for _ in range(3)]
        ss_b = [Buf("oss%d" % i, True) for i in range(3)]
        qbb = [ph.alloc("qb", [128, 512], BF16) for _ in range(3)]
        qb_b = [Buf("oqb%d" % i, True) for i in range(3)]
        tst = [ph.alloc("tst", [128, 4, 128], BF16) for _ in range(3)]
        tst_b = [Buf("otst%d" % i, True) for i in range(3)]
        ctb = [ph.alloc("ct", [128, 128], F32) for _ in range(3)]
        stb = [ph.alloc("stt", [128, 128], F32) for _ in range(3)]
        ct_b = [Buf("oct%d" % i, True) for i in range(3)]
        st_b = [Buf("ost%d" % i, True) for i in range(3)]
        g4 = ph.alloc("g4", [128, 2, 512], F32)
        G4_b = Buf("g4", True)
        S.dma('sp', g4[:, 0, :], gq4_d.partition_broadcast(128), IN_b, G4_b)
        S.dma('sp', g4[:, 1, :], gk4_d.partition_broadcast(128), IN_b, G4_b)

        wref = {}
        for blk in range(6):
            if blk in (0, 5):
                wt, wb = load_w(wview(od_w_in_d, blk * 512, 512))
            if blk == 0:
                for j in range(4):
                    for t5 in range(6):
                        b = next_bank(6)
                        mm_fm(wt, wb, j, t5, b)
                        sg, sg_b = nf()
                        S.op('act', (lambda h, sg=sg, b=b: h.activation(out=sg[:], in_=bank(b), func=AF.Copy)), reads=[PSB[b]], writes=[sg_b])
                        S.dma('sp', XB_d[j, :, t5 * 512:(t5 + 1) * 512], sg[:], sg_b, XB_b)
            elif blk == 5:
                for tt in range(24):
                    b = next_bank(6)
                    mm_tm(wt, wb, tt, b)
                    sg, sg_b = nb()
                    if tt < 16:
                        S.op('dve', (lambda h, sg=sg, b=b: h.tensor_copy(out=sg[:], in_=bank(b))), reads=[PSB[b]], writes=[sg_b])
                    else:
                        sf, sf_b = nf()
                        S.op('act', (lambda h, sf=sf, b=b: h.activation(out=sf[:], in_=bank(b), func=AF.Copy)), reads=[PSB[b]], writes=[sf_b])
                        S.op('dve', (lambda h, sg=sg, sf=sf: h.tensor_copy(out=sg[:], in_=sf[:])), reads=[sf_b], writes=[sg_b])
                        S.dma('sp', gqv_d[(tt - 16) * 128:(tt - 15) * 128, :], sf[:], sf_b, OUT_b)
                    S.dma('sp', V_d[tt * 128:(tt + 1) * 128, 0:512], sg[:], sg_b, V_b)
            else:
                isk = (blk == 4)
                for tt in range(24):
                    k = cnts['t'] % 3
                    hf = cnts['t'] % 2
                    cnts['t'] += 1
                    qf, sq, qn, qg, t1, t2 = (tm[n][k] for n in names)
                    qf_b, sq_b, qn_b, qg_b, t1_b, t2_b = (tm_b[n][k] for n in names)
                    ss, ss_bb = ssb[k], ss_b[k]
                    qb, qb_bb = qbb[k], qb_b[k]
                    ct, stt = ctb[k], stb[k]
                    ts_, ts_b = tst[k], tst_b[k]
                    gi = 1 if isk else 0

                    def st1(blk=blk, tt=tt, qf=qf, qf_b=qf_b, sq=sq, sq_b=sq_b, ss=ss, ss_bb=ss_bb):
                        if tt == 0:
                            wref[blk] = load_w(wview(od_w_in_d, blk * 512, 512))
                        wt, wb = wref[blk]
                        b = next_bank(6)
                        mm_tm(wt, wb, tt, b)
                        S.op('act', (lambda h: h.activation(out=qf[:], in_=bank(b), func=AF.Copy)), reads=[PSB[b]], writes=[qf_b])
                        S.op('act', (lambda h: h.activation(out=sq[:], in_=qf[:], func=AF.Square)), reads=[qf_b], writes=[sq_b])
                        S.op('dve', (lambda h: h.tensor_reduce(out=ss[:], in_=sq[:].rearrange("p (h d) -> p h d", d=128), axis=AX.X, op=ALU.add)),
                             reads=[sq_b], writes=[ss_bb])
                        S.op('dve', (lambda h: h.tensor_scalar(out=ss[:], in0=ss[:], scalar1=1.0 / 128.0, scalar2=EPS, op0=ALU.mult, op1=ALU.add)),
                             reads=[ss_bb], writes=[ss_bb])
                        S.op('act', (lambda h: h.activation(out=ss[:], in_=ss[:], func=AF.Sqrt)), reads=[ss_bb], writes=[ss_bb])

                    def st2(tt=tt, k=k, isk=isk, gi=gi, qf=qf, qf_b=qf_b, qn=qn, qn_b=qn_b, qg=qg, qg_b=qg_b, t1=t1, t1_b=t1_b, t2=t2, t2_b=t2_b,
                            ss=ss, ss_bb=ss_bb, qb=qb, qb_bb=qb_bb, ct=ct, stt=stt):
                        S.op('dve', (lambda h: h.reciprocal(out=ss[:], in_=ss[:])), reads=[ss_bb], writes=[ss_bb])
                        S.op('dve', (lambda h: h.tensor_tensor(
                            out=qn[:].rearrange("p (h d) -> p h d", d=128), in0=qf[:].rearrange("p (h d) -> p h d", d=128),
                            in1=ss[:].unsqueeze(2).to_broadcast([128, 4, 128]), op=ALU.mult)), reads=[qf_b, ss_bb], writes=[qn_b])
                        S.op('dve', (lambda h: h.tensor_tensor(out=qg[:], in0=qn[:], in1=g4[:, gi, :], op=ALU.mult)),
                             reads=[qn_b, G4_b], writes=[qg_b])
                        if tt >= 16:
                            if isk:
                                S.dma('sp', gqk_d[(tt - 16) * 128:(tt - 15) * 128, :], qg[:], qg_b, OUT_b)
                            S.op('act', (lambda h: h.activation(out=qb[:], in_=qg[:], func=AF.Copy)), reads=[qg_b], writes=[qb_bb])
                        else:
                            S.dma('sp', ct[:], ropeC_d[tt * 128:(tt + 1) * 128, :], IN_b, ct_b[k])
                            S.dma('sp', stt[:], ropeS_d[tt * 128:(tt + 1) * 128, :], IN_b, st_b[k])
                            S.op('dve', (lambda h: h.tensor_tensor(
                                out=t1[:].rearrange("p (h d) -> p h d", d=128), in0=qg[:].rearrange("p (h d) -> p h d", d=128),
                                in1=ct[:].unsqueeze(1).to_broadcast([128, 4, 128]), op=ALU.mult)), reads=[qg_b, ct_b[k]], writes=[t1_b])
                            for s_ in range(2):
                                S.op('dve', (lambda h, s_=s_: h.tensor_tensor(
                                    out=t2[:].rearrange("p (h r s i) -> p h r s i", h=4, r=2, s=2)[:, :, :, s_, :],
                                    in0=qg[:].rearrange("p (h r s i) -> p h r s i", h=4, r=2, s=2)[:, :, :, 1 - s_, :],
                                    in1=stt[:].rearrange("p (r s i) -> p r s i", r=2, s=2)[:, :, s_, :].unsqueeze(1).to_broadcast([128, 4, 2, 32]),
                                    op=ALU.mult)), reads=[qg_b, st_b[k]], writes=[t2_b])
                            S.op('dve', (lambda h: h.tensor_tensor(out=qb[:], in0=t1[:], in1=t2[:], op=ALU.add)),
                                 reads=[t1_b, t2_b], writes=[qb_bb])

                    def st3(tt=tt, blk=blk, isk=isk, hf=hf, qb=qb, qb_bb=qb_bb, ts_=ts_, ts_b=ts_b):
                        for hh in range(4):
                            S.op('pe', (lambda h, hh=hh: h.transpose(PSbf[:, hf * 512 + hh * 128: hf * 512 + (hh + 1) * 128],
                                                                     qb[:, hh * 128:(hh + 1) * 128], ident_b[:])),
                                 reads=[qb_bb, PERS_b], writes=[PSB[6]], sig=(hh == 3))
                        S.op('act', (lambda h: h.activation(out=ts_[:], in_=PSbf[:, hf * 512:(hf + 1) * 512].rearrange("p (h t) -> p h t", t=128), func=AF.Copy)),
                             reads=[PSB[6]], writes=[ts_b])
                        if isk:
                            S.dma('sp', KT_d[0:4, :, tt * 128:(tt + 1) * 128].rearrange("h p t -> p h t"), ts_[:], ts_b, KT_b)
                        else:
                            h0 = (blk - 1) * 4
                            S.dma('sp', QT_d[h0:h0 + 4, :, tt * 128:(tt + 1) * 128].rearrange("h p t -> p h t"), ts_[:], ts_b, QT_b)

                    stages3.append((st1, st2, st3))
        n3 = len(stages3)
        for i in range(n3 + 2):
            if i < n3:
                stages3[i][0]()
            if 0 <= i - 1 < n3:
                stages3[i - 1][1]()
            if 0 <= i - 2 < n3:
                stages3[i - 2][2]()
        S.barrier()
        S.recycle()

    def attn_odd():
        ph.reset(WB_OFF)
        ar = attn_arena()
        kctok = ph.alloc("gkctok", [128, 2, 512], BF16)
        vc = ph.alloc("gvc", [128, 2, 512], BF16)
        kcT = ph.alloc("gkcT", [128, 4, 256], BF16)
        KTOK_b, VC_b, KCT_b = Buf("gkctok", True), Buf("gvc", True), Buf("gkcT", True)
        S.dma('pool', kctok[:], cgk_d.rearrange("(c p) d -> p c d", p=128), IN_b, KTOK_b)
        S.dma('pool', vc[:], cgv_d.rearrange("(c p) d -> p c d", p=128), IN_b, VC_b)
        for kv in range(4):
            for c in range(2):
                S.op('pe', (lambda h, kv=kv, c=c: h.transpose(PSbf[:, c * 128:(c + 1) * 128], kctok[:, c, kv * 128:(kv + 1) * 128], ident_b[:])),
                     reads=[KTOK_b, PERS_b], writes=[PSB[6]], sig=(c == 1))
            S.op('dve', (lambda h, kv=kv: h.tensor_copy(out=kcT[:, kv, :], in_=PSbf[:, 0:256])), reads=[PSB[6]], writes=[KCT_b])
        qs = [ph.alloc("gqs", [128, TS], BF16) for _ in range(2)]
        ks = [ph.alloc("gks", [128, TS], BF16) for _ in range(2)]
        vs = [ph.alloc("gvs", [128, 16, 128], BF16) for _ in range(2)]
        qs_b = [Buf("gqs%d" % i, True) for i in range(2)]
        ks_b = [Buf("gks%d" % i, True) for i in range(2)]
        vs_b = [Buf("gvs%d" % i, True) for i in range(2)]
        ptb = [ph.alloc("gpt", [128, 18, 512], BF16) for _ in range(2)]
        ptb_b = [Buf("gpt%d" % i, True) for i in range(2)]
        recb = ph.alloc("grec", [128, 512], F32)
        recb_b = Buf("grec", True)
        c2 = 0
        scn = [0]
        stages = []
        for hq in range(12):
            kv = hq // 3
            i = hq % 2

            def head_loads(hq=hq, kv=kv, i=i):
                S.dma('sp', qs[i][:], QT_d[hq, :, 0:TS], QT_b, qs_b[i])
                S.dma('sp', ks[i][:], KT_d[kv, :, 0:TS], KT_b, ks_b[i])
                S.dma('sp', vs[i][:], V_d[0:TS, kv * 128:(kv + 1) * 128].rearrange("(c p) d -> p c d", p=128), V_b, vs_b[i])

            for qt in range(4):
                pt = ptb[c2 % 2]
                pt_b = ptb_b[c2 % 2]
                c2 += 1

                def st1(hq=hq, kv=kv, i=i, qt=qt, pt=pt, pt_b=pt_b, head_loads=head_loads):
                    if qt == 0:
                        head_loads()
                    for kc in range(18):
                        sb = scn[0] % 4
                        scn[0] += 1
                        if kc < 2:
                            lhs = (lambda kc=kc: kcT[:, kv, kc * 128:(kc + 1) * 128])
                            rb = KCT_b
                        else:
                            lhs = (lambda kc=kc: ks[i][:, (kc - 2) * 128:(kc - 1) * 128])
                            rb = ks_b[i]
                        S.op('pe', (lambda h, sb=sb, lhs=lhs: h.matmul(bank(sb), lhs(), qs[i][:, qt * 512:(qt + 1) * 512], start=True, stop=True)),
                             reads=[rb, qs_b[i]], writes=[PSB[sb]], sig=True)
                        S.op('act', (lambda h, kc=kc, sb=sb: h.activation(out=pt[:, kc, :], in_=bank(sb), func=AF.Exp, scale=ISQ128)),
                             reads=[PSB[sb]], writes=[pt_b])

                def st2(hq=hq, kv=kv, i=i, qt=qt, pt=pt, pt_b=pt_b):
                    for kc in range(18):
                        if kc < 2:
                            lhs = (lambda kc=kc: vc[:, kc, kv * 128:(kv + 1) * 128])
                            rb = VC_b
                        else:
                            lhs = (lambda kc=kc: vs[i][:, kc - 2, :])
                            rb = vs_b[i]
                        S.op('pe', (lambda h, lhs=lhs, kc=kc: h.matmul(bank(4), lhs(), pt[:, kc, :], start=(kc == 0), stop=(kc == 17))),
                             reads=[rb, pt_b], writes=[PSB[4]], sig=(kc == 17))
                    for kc in range(18):
                        S.op('pe', (lambda h, kc=kc: h.matmul(bank(5), ones_b[:], pt[:, kc, :], start=(kc == 0), stop=(kc == 17))),
                             reads=[pt_b, PERS_b], writes=[PSB[5]], sig=(kc == 17))
                    S.op('dve', (lambda h: h.reciprocal(out=recb[:], in_=bank(5))), reads=[PSB[5]], writes=[recb_b])
                    S.op('dve', (lambda h: h.tensor_tensor(out=AT[:, 4 + hq, qt * 512:(qt + 1) * 512], in0=bank(4), in1=recb[:], op=ALU.mult)),
                         reads=[PSB[4], recb_b], writes=[AT_b[qt]])

                stages.append((st1, st2))
        pipeline2(stages)
        stages = []
        for hq in range(12):
            block_attn_prompt(hq, hq // 3, (hq // 3) * 128, 4 + hq, ar, stages)
        pipeline2(stages)
        S.barrier()
        S.recycle()

    def pool_phase():
        ph.reset(WB_OFF)
        pw = ph.alloc("pw", [128, 4, 128], BF16)
        psc = ph.alloc("psc", [128, 4], F32)
        PC_b = Buf("pconst", True)
        S.dma('pool', pw[:], pw_d.rearrange("g c d -> c g d"), IN_b, PC_b)
        S.dma('sp', psc[:], pscT_d, IN_b, PC_b)
        L = TS + 32
        X0 = ph.alloc("px0", [128, L], F32)
        A = ph.alloc("pA", [128, L], F32)
        Bb = ph.alloc("pB", [128, L], F32)
        invc = ph.alloc("pinvc", [128, TS], F32)
        tmp = ph.alloc("ptmp", [128, TS], F32)
        pooled = ph.alloc("ppooled", [128, TS], BF16)
        X0_b, A_b, B_b, I_b, T_b, P_b = (Buf(n, True) for n in ("px0", "pA", "pB", "pinvc", "ptmp", "ppooled"))
        S.op('dve', (lambda h: h.memset(X0[:], 0.0)), writes=[X0_b])

        def chain(levels, Lx):
            cur, cur_b = X0, X0_b
            outs = [(A, A_b), (Bb, B_b)]
            for lev in range(1, levels + 1):
                o, o_b = outs[lev % 2]
                if lev == 1:
                    S.op('dve', (lambda h, o=o, cur=cur: h.tensor_tensor(out=o[:, 1:Lx], in0=cur[:, 0:Lx - 1], in1=cur[:, 1:Lx], op=ALU.add)),
                         reads=[cur_b], writes=[o_b])
                else:
                    d = 2 ** (lev - 2)
                    S.op('dve', (lambda h, o=o, cur=cur, d=d: h.tensor_tensor(out=o[:, d:Lx - d], in0=cur[:, 0:Lx - 2 * d], in1=cur[:, 2 * d:Lx], op=ALU.add)),
                         reads=[cur_b], writes=[o_b])
                cur, cur_b = o, o_b
            return cur, cur_b

        for g in range(4):
            S.dma('sp', X0[:, 16:16 + TS], XB_d[g, :, 0:TS], XB_b, X0_b)
            S.dma('sp', invc[:], invc_s_d[g].partition_broadcast(128), IN_b, I_b)
            sw, sw_b = chain(g + 1, L)
            S.op('dve', (lambda h, sw=sw: h.tensor_tensor(out=tmp[:], in0=sw[:, 16:16 + TS], in1=invc[:], op=ALU.mult)), reads=[sw_b, I_b], writes=[T_b])
            S.op('dve', (lambda h: h.tensor_tensor(out=pooled[:], in0=tmp[:], in1=X0[:, 16:16 + TS], op=ALU.subtract)), reads=[T_b, X0_b], writes=[P_b])
            for t5 in range(4):
                b = next_bank(6)
                S.op('pe', (lambda h, b=b, g=g, t5=t5: h.matmul(bank(b), pw[:, g, :], pooled[:, t5 * 512:(t5 + 1) * 512], start=True, stop=True)),
                     reads=[P_b, PC_b], writes=[PSB[b]], sig=True)
                S.op('act', (lambda h, b=b, g=g, t5=t5: h.activation(out=AT[:, g, t5 * 512:(t5 + 1) * 512], in_=bank(b), func=AF.Copy, scale=psc[:, g:g + 1])),
                     reads=[PSB[b], PC_b], writes=[AT_b[t5]])
        S.op('dve', (lambda h: h.memset(X0[:], 0.0)), writes=[X0_b])
        Lp = 256 + 32

        def v3(t):
            return t[:, 0:4 * Lp].rearrange("p (s t) -> p s t", s=4)

        def chain3(levels):
            cur, cur_b = X0, X0_b
            outs = [(A, A_b), (Bb, B_b)]
            for lev in range(1, levels + 1):
                o, o_b = outs[lev % 2]
                if lev == 1:
                    S.op('dve', (lambda h, o=o, cur=cur: h.tensor_tensor(out=v3(o)[:, :, 1:Lp], in0=v3(cur)[:, :, 0:Lp - 1], in1=v3(cur)[:, :, 1:Lp], op=ALU.add)),
                         reads=[cur_b], writes=[o_b])
                else:
                    d = 2 ** (lev - 2)
                    S.op('dve', (lambda h, o=o, cur=cur, d=d: h.tensor_tensor(out=v3(o)[:, :, d:Lp - d], in0=v3(cur)[:, :, 0:Lp - 2 * d], in1=v3(cur)[:, :, 2 * d:Lp], op=ALU.add)),
                         reads=[cur_b], writes=[o_b])
                cur, cur_b = o, o_b
            return cur, cur_b

        for g in range(4):
            S.dma('sp', invc[:, 0:256], invc_p_d[g].partition_broadcast(128), IN_b, I_b)
            S.dma('sp', v3(X0)[:, :, 16:16 + 256], XB_d[g, :, TS:T].rearrange("p (s t) -> p s t", s=4), XB_b, X0_b)
            sw, sw_b = chain3(g + 1)
            S.op('dve', (lambda h, sw=sw: h.tensor_tensor(out=tmp[:, 0:1024].rearrange("p (s t) -> p s t", s=4), in0=v3(sw)[:, :, 16:16 + 256],
                                                          in1=invc[:, 0:256].unsqueeze(1).to_broadcast([128, 4, 256]), op=ALU.mult)),
                 reads=[sw_b, I_b], writes=[T_b])
            S.op('dve', (lambda h: h.tensor_tensor(out=pooled[:, 0:1024].rearrange("p (s t) -> p s t", s=4), in0=tmp[:, 0:1024].rearrange("p (s t) -> p s t", s=4),
                                                   in1=v3(X0)[:, :, 16:16 + 256], op=ALU.subtract)),
                 reads=[T_b, X0_b], writes=[P_b])
            for t5 in range(2):
                b = next_bank(6)
                S.op('pe', (lambda h, b=b, g=g, t5=t5: h.matmul(bank(b), pw[:, g, :], pooled[:, t5 * 512:(t5 + 1) * 512], start=True, stop=True)),
                     reads=[P_b, PC_b], writes=[PSB[b]], sig=True)
                S.op('act', (lambda h, b=b, g=g, t5=t5: h.activation(out=AT[:, g, TS + t5 * 512: TS + (t5 + 1) * 512], in_=bank(b), func=AF.Copy, scale=psc[:, g:g + 1])),
                     reads=[PSB[b], PC_b], writes=[AT_b[4 + t5]])
        S.barrier()
        S.recycle()

    def final_phase():
        ph.reset(WB_OFF)
        ar = norm_arena(512, nxi=2, nsq=4, ntmp=1)
        yT = [ph.alloc("yT", [128, 4, 512], F32) for _ in range(2)]
        yT_b = [Buf("yT%d" % i, True) for i in range(2)]
        ytok = [ph.alloc("ytok", [128, 512], F32) for _ in range(4)]
        ytok_b = [Buf("ytok%d" % i, True) for i in range(4)]
        cnt = {'y': 0, 'q': 0}

        def stage_b(t5, xi, xi_b, rstd, rstd_b):
            for dq in range(4):
                y = yT[cnt['q'] % 2]
                y_b = yT_b[cnt['q'] % 2]
                cnt['q'] += 1
                for j in range(4):
                    dc = dq * 4 + j
                    S.op('dve', (lambda h, y=y, j=j, dc=dc: h.scalar_tensor_tensor(
                        out=y[:, j, :], in0=xi[:, dc, 0:512], scalar=ngT[:, 4, dc:dc + 1], in1=rstd[:, 0:512], op0=ALU.mult, op1=ALU.mult)),
                        reads=[xi_b, rstd_b, PERS_b], writes=[y_b])
                for ts in range(4):
                    b = next_bank(6)
                    for j in range(4):
                        S.op('pe', (lambda h, b=b, j=j, y=y, ts=ts: h.transpose(
                            PS[:, b * 512 + j * 128: b * 512 + (j + 1) * 128], y[:, j, ts * 128:(ts + 1) * 128], ident_f[:])),
                            reads=[y_b, PERS_b], writes=[PSB[b]], sig=(j == 3))
                    k = cnt['y'] % 4
                    cnt['y'] += 1
                    S.op('act', (lambda h, b=b, k=k: h.activation(out=ytok[k][:], in_=bank(b), func=AF.Copy)),
                         reads=[PSB[b]], writes=[ytok_b[k]])
                    tok0 = t5 * 512 + ts * 128
                    if tok0 < TS:
                        S.dma('sp', ys_d[tok0:tok0 + 128, dq * 512:(dq + 1) * 512], ytok[k][:], ytok_b[k], OUT_b)
                    else:
                        S.dma('sp', yp_d[tok0 - TS:tok0 - TS + 128, dq * 512:(dq + 1) * 512], ytok[k][:], ytok_b[k], OUT_b)

        prev = None
        for t5 in range(6):
            st = norm_stats([(t5 * 512, 512)], ar)
            if prev is not None:
                stage_b(*prev)
            prev = (t5,) + tuple(st)
        stage_b(*prev)
        S.barrier()
        S.recycle()

    STAGES = ['n1_0', 'g1_0', 'attn_0', 'fnet_0', 'mix0', 'ffn0', 'n1_1', 'g1_1', 'attn_1', 'pool_1', 'mix1', 'ffn1', 'final']
    nstage = len(STAGES) if stop_after is None else STAGES.index(stop_after) + 1
    fns = [in_phase, g1_even, attn_even, fnet_phase, lambda: g2_phase(0, ev_w_out_d), lambda: ffn_phase(0),
           lambda: norm1_phase(1), g1_odd, attn_odd, pool_phase, lambda: g2_phase(1, od_w_out_d), lambda: ffn_phase(1), final_phase]
    for fn in fns[:nstage]:
        fn()
    S.finish()
    S.emit()
    st.close()
    return nc


def _build_bt(bias):
    NEG = np.float32(-30000.0)
    kc = np.arange(64)[:, None]
    qc = np.arange(64)[None, :]
    qstart = np.clip(qc - 8, 0, 48)
    colvalid = (kc >= qstart) & (kc < qstart + 16)
    dcidx = np.clip(kc - qc, -15, 15) + 15
    types = ([(5, 5 + dm) for dm in range(-2, 3)] + [(0, m) for m in range(4)] + [(1, m) for m in range(4)]
             + [(14, m) for m in range(12, 16)] + [(15, m) for m in range(12, 16)])
    bt = np.full((12, 21, 128, 128), NEG, np.float32)
    for ti, (j, m) in enumerate(types):
        for a in range(2):
            for b in range(2):
                kr = 2 * m + a
                r = 2 * j + b
                r0 = min(max(r - 4, 0), 24)
                if not (r0 <= kr < r0 + 8):
                    continue
                blk = bias[:, kr - r + 7][:, dcidx]
                bt[:, ti, a * 64:(a + 1) * 64, b * 64:(b + 1) * 64] = np.where(colvalid[None], blk, NEG)
    return bt


def _dft(n):
    k = np.arange(n, dtype=np.int64)
    ang = 2.0 * np.pi * ((k[:, None] * k[None, :]) % n).astype(np.float64) / n
    return (np.cos(ang) / np.sqrt(n)).astype(np.float32), (np.sin(ang) / np.sqrt(n)).astype(np.float32)


_CONST = {}


def _consts():
    if _CONST:
        return _CONST
    c2k, s2k = _dft(2048)
    c256, s256 = _dft(256)
    c128, s128 = _dft(128)
    t = np.arange(TS)
    pos = np.stack([(t // 64), (t % 64)], 1).astype(np.float32)
    inv = (10000.0 ** (-np.arange(0, 64, 2, dtype=np.float32) / 64.0)).astype(np.float32)
    ang = pos[:, :, None] * inv[None, None, :]
    cs = np.cos(ang).astype(np.float32)
    sn = np.sin(ang).astype(np.float32)
    ropeC = np.stack([cs, cs], 2).reshape(TS, 128)
    ropeS = np.stack([-sn, sn], 2).reshape(TS, 128)

    def invc(n):
        tt = np.arange(n)
        out = []
        for w in (2, 4, 8, 16):
            lo = np.clip(tt - w // 2, 0, n)
            hi = np.clip(tt + w // 2, 0, n)
            out.append(1.0 / (hi - lo).astype(np.float32))
        return np.stack(out, 0).astype(np.float32)

    _CONST.update(cn2k=c2k, sn2k=s2k, cn256=c256, sn256=s256, cc=np.ascontiguousarray(np.concatenate([c128, -s128], 1)),
                  ropeC=np.ascontiguousarray(ropeC), ropeS=np.ascontiguousarray(ropeS), invc_s=invc(TS), invc_p=invc(256),
                  ident=np.eye(128, dtype=np.float32))
    return _CONST


def make_in_maps(inp):
    f = lambda a: np.ascontiguousarray(np.asarray(a, dtype=np.float32))
    shared = {
        "ada_w": f(inp["ada_w"]),
        "ada_bT": f(np.asarray(inp["ada_b"]).reshape(2, 96, 128).transpose(0, 2, 1)),
        "ngT": f(np.stack([inp["norm1_g"][0], inp["norm1_g"][1], inp["norm2_g"][0], inp["norm2_g"][1], inp["final_norm_g"]], 0)
                 .reshape(5, 16, 128).transpose(2, 0, 1)),
        "ev_w_in": f(inp["ev_w_in"][0]),
        "ev_w_out": f(inp["ev_w_out"][0]),
        "od_w_in": f(inp["od_w_in"][0]),
        "od_w_out": f(inp["od_w_out"][0]),
        "w_up": f(inp["ffn_w_up"]),
        "w_down": f(inp["ffn_w_down"]),
        "cwT": f(np.asarray(inp["ffn_conv_w"]).reshape(2, 3, 88, 128).transpose(3, 0, 2, 1)),
        "cbT": f(np.asarray(inp["ffn_conv_b"]).reshape(2, 88, 128).transpose(2, 0, 1)),
        "fin_g": f(inp["final_norm_g"]),
        "bt": _build_bt(f(inp["ev_na_bias"][0])),
        "fw": f(inp["ev_fnet_w"][0]),
        "pw": f(inp["od_pool_w"][0]),
        "pscT": f(np.asarray(inp["od_pool_scale"][0]).reshape(4, 128).T),
        "gq4": f(np.tile(np.asarray(inp["od_q_norm_g"][0]), 4)),
        "gk4": f(np.tile(np.asarray(inp["od_k_norm_g"][0]), 4)),
    }
    shared.update(_consts())
    maps = []
    for c in range(NCORES):
        m = dict(shared)
        m["xs"] = f(inp["x_sample"][c])
        m["xp"] = f(np.asarray(inp["x_prompt"][4 * c:4 * c + 4]).reshape(TP, D))
        m["condT"] = f(np.stack([inp["c"][c], inp["c_ctx"]], axis=1))
        m["cnk"] = f(np.asarray(inp["cache_na_k"][c, 0]).reshape(256, 1536))
        m["cnv"] = f(np.asarray(inp["cache_na_v"][c, 0]).reshape(256, 1536))
        m["cgk"] = f(np.asarray(inp["cache_gqa_k"][c, 0]).reshape(256, 512))
        m["cgv"] = f(np.asarray(inp["cache_gqa_v"][c, 0]).reshape(256, 512))
        maps.append(m)
    return maps


_NC_CACHE = {}


def kernel(**inputs):
    if "nc" not in _NC_CACHE:
        _NC_CACHE["nc"] = build()
    nc = _NC_CACHE["nc"]
    maps = make_in_maps(inputs)
    res = run_bass_kernel_spmd(nc, maps, core_ids=list(range(NCORES)))
    r = res.results
    y_prompt = np.concatenate([r[c]["yp"].reshape(4, 256, D) for c in range(NCORES)], 0).astype(np.float32)
    y_sample = np.stack([r[c]["ys"] for c in range(NCORES)], 0).astype(np.float32)
    nak = np.concatenate([r[c]["nak"].reshape(4, 1, 256, 12, 128) for c in range(NCORES)], 0).astype(np.float32)
    nav = np.concatenate([r[c]["nav"].reshape(4, 1, 256, 12, 128) for c in range(NCORES)], 0).astype(np.float32)
    gqk = np.concatenate([r[c]["gqk"].reshape(4, 1, 256, 4, 128) for c in range(NCORES)], 0).astype(np.float32)
    gqv = np.concatenate([r[c]["gqv"].reshape(4, 1, 256, 4, 128) for c in range(NCORES)], 0).astype(np.float32)
    return (y_prompt, y_sample, nak, nav, gqk, gqv)
```
